# Optimizing a Trainium2 kernel written in Bass

```python
import jax
import jax.numpy as jnp
from jax import lax
import numpy as np

D_MODEL = 1024
BATCH = 2
SEQ = 8192
DEPTH = 2
DEC_BATCH = 32
DEC_SEQ = 8
PAST_LEN = 8192
PAGE_SIZE = 128

N_A_LAYERS = DEPTH // 2
N_B_LAYERS = DEPTH - N_A_LAYERS
MIX_WIDTH = D_MODEL
MEM_HEADS = 4
MEM_HEAD_DIM = 64
MEM_WIDTH = MEM_HEADS * MEM_HEAD_DIM
N_MEM = 256
MAIN_WIDTH = MIX_WIDTH - MEM_WIDTH
RET_HEADS = 6
RET_HEAD_DIM = MAIN_WIDTH // RET_HEADS
RET_CHUNK = 128
DIL_PAIRS = ((128, 1), (512, 4), (2048, 16))
GROUP_HEADS = 4
DIL_HEADS = GROUP_HEADS * len(DIL_PAIRS)
DIL_HEAD_DIM = MAIN_WIDTH // DIL_HEADS
FFN_HIDDEN = -((-8 * D_MODEL) // (3 * 256)) * 256
ROPE_THETA = 10000.0
LN_EPS = 1e-5
ALPHA = (2 * DEPTH) ** 0.25
BETA = (8 * DEPTH) ** -0.25

kernel_name = 'yoco_retention_dilated_attn_step'


def layer_norm(x, g, b):
    xf = x.astype(jnp.float32)
    mu = jnp.mean(xf, -1, keepdims=True)
    var = jnp.mean(jnp.square(xf - mu), -1, keepdims=True)
    y = (xf - mu) * lax.rsqrt(var + LN_EPS) * g.astype(jnp.float32) + b.astype(jnp.float32)
    return y.astype(x.dtype)


def deepnorm_residual(x, h, g, b):
    return layer_norm(ALPHA * x + h, g, b)


def swiglu_ffn(x, w_in, w_out):
    gate, up = jnp.split(x @ w_in, 2, axis=-1)
    return (jax.nn.silu(gate) * up) @ w_out


def rope(x, pos):
    d = x.shape[-1]
    inv = ROPE_THETA ** (-jnp.arange(0, d, 2, dtype=jnp.float32) / d)
    ang = pos[:, None] * inv[None, :]
    cos = jnp.cos(ang)[:, None, :]
    sin = jnp.sin(ang)[:, None, :]
    xf = x.astype(jnp.float32)
    x1, x2 = xf[..., : d // 2], xf[..., d // 2:]
    return jnp.concatenate([x1 * cos - x2 * sin, x2 * cos + x1 * sin], -1).astype(x.dtype)


def project_memory_kv(mem, w):
    b, n, _ = mem.shape
    return (mem @ w).reshape(b, n, 2, MEM_HEADS, MEM_HEAD_DIM)


def memory_attention(q, mem_kv):
    b, s = q.shape[0], q.shape[1]
    sc = jnp.einsum('bshd,bnhd->bhsn', q, mem_kv[:, :, 0]).astype(jnp.float32) * MEM_HEAD_DIM ** -0.5
    p = jax.nn.softmax(sc, axis=-1).astype(q.dtype)
    o = jnp.einsum('bhsn,bnhd->bshd', p, mem_kv[:, :, 1])
    return o.reshape(b, s, MEM_WIDTH)


def retention_log_decay():
    return jnp.log1p(-jnp.exp2(-5.0 - jnp.arange(RET_HEADS, dtype=jnp.float32)))


def retention_inputs(x, w_in, pos):
    b, s, _ = x.shape
    q, k, v, gate, qm = jnp.split(x @ w_in, [MAIN_WIDTH, 2 * MAIN_WIDTH, 3 * MAIN_WIDTH, 4 * MAIN_WIDTH], axis=-1)
    heads = lambda t: t.reshape(b, s, RET_HEADS, RET_HEAD_DIM)
    q = rope(heads(q), pos)
    k = rope(heads(k), pos) * RET_HEAD_DIM ** -0.5
    tr = lambda t: t.astype(jnp.float32).transpose(0, 2, 1, 3)
    return tr(q), tr(k), tr(heads(v)), gate, qm.reshape(b, s, MEM_HEADS, MEM_HEAD_DIM)


def retention_chunk(q, k, v, state, log_g):
    c = q.shape[2]
    i = jnp.arange(c, dtype=jnp.float32)
    diff = i[:, None] - i[None, :]
    lg = log_g[:, None, None]
    decay = jnp.where(diff >= 0, jnp.exp(jnp.maximum(diff, 0.0) * lg), 0.0)
    inner = jnp.einsum('bhqk,bhkv->bhqv', jnp.einsum('bhqd,bhkd->bhqk', q, k) * decay, v)
    cross = jnp.einsum('bhqd,bhdv->bhqv', q, state) * jnp.exp((i[None, :, None] + 1.0) * lg)
    k_dec = k * jnp.exp((c - 1.0 - i)[None, :, None] * lg)
    new_state = jnp.exp(c * lg) * state + jnp.einsum('bhkd,bhkv->bhdv', k_dec, v)
    return inner + cross, new_state


def retention_prompt(q, k, v, log_g):
    b, h, s, dk = q.shape
    dv = v.shape[-1]
    nc = s // RET_CHUNK
    blocks = lambda t: t.reshape(b, h, nc, RET_CHUNK, t.shape[-1]).transpose(2, 0, 1, 3, 4)

    def step(state, qkv):
        qc, kc, vc = qkv
        o, state = retention_chunk(qc, kc, vc, state, log_g)
        return state, o

    state0 = jnp.zeros((b, h, dk, dv), jnp.float32)
    state, o = lax.scan(step, state0, (blocks(q), blocks(k), blocks(v)))
    return o.transpose(1, 2, 0, 3, 4).reshape(b, h, s, dv), state


def retention_output(o, gate):
    b, h, s, dv = o.shape
    mu = jnp.mean(o, -1, keepdims=True)
    var = jnp.mean(jnp.square(o - mu), -1, keepdims=True)
    on = ((o - mu) * lax.rsqrt(var + LN_EPS)).transpose(0, 2, 1, 3).reshape(b, s, h * dv)
    return jax.nn.silu(gate) * on.astype(gate.dtype)


def dilated_inputs(x, w_in, pos):
    b, s, _ = x.shape
    q, qm = jnp.split(x @ w_in, [MAIN_WIDTH], axis=-1)
    q = rope(q.reshape(b, s, DIL_HEADS, DIL_HEAD_DIM), pos)
    return q, qm.reshape(b, s, MEM_HEADS, MEM_HEAD_DIM)


def shared_kv(x, w_kv, pos):
    b, s, _ = x.shape
    kv = (x @ w_kv).reshape(b, s, 2, DIL_HEADS, DIL_HEAD_DIM)
    return jnp.stack([rope(kv[:, :, 0], pos), kv[:, :, 1]], axis=2)


def dilated_prompt(q, k, v, window, dil):
    b, s, hg, d = q.shape
    blk = window // dil
    m = s // dil
    nb = -(-m // blk)
    mp = nb * blk

    def to_blocks(t):
        t = t.reshape(b, m, dil, hg, d).transpose(0, 2, 1, 3, 4)
        t = jnp.pad(t, ((0, 0), (0, 0), (0, mp - m), (0, 0), (0, 0)))
        return t.reshape(b, dil, nb, blk, hg, d)

    def with_prev(t):
        prev = jnp.concatenate([jnp.zeros_like(t[:, :, :1]), t[:, :, :-1]], axis=2)
        return jnp.concatenate([prev, t], axis=3)

    qb = to_blocks(q)
    kk = with_prev(to_blocks(k))
    vv = with_prev(to_blocks(v))
    sc = jnp.einsum('brcqhd,brckhd->brchqk', qb, kk).astype(jnp.float32) * d ** -0.5
    qi = jnp.arange(blk)[:, None]
    kj = jnp.arange(2 * blk)[None, :]
    delta = qi + blk - kj
    band = (delta >= 0) & (delta <= blk)
    valid = band[None] & ((jnp.arange(nb)[:, None, None] > 0) | (kj >= blk)[None])
    sc = jnp.where(valid[:, None], sc, -jnp.inf)
    lse = jax.nn.logsumexp(sc, axis=-1)
    p = jnp.exp(sc - lse[..., None]).astype(v.dtype)
    o = jnp.einsum('brchqk,brckhd->brcqhd', p, vv)

    def from_blocks(t):
        t = t.reshape((b, dil, mp) + t.shape[4:])[:, :, :m]
        t = jnp.swapaxes(t, 1, 2)
        return t.reshape((b, s) + t.shape[3:])

    return from_blocks(o), from_blocks(lse.transpose(0, 1, 2, 4, 3))


def dilated_sample(q, kv_all, window, dil, n_buf):
    t = q.shape[1]
    d = q.shape[-1]
    n = window // dil + 1
    idx = n_buf + jnp.arange(t)[:, None] - dil * jnp.arange(n)[None, :]
    valid = idx >= 0
    kg = kv_all[:, jnp.maximum(idx, 0)]
    sc = jnp.einsum('bthd,btjhd->bhtj', q, kg[:, :, :, 0]).astype(jnp.float32) * d ** -0.5
    sc = jnp.where(valid[None, None], sc, -jnp.inf)
    lse = jax.nn.logsumexp(sc, axis=-1)
    p = jnp.exp(sc - lse[..., None]).astype(q.dtype)
    o = jnp.einsum('bhtj,btjhd->bthd', p, kg[:, :, :, 1])
    return o, lse.transpose(0, 2, 1)


def combine_groups(outs, lses):
    o = jnp.stack(outs, axis=2)
    w = jax.nn.softmax(jnp.stack(lses, axis=2), axis=2)
    o = o * w[..., None].astype(o.dtype)
    return o.reshape(o.shape[0], o.shape[1], -1)


def setup_inputs(seed: int = 0) -> dict:
    key = jax.random.key(seed)
    ks = jax.random.split(key, 20)
    nrm = lambda k, shape, scale=1.0: scale * jax.random.normal(k, shape, jnp.float32)
    d = D_MODEL
    return {
        'x_prompt': nrm(ks[0], (BATCH, SEQ, d)),
        'x_sample': nrm(ks[1], (DEC_BATCH, DEC_SEQ, d)),
        'mem_prompt': nrm(ks[2], (BATCH, N_MEM, d)),
        'cache_mem_kv': nrm(ks[3], (DEPTH, DEC_BATCH, N_MEM, 2, MEM_HEADS, MEM_HEAD_DIM)),
        'state_ret': nrm(ks[4], (N_A_LAYERS, DEC_BATCH, RET_HEADS, RET_HEAD_DIM, RET_HEAD_DIM)),
        'cache_win_kv_g1': nrm(ks[5], (DEC_BATCH, min(DIL_PAIRS[0][0], PAST_LEN), 2, GROUP_HEADS, DIL_HEAD_DIM)),
        'cache_win_kv_g2': nrm(ks[6], (DEC_BATCH, min(DIL_PAIRS[1][0], PAST_LEN), 2, GROUP_HEADS, DIL_HEAD_DIM)),
        'cache_win_kv_g3': nrm(ks[7], (DEC_BATCH, min(DIL_PAIRS[2][0], PAST_LEN), 2, GROUP_HEADS, DIL_HEAD_DIM)),
        'w_in_a': nrm(ks[8], (N_A_LAYERS, d, 4 * MAIN_WIDTH + MEM_WIDTH), d ** -0.5),
        'w_in_b': nrm(ks[9], (N_B_LAYERS, d, MAIN_WIDTH + MEM_WIDTH), d ** -0.5),
        'w_out': nrm(ks[10], (DEPTH, MIX_WIDTH, d), BETA * MIX_WIDTH ** -0.5),
        'w_kv_shared': nrm(ks[11], (d, 2 * MAIN_WIDTH), d ** -0.5),
        'w_mem_kv': nrm(ks[12], (DEPTH, d, 2 * MEM_WIDTH), d ** -0.5),
        'ln_mix_g': 1.0 + nrm(ks[13], (DEPTH, d), 0.02),
        'ln_mix_b': nrm(ks[14], (DEPTH, d), 0.02),
        'ln_ffn_g': 1.0 + nrm(ks[15], (DEPTH, d), 0.02),
        'ln_ffn_b': nrm(ks[16], (DEPTH, d), 0.02),
        'w_ffn_in': nrm(ks[17], (DEPTH, d, 2 * FFN_HIDDEN), d ** -0.5),
        'w_ffn_out': nrm(ks[18], (DEPTH, FFN_HIDDEN, d), BETA * FFN_HIDDEN ** -0.5),
    }


def reference(x_prompt, x_sample, mem_prompt, cache_mem_kv, state_ret, cache_win_kv_g1,
              cache_win_kv_g2, cache_win_kv_g3, w_in_a, w_in_b, w_out, w_kv_shared, w_mem_kv,
              ln_mix_g, ln_mix_b, ln_ffn_g, ln_ffn_b, w_ffn_in, w_ffn_out):
    s = x_prompt.shape[1]
    t = x_sample.shape[1]
    pos_p = jnp.arange(s, dtype=jnp.float32)
    pos_s = PAST_LEN + jnp.arange(t, dtype=jnp.float32)
    log_g = retention_log_decay()
    win_caches = (cache_win_kv_g1, cache_win_kv_g2, cache_win_kv_g3)
    xp, xs = x_prompt, x_sample
    ret_p, ret_s, mem_kv_p = [], [], []
    for l in range(DEPTH):
        mkv_p = project_memory_kv(mem_prompt, w_mem_kv[l])
        mkv_s = cache_mem_kv[l]
        mem_kv_p.append(mkv_p)
        if l < N_A_LAYERS:
            q, k, v, gate, qm = retention_inputs(xp, w_in_a[l], pos_p)
            o, st = retention_prompt(q, k, v, log_g)
            mix_p = jnp.concatenate([retention_output(o, gate), memory_attention(qm, mkv_p)], -1)
            ret_p.append(st.astype(xp.dtype))
            q, k, v, gate, qm = retention_inputs(xs, w_in_a[l], pos_s)
            o, st = retention_chunk(q, k, v, state_ret[l].astype(jnp.float32), log_g)
            mix_s = jnp.concatenate([retention_output(o, gate), memory_attention(qm, mkv_s)], -1)
            ret_s.append(st.astype(state_ret.dtype))
        else:
            if l == N_A_LAYERS:
                kv_p = shared_kv(xp, w_kv_shared, pos_p)
                kv_s = shared_kv(xs, w_kv_shared, pos_s)
                win_p, win_s, kv_all_s = [], [], []
                for g, (window, _) in enumerate(DIL_PAIRS):
                    hs = slice(g * GROUP_HEADS, (g + 1) * GROUP_HEADS)
                    win_p.append(kv_p[:, s - min(window, s):, :, hs])
                    kva = jnp.concatenate([win_caches[g], kv_s[:, :, :, hs]], axis=1)
                    kv_all_s.append(kva)
                    win_s.append(kva[:, kva.shape[1] - min(window, kva.shape[1]):])
            bl = l - N_A_LAYERS
            q_p, qm_p = dilated_inputs(xp, w_in_b[bl], pos_p)
            q_s, qm_s = dilated_inputs(xs, w_in_b[bl], pos_s)
            outs_p, lses_p, outs_s, lses_s = [], [], [], []
            for g, (window, dil) in enumerate(DIL_PAIRS):
                hs = slice(g * GROUP_HEADS, (g + 1) * GROUP_HEADS)
                o, lse = dilated_prompt(q_p[:, :, hs], kv_p[:, :, 0, hs], kv_p[:, :, 1, hs], window, dil)
                outs_p.append(o)
                lses_p.append(lse)
                o, lse = dilated_sample(q_s[:, :, hs], kv_all_s[g], window, dil, win_caches[g].shape[1])
                outs_s.append(o)
                lses_s.append(lse)
            mix_p = jnp.concatenate([combine_groups(outs_p, lses_p), memory_attention(qm_p, mkv_p)], -1)
            mix_s = jnp.concatenate([combine_groups(outs_s, lses_s), memory_attention(qm_s, mkv_s)], -1)
        xp = deepnorm_residual(xp, mix_p @ w_out[l], ln_mix_g[l], ln_mix_b[l])
        xs = deepnorm_residual(xs, mix_s @ w_out[l], ln_mix_g[l], ln_mix_b[l])
        xp = deepnorm_residual(xp, swiglu_ffn(xp, w_ffn_in[l], w_ffn_out[l]), ln_ffn_g[l], ln_ffn_b[l])
        xs = deepnorm_residual(xs, swiglu_ffn(xs, w_ffn_in[l], w_ffn_out[l]), ln_ffn_g[l], ln_ffn_b[l])
    return (xp, xs, jnp.stack(ret_p), jnp.stack(ret_s), jnp.stack(mem_kv_p),
            win_p[0], win_p[1], win_p[2], win_s[0], win_s[1], win_s[2])
```

```python
import numpy as np
from contextlib import ExitStack
import concourse.bass as bass
import concourse.mybir as mybir
from concourse.bass_utils import run_bass_kernel_spmd

F32 = mybir.dt.float32
BF16 = mybir.dt.bfloat16
AF = mybir.ActivationFunctionType
ALU = mybir.AluOpType
AX = mybir.AxisListType

NEG = -30000.0
ALPHA = 4.0 ** 0.25
LN_EPS = 1e-5
NCORES = 8
T = 2048
NT = 16
TS = 32
TT = T + TS
FF = 2816
NF = 22
DILS = (1, 4, 16)
WINS = (128, 512, 2048)


class Op:
    __slots__ = ("eng", "fn", "sdeps", "is_dma", "lane", "inc", "marked", "count", "idx")


class Sched:
    ENGS = ("pe", "act", "dve", "pool", "sp")

    def __init__(self):
        self.ops = {e: [] for e in self.ENGS}
        self.all = []
        self.lw = {}
        self.rd = {}
        self.floor = None
        self.last = {}
        self.pending_dma = []
        self.lanes = []

    def add(self, eng, fn, r=(), w=(), dma=False, lane=None, inc=16):
        op = Op()
        op.eng, op.fn, op.is_dma, op.lane, op.inc = eng, fn, dma, lane, inc
        op.marked = dma
        op.count = 0
        op.idx = len(self.all)
        if dma and lane not in self.lanes:
            self.lanes.append(lane)
        deps = {}

        def dep(a, kind):
            if (not a.is_dma) and a.eng == eng and (eng == "pe" or kind == "war"):
                return
            deps[a.idx] = a

        for k in r:
            a = self.lw.get(k)
            if a is not None:
                dep(a, "raw")
        for k in w:
            a = self.lw.get(k)
            if a is not None:
                dep(a, "waw")
            for a in self.rd.get(k, ()):
                dep(a, "war")
        if self.floor is not None and not (eng == "pool" and not dma and False):
            deps[self.floor.idx] = self.floor
        for k in w:
            self.lw[k] = op
            self.rd[k] = []
        for k in r:
            if k in w:
                continue
            lst = self.rd.setdefault(k, [])
            if not dma:
                lst[:] = [x for x in lst if x.is_dma or x.eng != eng]
            lst.append(op)
        op.sdeps = list(deps.values())
        for a in op.sdeps:
            a.marked = True
        self.all.append(op)
        self.ops[eng].append(op)
        if dma:
            self.pending_dma.append(op)
        else:
            self.last[eng] = op
        return op

    def barrier(self):
        deps = list(self.last.values()) + list(self.pending_dma)
        op = Op()
        op.eng, op.fn, op.is_dma, op.lane, op.inc = "pool", (lambda e: e.nop()), False, None, 1
        op.marked = False
        op.count = 0
        op.idx = len(self.all)
        op.sdeps = deps
        for a in deps:
            a.marked = True
        self.all.append(op)
        self.ops["pool"].append(op)
        self.last = {"pool": op}
        self.pending_dma = []
        self.floor = op
        self.lw = {}
        self.rd = {}

    def emit(self, nc):
        cnt = {}
        for op in self.all:
            key = op.lane if op.is_dma else op.eng
            if op.marked:
                cnt[key] = cnt.get(key, 0) + (op.inc if op.is_dma else 1)
            op.count = cnt.get(key, 0)
        keys = ["pe", "act", "dve", "pool"] + self.lanes
        with ExitStack() as es:
            sems = {}
            for i, k in enumerate(keys):
                sems[k] = es.enter_context(nc.semaphore("s%d" % i))
            block = es.enter_context(nc.Block())

            def mk(engname):
                def body(e):
                    waited = {}
                    for op in self.ops[engname]:
                        need = {}
                        for a in op.sdeps:
                            key = a.lane if a.is_dma else a.eng
                            if a.count > need.get(key, 0):
                                need[key] = a.count
                        for key, val in need.items():
                            if waited.get(key, 0) < val:
                                e.wait_ge(sems[key], val)
                                waited[key] = val
                        ins = op.fn(e)
                        if op.marked:
                            key = op.lane if op.is_dma else op.eng
                            if isinstance(ins, list):
                                for x in ins:
                                    x.then_inc(sems[key], op.inc // len(ins))
                            else:
                                ins.then_inc(sems[key], op.inc if op.is_dma else 1)
                return body

            block.sync(mk("sp"))
            block.scalar(mk("act"))
            block.vector(mk("dve"))
            block.gpsimd(mk("pool"))
            block.tensor(mk("pe"))


class Arena:
    def __init__(self, nc, base=16512, limit=229376):
        self.nc, self.top, self.limit, self.n = nc, base, limit, 0
        self.peak = base

    def alloc(self, name, shape, dtype):
        esz = 4 if dtype == F32 else 2
        nb = esz
        for s in shape[1:]:
            nb *= s
        nb = (nb + 63) // 64 * 64
        off = self.top
        self.top += nb
        self.peak = max(self.peak, self.top)
        assert self.top <= self.limit, "SBUF overflow at %s: %d" % (name, self.top)
        self.n += 1
        return self.nc.alloc_sbuf_tensor_at("%s_%d" % (name, self.n), list(shape), dtype, offset=off).ap()

    def mark(self):
        return self.top

    def release(self, m):
        self.top = m


def _rope_tab(pos, d):
    inv = (np.float32(10000.0) ** (-(np.arange(0, d, 2, dtype=np.float32)) / np.float32(d))).astype(np.float32)
    ang = (pos.astype(np.float32)[:, None] * inv[None, :]).astype(np.float32)
    return np.cos(ang).astype(np.float32), np.sin(ang).astype(np.float32)


class Cols:
    def __init__(self):
        self.n = 0
        self.m = {}

    def add(self, name, w):
        self.m[name] = (self.n, w)
        self.n += w


def _layout():
    c0 = Cols()
    c0.add("dq", 12); c0.add("dk", 12); c0.add("dec1", 192)
    c0.add("G128", 768); c0.add("G8", 768)
    c0.add("causal", 128); c0.add("mask_s", 32); c0.add("blk", 128); c0.add("rowm", 4)
    c0.add("nhalf", 1)
    c1 = Cols()
    c1.add("mb4", 512); c1.add("mb4h", 512)
    for g in range(3):
        c1.add("sm%d" % g, (WINS[g] // 128) * 16); c1.add("smn%d" % g, 64)
    cc = Cols()
    cc.add("ident", 128); cc.add("ones", 64); cc.add("ones32", 64)
    return cc, c0, c1


CC_L, C0_L, C1_L = _layout()


def build_consts(c):
    b, j = divmod(c, 4)
    h = np.arange(6, dtype=np.float64)
    lg = np.log1p(-np.exp2(-5.0 - h))
    p = np.arange(128)
    cc = np.zeros((128, CC_L.n), np.float32)
    c0 = np.zeros((128, C0_L.n), np.float32)
    c1 = np.zeros((128, C1_L.n), np.float32)

    def put(arr, lay, name, val):
        o, w = lay.m[name]
        arr[:, o:o + w] = np.asarray(val, np.float32).reshape(128, w)

    put(cc, CC_L, "ident", np.eye(128))
    put(cc, CC_L, "ones", np.ones((128, 64)))
    o32 = np.zeros((128, 64)); o32[:32] = 1.0
    put(cc, CC_L, "ones32", o32)
    pos0 = np.zeros((65, 128), np.float32)
    for i in range(64):
        pos0[i] = np.maximum(2048 * (j - 3) + 128 * i + p, 0)
    pos0[64, :32] = 8192 + (p[:32] % 8)
    cs, sn = _rope_tab(pos0.reshape(-1), 128)
    rope0 = np.concatenate([cs.reshape(65, 128, 64), sn.reshape(65, 128, 64)], 2).astype(np.float32)
    pos1 = np.zeros((33, 128), np.float32)
    for i in range(32):
        pos1[i] = np.maximum(2048 * (j - 1) + 128 * i + p, 0)
    pos1[32, :32] = 8192 + (p[:32] % 8)
    cs, sn = _rope_tab(pos1.reshape(-1), 64)
    rope1 = np.concatenate([cs.reshape(33, 128, 32), sn.reshape(33, 128, 32)], 2).astype(np.float32)
    sc = 128.0 ** -0.5
    dq = np.zeros((128, 2, 6)); dk = np.zeros((128, 2, 6))
    dq[:, 0] = np.exp((p[:, None] + 1.0) * lg[None]); dk[:, 0] = np.exp(-(p[:, None] + 1.0) * lg[None]) * sc
    dq[:, 1] = np.exp(((p[:, None] % 8) + 1.0) * lg[None]); dk[:, 1] = np.exp(-((p[:, None] % 8) + 1.0) * lg[None]) * sc
    put(c0, C0_L, "dq", dq); put(c0, C0_L, "dk", dk)
    dec1 = np.zeros((128, 32, 6))
    for i in range(32):
        dec1[:, i] = np.exp((4095.0 - (128 * i + p[:, None])) * lg[None]) * sc
    put(c0, C0_L, "dec1", dec1)
    put(c0, C0_L, "G128", np.broadcast_to(np.repeat(np.exp(128.0 * lg), 128)[None], (128, 768)))
    put(c0, C0_L, "G8", np.broadcast_to(np.repeat(np.exp(8.0 * lg), 128)[None], (128, 768)))
    put(c0, C0_L, "causal", (p[:, None] <= p[None, :]).astype(np.float32))
    ms = np.zeros((128, 32))
    for k in range(32):
        for q in range(32):
            ms[k, q] = 1.0 if (k // 8 == q // 8 and k % 8 <= q % 8) else 0.0
    put(c0, C0_L, "mask_s", ms)
    blk = np.zeros((128, 4, 32))
    for bl in range(4):
        blk[:, bl, 8 * bl:8 * bl + 8] = 1.0
    put(c0, C0_L, "blk", blk)
    rowm = np.zeros((128, 4))
    for bl in range(4):
        rowm[8 * bl:8 * bl + 8, bl] = 1.0
    put(c0, C0_L, "rowm", rowm)
    put(c0, C0_L, "nhalf", np.full((128, 1), -0.5))
    mprev = np.where(p[:, None] >= p[None, :], 1.0, 0.0)
    mcur = np.where(p[:, None] <= p[None, :], 1.0, 0.0)
    put(c1, C1_L, "mb4", np.concatenate([mprev, mcur, mprev, mcur], 1))
    mprevh = mprev * (0.0 if j == 0 else 1.0)
    put(c1, C1_L, "mb4h", np.concatenate([mprevh, mcur, mprevh, mcur], 1))
    for g, dil in enumerate(DILS):
        t = np.arange(8)
        nm = WINS[g] // 128
        mf = (((p[:, None] - t[None]) % dil == 0) & (p[:, None] >= t[None])).astype(np.float32)
        mr = (((p[:, None] - t[None]) % dil == 0)).astype(np.float32)
        sm = np.zeros((128, nm, 2, 8), np.float32)
        for mt in range(nm):
            sm[:, mt, :, :] = (mf if mt == 0 else mr)[:, None, :]
        put(c1, C1_L, "sm%d" % g, sm)
        mn = np.zeros((128, 4, 2, 8), np.float32)
        for r_ in range(32):
            blr, tr = divmod(r_, 8)
            for tq in range(8):
                if tr <= tq and (tq - tr) % dil == 0:
                    mn[r_, blr, :, tq] = 1.0
        put(c1, C1_L, "smn%d" % g, mn)
    return cc, c0, c1, rope0, rope1


class Prog:
    def __init__(self, stage=99, debug=False):
        self.stage = stage
        self.debug = debug
        nc = self.nc = bass.Bass("TRN2", target_bir_lowering=False)
        self.S = Sched()
        self.A = Arena(nc)
        specs = self.specs = {}

        def din(name, shape):
            specs[name] = (list(shape), F32, "ExternalInput")

        def dout(name, shape):
            specs[name] = (list(shape), F32, "ExternalOutput")

        def dscr(name, shape, dt=F32):
            specs[name] = (list(shape), dt, "Internal")

        class LazyD(dict):
            def __missing__(d, name):
                shape, dt, kind = specs[name]
                if kind == "Internal":
                    ap = nc.dram_tensor(name, shape, dt).ap()
                else:
                    ap = nc.dram_tensor(name, shape, dt, kind=kind).ap()
                d[name] = ap
                return ap

        D = self.D = LazyD()
        din("xp", [T, 1024]); din("xpv", [T, 1024]); din("xpp", [2 * T, 1024]); din("xs", [TS, 1024]); din("mem", [256, 1024])
        din("rope0", [65, 128, 128]); din("rope1", [33, 128, 64])
        din("cmkv", [2, 4, 256, 512]); din("sret", [4, 6, 128, 128])
        din("cw0", [4, 128, 512]); din("cw1", [4, 512, 512]); din("cw2", [4, 2048, 512])
        din("w_in_a", [1024, 3328]); din("w_in_b", [1024, 1024]); din("w_out", [2, 1024, 1024])
        din("w_kv", [1024, 1536]); din("w_mem", [2, 1024, 512])
        din("ln_mix_g", [2, 1024]); din("ln_mix_b", [2, 1024]); din("ln_ffn_g", [2, 1024]); din("ln_ffn_b", [2, 1024])
        din("w_ffn_in", [2, 1024, 2 * FF]); din("w_ffn_out", [2, FF, 1024])
        din("cc", [128, CC_L.n]); din("c0", [128, C0_L.n]); din("c1", [128, C1_L.n])
        dout("yp", [T, 1024]); dout("ys", [TS, 1024])
        dout("srp", [6, 128, 128]); dout("srs", [4, 6, 128, 128]); dout("mkv", [2, 256, 512])
        dout("wp0", [128, 512]); dout("wp1", [512, 512]); dout("wp2", [2048, 512])
        dout("ws0", [4, 128, 512]); dout("ws1", [4, 512, 512]); dout("ws2", [4, 2048, 512])
        dscr("x1s", [TT, 1024]); dscr("x2s", [TT, 1024])
        dscr("hK0", [128, 2, 128], BF16); dscr("hK1", [128, 2, 512], BF16); dscr("hK2", [128, 2, 2048], BF16)
        dscr("hV0", [128, 1, 256], BF16); dscr("hV1", [128, 4, 256], BF16); dscr("hV2", [128, 16, 256], BF16)
        dout("dbg_sin", [128, 768])
        if stage > 3:
            for nm_ in list(specs):
                if nm_ != "dbg_sin" or debug:
                    D[nm_]
        self.PS = nc.alloc_psum_tensor("ps", [128, 4096], F32).ap()
        self.PSB = self.PS.bitcast(BF16)
        self.fresh = {}
        self.build()

    def psf(self, b, n=512, p=128, off=0):
        return self.PS[0:p, 512 * b + off:512 * b + off + n]

    def psb(self, b, n=1024, p=128, off=0):
        return self.PSB[0:p, 1024 * b + off:1024 * b + off + n]

    def pbegin(self, *banks):
        for b in banks:
            self.fresh[b] = [True] * 4

    def mm(self, out, bank, lhsT, rhs, r, w=None, p0=0, p1=128, stop=False):
        fr = self.fresh[bank]
        qs = list(range(p0 // 32, (p1 + 31) // 32))
        start = fr[qs[0]]
        for q in qs:
            assert fr[q] == start
            fr[q] = False
        wk = [("ps", bank)] if w is None else w
        self.S.add("pe", lambda e: e.matmul(out, lhsT=lhsT, rhs=rhs, start=start, stop=stop, skip_group_check=True),
                   r=list(r) + ([] if start else wk), w=wk)

    def tr(self, out, bank, in_, ident, r):
        self.S.add("pe", lambda e: e.transpose(out, in_, ident), r=list(r), w=[("ps", bank)])

    def dma(self, q, out, in_, r, w, lane):
        self.S.add(q, lambda e: e.dma_start(out=out, in_=in_), r=r, w=w, dma=True, lane=lane)

    def dmas(self, q, pairs, r, w, lane):
        pairs = list(pairs)
        self.S.add(q, lambda e: [e.dma_start(out=o, in_=i) for (o, i) in pairs], r=r, w=w, dma=True, lane=lane, inc=16 * len(pairs))

    def act(self, out, in_, func, r, w, bias=None, scale=None):
        kw = {}
        if bias is not None:
            kw["bias"] = bias
        if scale is not None:
            kw["scale"] = scale
        self.S.add("act", lambda e: e.activation(out, in_, func, **kw), r=r, w=w)

    def tt(self, eng, out, in0, in1, op, r, w):
        self.S.add(eng, lambda e: e.tensor_tensor(out, in0, in1, op), r=r, w=w)

    def ts(self, eng, out, in0, s1, s2, op0, op1, r, w):
        if s2 is None:
            self.S.add(eng, lambda e: e.tensor_scalar(out, in0, s1, None, op0), r=r, w=w)
        else:
            self.S.add(eng, lambda e: e.tensor_scalar(out, in0, s1, s2, op0, op1), r=r, w=w)

    def stt(self, eng, out, in0, scalar, in1, op0, op1, r, w):
        self.S.add(eng, lambda e: e.scalar_tensor_tensor(out, in0, scalar, in1, op0, op1), r=r, w=w)

    def cp(self, eng, out, in_, r, w):
        if eng == "act":
            self.S.add("act", lambda e: e.copy(out, in_), r=r, w=w)
        else:
            self.S.add(eng, lambda e: e.tensor_copy(out, in_), r=r, w=w)

    def cst(self, blob, lay, name, p=128):
        o, w = lay.m[name]
        return blob[0:p, o:o + w]

    def load_ln(self, g_ap, b_ap, l):
        self.dma("sp", self.lnp[:, 0, :], g_ap[l:l + 1, :].broadcast_to([128, 1024]), r=[], w=["lnp0"], lane="lnp0")
        self.dma("sp", self.lnp[:, 1, :], b_ap[l:l + 1, :].broadcast_to([128, 1024]), r=[], w=["lnp1"], lane="lnp1")

    def layernorm(self, P, z, zkey, stat, skey):
        st6 = stat[0:P, 0:12]
        for hf in range(2):
            o6 = stat[0:P, 6 * hf:6 * hf + 6]
            zi = z[:, 512 * hf:512 * hf + 512]
            self.S.add("dve", lambda e, o6=o6, zi=zi: e.bn_stats(o6, zi), r=[zkey], w=[skey])
        mv = stat[0:P, 12:14]
        self.S.add("dve", lambda e: e.bn_aggr(mv, st6), r=[skey], w=[skey])
        ve = stat[0:P, 14:15]
        self.ts("dve", ve, stat[0:P, 13:14], LN_EPS, None, ALU.add, None, r=[skey], w=[skey])
        rstd = stat[0:P, 15:16]
        self.tt("pool", rstd, ve, self.cst(self.C0, C0_L, "nhalf", P), ALU.pow, r=[skey, "C0"], w=[skey])
        nmr = stat[0:P, 14:15]
        self.stt("dve", nmr, stat[0:P, 12:13], -1.0, rstd, ALU.mult, ALU.mult, r=[skey], w=[skey])
        self.act(z, z, AF.Identity, r=[zkey, skey], w=[zkey], bias=nmr, scale=rstd)
        self.tt("pool", z, z, self.lnp[0:P, 0, :], ALU.mult, r=[zkey, "lnp0"], w=[zkey])
        self.tt("pool", z, z, self.lnp[0:P, 1, :], ALU.add, r=[zkey, "lnp1"], w=[zkey])

    def to_XT(self, P, src_bf, skey, i, bank):
        c0 = 128 * i
        for c in range(8):
            self.tr(self.psb(bank, P, off=128 * c), bank, src_bf[:, 128 * c:128 * c + 128], self.identb[0:P, 0:P], r=[skey, "CCB"])
        src = self.psb(bank, 1024).rearrange("p (c t) -> p c t", c=8)[:, :, 0:P]
        self.cp("dve", self.XT[:, :, c0:c0 + P], src, r=[("ps", bank)], w=[("XT", i)])

    def rope_tm(self, P, src, H, d2, cos, sin, dst, rkeys, wkey, tmp, tkey):
        sv = src.rearrange("p (h t d) -> p h t d", h=H, t=2)
        dv = dst.rearrange("p (h t d) -> p h t d", h=H, t=2)
        lo, hi = sv[:, :, 0, :], sv[:, :, 1, :]
        cb = cos.unsqueeze(1).broadcast_to([P, H, d2])
        sb = sin.unsqueeze(1).broadcast_to([P, H, d2])
        t = [tmp[0:P, k, 0:H * d2].rearrange("p (h d) -> p h d", h=H) for k in range(4)]
        self.tt("dve", t[0], lo, cb, ALU.mult, r=rkeys, w=[(tkey, 0)])
        self.tt("dve", t[1], hi, sb, ALU.mult, r=rkeys, w=[(tkey, 1)])
        self.tt("dve", t[2], hi, cb, ALU.mult, r=rkeys, w=[(tkey, 2)])
        self.tt("dve", t[3], lo, sb, ALU.mult, r=rkeys, w=[(tkey, 3)])
        self.tt("pool", dv[:, :, 0, :], t[0], t[1], ALU.subtract, r=[(tkey, 0), (tkey, 1)], w=[wkey])
        self.tt("pool", dv[:, :, 1, :], t[2], t[3], ALU.add, r=[(tkey, 2), (tkey, 3)], w=[wkey])

    def proj_tm(self, P, i, W, wkey, col0, ncols, bank0):
        c0 = 128 * i
        done = 0
        b = bank0
        while done < ncols:
            n = min(512, ncols - done)
            self.pbegin(b)
            for d in range(8):
                self.mm(self.psf(b, n, P), b, self.XT[:, d, c0:c0 + P], W[:, d, col0 + done:col0 + done + n],
                        r=[("XT", i), wkey], p1=P, stop=(d == 7))
            done += n
            b += 1

    def build(self):
        nc, S, A, D = self.nc, self.S, self.A, self.D
        self.XT = A.alloc("XT", [128, 8, TT], BF16)
        self.CC = A.alloc("CC", [128, CC_L.n], F32)
        self.CCB = A.alloc("CCB", [128, CC_L.n], BF16)
        self.lnp = A.alloc("lnp", [128, 2, 1024], F32)
        self.memKT = A.alloc("memKT", [128, 2, 256], BF16)
        self.memV = A.alloc("memV", [128, 2, 256], BF16)
        self.dma("sp", self.CC, D["cc"], r=[], w=["CC"], lane="CC")
        self.cp("dve", self.CCB, self.CC, r=["CC"], w=["CCB"])
        self.identb = self.cst(self.CCB, CC_L, "ident")
        self.onesb = self.cst(self.CCB, CC_L, "ones")
        self.ones32b = self.cst(self.CCB, CC_L, "ones32")
        if self.stage <= -4:
            self.C0 = A.alloc("C0", [128, C0_L.n], F32)
            self.dma("sp", self.C0, D["c0"], r=[], w=["C0"], lane="C0")
            self.rt = [A.alloc("rt", [128, 128], F32) for _ in range(2)]
            self.rtn = 0
            m_ = A.mark()
            xin_ = A.alloc("xin_", [128, 1024], F32)
            xb_ = A.alloc("xb_", [128, 1024], BF16)
            for i in range(NT + 1):
                P = 128 if i < NT else TS
                src = D["xp"][128 * i:128 * i + 128, :] if i < NT else D["xs"]
                self.dma("sp", xin_[0:P, :], src, r=[], w=["xin_"], lane="xin_")
                self.cp("act", xb_[0:P, :], xin_[0:P, :], r=["xin_"], w=["xb_"])
                self.to_XT(P, xb_[0:P, :], "xb_", i, 6)
            S.barrier()
            A.release(m_)
            self.halo_kv()
            if self.stage != -4.05:
                self.layer1()
            S.barrier()
            S.emit(nc)
            return
        if self.stage <= -2:
            self.C0 = A.alloc("C0", [128, C0_L.n], F32)
            self.dma("sp", self.C0, D["c0"], r=[], w=["C0"], lane="C0")
            self.Sst = A.alloc("Sst", [128, 768], F32)
            self.Sstb = A.alloc("Sstb", [128, 768], BF16)
            self.rt = [A.alloc("rt", [128, 128], F32) for _ in range(2)]
            self.rtn = 0
            S.add("pool", lambda e: e.memset(self.Sst, 0.0), r=[], w=["Sst"])
            S.add("pool", lambda e: e.memset(self.Sstb, 0.0), r=[], w=["Sstb"])
            self.mem_kv(0)
            self.l0_pass2(1)
            S.barrier()
            S.emit(nc)
            return
        if self.stage < 0:
            self.C0 = A.alloc("C0", [128, C0_L.n], F32)
            self.dma("sp", self.C0, D["c0"], r=[], w=["C0"], lane="C0")
            self.mem_kv(0)
            S.barrier()
            S.emit(nc)
            return
        self.layer0()
        if self.stage > 3:
            S.barrier()
            self.layer1()
        S.barrier()
        S.emit(nc)

    def mem_kv(self, l, stat_bank=6):
        S, A, D = self.S, self.A, self.D
        st = self.stage
        m = A.mark()
        Wm = A.alloc("Wm", [128, 8, 512], BF16)
        memT = A.alloc("memT", [128, 8, 256], BF16)
        mtile = A.alloc("mtile", [128, 1024], F32)
        mtb = A.alloc("mtb", [128, 1024], BF16)
        mo = A.alloc("mo", [128, 512], F32)
        wm_v = D["w_mem"][l].rearrange("(c p) f -> p c f", p=128)
        self.dmas("pool", [(Wm[:, c, :], wm_v[:, c, :]) for c in range(8)], r=[], w=["Wm"], lane="Wm")
        for t in range(2):
            self.dma("sp", mtile, D["mem"][128 * t:128 * t + 128, :], r=[], w=["mtile"], lane="mtile")
            self.cp("act", mtb, mtile, r=["mtile"], w=["mtb"])
            if st == -1.1:
                continue
            for c in range(8):
                self.tr(self.psb(7, 128, off=128 * c), 7, mtb[:, 128 * c:128 * c + 128], self.identb, r=["mtb", "CCB"])
            self.cp("dve", memT[:, :, 128 * t:128 * t + 128], self.psb(7, 1024).rearrange("p (c t) -> p c t", c=2 * 4),
                    r=[("ps", 7)], w=["memT"])
        if st in (-1.1, -1.2):
            S.barrier(); A.release(m); return
        for t in range(2):
            self.pbegin(6)
            for d in range(8):
                self.mm(self.psf(6), 6, memT[:, d, 128 * t:128 * t + 128], Wm[:, d, :], r=["memT", "Wm"], stop=(d == 7))
            if st == -1.25:
                continue
            self.cp("dve", mo, self.psf(6), r=[("ps", 6)], w=["mo"])
            if st == -1.3:
                continue
            self.cp("dve", self.memV[:, t, :], self.psf(6, 256, off=256), r=[("ps", 6)], w=["memV"])
            if st == -1.4:
                continue
            if st == -1.51 and t == 1:
                continue
            if st == -1.52:
                self.dma("sp", D["dbg_sin"][:, 0:512], mo, r=["mo"], w=[], lane="mo")
                continue
            if st == -1.53:
                self.dma("pool", D["mkv"][l, 128 * t:128 * t + 128, :], mo, r=["mo"], w=[], lane="mo")
                continue
            self.dma("sp", D["mkv"][l, 128 * t:128 * t + 128, :], mo, r=["mo"], w=[], lane="mo")
        if st in (-1.25, -1.3, -1.4, -1.5, -1.51, -1.52, -1.53, -1.54):
            S.barrier(); A.release(m); return
        for c in range(2):
            self.pbegin(7)
            for d in range(8):
                self.mm(self.psf(7, 256), 7, Wm[:, d, 128 * c:128 * c + 128], memT[:, d, :], r=["memT", "Wm"], stop=(d == 7))
            self.cp("dve", self.memKT[:, c, :], self.psf(7, 256), r=[("ps", 7)], w=["memKT"])
        S.barrier()
        A.release(m)

    def mem_attn(self, P, QMT, qkey, KT, V, kvkeys, dst, dkey, n_q, PTm, rL, banks=(2, 3, 7)):
        bs0, bs1, bo = banks
        self.pbegin(bs0, bs1)
        for h in range(4):
            c, par = divmod(h, 2)
            bank = bs0 if par == 0 else bs1
            for mt in range(2):
                slot = (2 * c + mt) * n_q
                self.mm(self.psf(bank, n_q, off=slot), bank, KT[64 * par:64 * par + 64, c, 128 * mt:128 * mt + 128],
                        QMT[64 * par:64 * par + 64, c, :], r=[qkey] + kvkeys, stop=True)
        for k, bank in enumerate((bs0, bs1)):
            self.act(PTm[:, 4 * k * n_q:(4 * k + 4) * n_q], self.psf(bank, 4 * n_q), AF.Exp, r=[("ps", bank)], w=["PTm"], scale=0.125)
        self.pbegin(bo)
        for h in range(4):
            c, par = divmod(h, 2)
            for mt in range(2):
                ix = 4 * par + 2 * c + mt
                rhs = PTm[:, ix * n_q:(ix + 1) * n_q]
                self.mm(self.psf(bo, n_q, off=c * n_q)[64 * par:64 * par + 64, :], bo, V[:, mt, 64 * h:64 * h + 64], rhs,
                        r=["PTm"] + kvkeys, p0=64 * par, p1=64 * par + 64)
            for mt in range(2):
                ix = 4 * par + 2 * c + mt
                rhs = PTm[:, ix * n_q:(ix + 1) * n_q]
                self.mm(self.psf(bo, n_q, off=(2 + c) * n_q)[64 * par:64 * par + 64, :], bo, self.onesb, rhs,
                        r=["PTm", "CCB"], p0=64 * par, p1=64 * par + 64, stop=True)
        S = self.S
        rl = rL[:, 0:2 * n_q]
        S.add("dve", lambda e: e.reciprocal(rl, self.psf(bo, 2 * n_q, off=2 * n_q)), r=[("ps", bo)], w=["rL"])
        self.tt("dve", dst, self.psf(bo, 2 * n_q).rearrange("p (c q) -> p c q", c=2), rl.rearrange("p (c q) -> p c q", c=2),
                ALU.mult, r=[("ps", bo), "rL"], w=[dkey])

    def layer0(self):
        nc, S, A, D = self.nc, self.S, self.A, self.D
        C0 = self.C0 = A.alloc("C0", [128, C0_L.n], F32)
        self.dma("sp", C0, D["c0"], r=[], w=["C0"], lane="C0")
        Sst = self.Sst = A.alloc("Sst", [128, 768], F32)
        Sstb = self.Sstb = A.alloc("Sstb", [128, 768], BF16)
        self.rt = [A.alloc("rt", [128, 128], F32) for _ in range(2)]
        self.rtn = 0
        dec1 = self.cst(C0, C0_L, "dec1").rearrange("p (i h) -> p i h", i=32)
        wa_v = D["w_in_a"].rearrange("(c p) f -> p c f", p=128)

        m1 = A.mark()
        Wkv = A.alloc("Wkv0", [128, 8, 1536], BF16)
        self.dmas("pool", [(Wkv[:, c, :], wa_v[:, c, 768:2304]) for c in range(8)], r=[], w=["Wkv0"], lane="Wkv0")
        xin = [A.alloc("xin", [128, 1024], F32) for _ in range(2)]
        xb = A.alloc("xb", [128, 1024], BF16)
        Kb = A.alloc("Kb", [128, 768], BF16)
        Vb = A.alloc("Vb", [128, 768], BF16)
        tmp = A.alloc("ropetmp", [128, 4, 384], F32)
        self.pbegin(4, 5)
        NP1 = 2 * NT
        for i in range(NP1):
            xi = xin[i % 2]
            xk = ("xin", i % 2)
            self.dma("sp", xi, D["xpp"][128 * i:128 * i + 128, :], r=[], w=[xk], lane="xin%d" % (i % 2))
            rc, rs, rk = self.rope_load(D["rope0"], i, 64)
            self.cp("act", xb, xi, r=[xk], w=["xb"])
            ti = i % 2
            self.to_XT(128, xb, "xb", ti, 6)
            c0 = 128 * ti
            for (col0, b0) in ((0, 0), (768, 2)):
                for (off, n, b) in ((0, 512, b0), (512, 256, b0 + 1)):
                    self.pbegin(b)
                    for d in range(8):
                        self.mm(self.psf(b, n), b, self.XT[:, d, c0:c0 + 128], Wkv[:, d, col0 + off:col0 + off + n],
                                r=[("XT", ti), "Wkv0"], stop=(d == 7))
            self.rope_tm(128, self.PS[:, 0:768], 6, 64, rc, rs, Kb, [("ps", 0), ("ps", 1), rk], "Kb", tmp, "rtmp")
            self.tt("dve", Vb.rearrange("p (h v) -> p h v", h=6), self.PS[:, 1024:1792].rearrange("p (h v) -> p h v", h=6),
                    dec1[:, i, :].unsqueeze(2).broadcast_to([128, 6, 128]), ALU.mult, r=[("ps", 2), ("ps", 3), "C0"], w=["Vb"])
            for h in range(6):
                b = 4 if h < 4 else 5
                self.mm(self.psf(b, 128, off=128 * (h % 4)), b, Kb[:, 128 * h:128 * h + 128], Vb[:, 128 * h:128 * h + 128],
                        r=["Kb", "Vb"], stop=(i == NP1 - 1))
        self.cp("dve", Sst[:, 0:512], self.psf(4), r=[("ps", 4)], w=["Sst"])
        self.cp("dve", Sst[:, 512:768], self.psf(5, 256), r=[("ps", 5)], w=["Sst"])
        self.cp("dve", Sstb, Sst, r=["Sst"], w=["Sstb"])
        if self.debug:
            self.dma("sp", D["dbg_sin"], Sst, r=["Sst"], w=[], lane="dbg")
        if self.stage <= 1:
            return
        S.barrier()
        A.release(m1)
        self.mem_kv(0)
        for seg in range(2):
            self.l0_pass2(seg)
            if self.stage <= 2 and seg == 1:
                return
            self.ffn(0, seg == 1, False)
            if seg == 0:
                self.halo_kv()
            if self.stage <= 2.5 and seg == 0:
                return

    def rope_load(self, tab, idx, d2):
        k = self.rtn % 2
        self.rtn += 1
        rt = self.rt[k]
        self.dma("sp", rt[:, 0:2 * d2], tab[idx], r=[], w=[("rt", k)], lane="rt%d" % k)
        return rt[:, 0:d2], rt[:, d2:2 * d2], ("rt", k)

    def l0_pass2(self, seg):
        nc, S, A, D = self.nc, self.S, self.A, self.D
        C0 = self.C0
        Sst, Sstb = self.Sst, self.Sstb
        dq = self.cst(C0, C0_L, "dq").rearrange("p (v h) -> p v h", v=2)
        dk = self.cst(C0, C0_L, "dk").rearrange("p (v h) -> p v h", v=2)
        wa_v = D["w_in_a"].rearrange("(c p) f -> p c f", p=128)
        m2 = A.mark()
        Wa = A.alloc("Wa", [128, 8, 3328], BF16)
        self.dmas("pool", [(Wa[:, c, :], wa_v[:, c, :]) for c in range(8)], r=[], w=["Wa"], lane="Wa")
        Wo = A.alloc("Wo", [128, 8, 1024], BF16)
        wo_v = D["w_out"][0].rearrange("(c p) f -> p c f", p=128)
        self.dmas("pool", [(Wo[:, c, :], wo_v[:, c, :]) for c in range(8)], r=[], w=["Wo"], lane="Wo")
        wak = ["Wa"]
        self.load_ln(D["ln_mix_g"], D["ln_mix_b"], 0)
        xin = [A.alloc("xin", [128, 1024], F32) for _ in range(2)]
        xb = A.alloc("xb", [128, 1024], BF16)
        tmp = A.alloc("ropetmp", [128, 4, 384], F32)
        Qr = A.alloc("Qr", [128, 768], F32)
        Kr = A.alloc("Kr", [128, 768], F32)
        Qb = A.alloc("Qb", [128, 768], BF16)
        Kb = A.alloc("Kb", [128, 768], BF16)
        Vb = A.alloc("Vb", [128, 768], BF16)
        Gs = A.alloc("Gs", [128, 768], F32)
        QT = A.alloc("QT", [128, 6, 128], BF16)
        KT = A.alloc("KT", [128, 6, 128], BF16)
        PT = A.alloc("PT", [128, 6, 128], BF16)
        Of = Qr
        gst = A.alloc("gst", [128, 64], F32)
        mixb = A.alloc("mixb", [128, 768], BF16)
        mixT = A.alloc("mixT", [128, 8, 128], BF16)
        QMT = A.alloc("QMT", [128, 2, 128], BF16)
        PTm = A.alloc("PTm", [128, 1024], BF16)
        rL = A.alloc("rL", [128, 256], F32)
        stat = A.alloc("stat", [128, 16], F32)
        stmp = Kr
        causal = self.cst(C0, C0_L, "causal")
        mask_s = self.cst(C0, C0_L, "mask_s", 32)
        G128 = self.cst(C0, C0_L, "G128")
        G8 = self.cst(C0, C0_L, "G8")
        Stf = A.alloc("Stf", [128, 4, 768], F32)
        Stb = A.alloc("Stbf", [128, 4, 768], BF16)
        Qm = A.alloc("Qm", [128, 6, 4, 32], BF16)
        Kbm = A.alloc("Kbm", [32, 4, 768], BF16)
        mks = A.alloc("mks", [128, 2, 256], BF16)
        mvs = A.alloc("mvs", [128, 2, 256], BF16)
        ctile = A.alloc("ctile", [128, 512], F32)
        ctb = A.alloc("ctb", [128, 512], BF16)

        tl = range(NT + (1 if seg == 1 else 0))
        if self.stage <= -2:
            tl = [0] if self.stage > -3 else [NT]
        for i in tl:
            P = 128 if i < NT else TS
            v = 0 if i < NT else 1
            c0 = 128 * i
            xi = xin[i % 2]
            xk = ("xin", i % 2)
            xsrc = D["xp"] if seg == 1 else D["xpv"]
            src = xsrc[128 * i:128 * i + 128, :] if i < NT else D["xs"]
            self.dma("sp", xi[0:P, :], src, r=[], w=[xk], lane="xin%d" % (i % 2))
            rc, rs, rk = self.rope_load(D["rope0"], (32 + 16 * seg + i) if i < NT else 64, 64)
            rc, rs = rc[0:P, :], rs[0:P, :]
            self.cp("act", xb[0:P, :], xi[0:P, :], r=[xk], w=["xb"])
            self.to_XT(P, xb[0:P, :], "xb", i, 6)
            self.proj_tm(P, i, Wa, wak[0], 0, 768, 0)
            self.rope_tm(P, self.PS[0:P, 0:768], 6, 64, rc, rs, Qr[0:P, :], [("ps", 0), ("ps", 1), rk], "Qr", tmp, "rtmp")
            self.tt("pool", Qb[0:P, :].rearrange("p (h v) -> p h v", h=6), Qr[0:P, :].rearrange("p (h v) -> p h v", h=6),
                    dq[0:P, v, :].unsqueeze(2).broadcast_to([P, 6, 128]), ALU.mult, r=["Qr", "C0"], w=["Qb"])
            if self.stage < 0 and int(abs(self.stage) * 10 + 1e-6) - 10 * int(abs(self.stage)) == 1:
                break
            self.proj_tm(P, i, Wa, wak[0], 768, 768, 2)
            self.rope_tm(P, self.PS[0:P, 1024:1792], 6, 64, rc, rs, Kr[0:P, :], [("ps", 2), ("ps", 3), rk], "Kr", tmp, "rtmp")
            self.tt("pool", Kb[0:P, :].rearrange("p (h v) -> p h v", h=6), Kr[0:P, :].rearrange("p (h v) -> p h v", h=6),
                    dk[0:P, v, :].unsqueeze(2).broadcast_to([P, 6, 128]), ALU.mult, r=["Kr", "C0"], w=["Kb"])
            if self.stage < 0 and int(abs(self.stage) * 10 + 1e-6) - 10 * int(abs(self.stage)) == 2:
                break
            for (srcb, skey, dstT, dkey, bank) in ((Qb, "Qb", QT, "QT", 4), (Kb, "Kb", KT, "KT", 5)):
                for h in range(6):
                    self.tr(self.psb(bank, P, off=128 * h), bank, srcb[0:P, 128 * h:128 * h + 128], self.identb[0:P, 0:P], r=[skey, "CCB"])
                self.cp("dve", dstT[:, :, 0:P], self.psb(bank, 768).rearrange("p (h t) -> p h t", h=6)[:, :, 0:P], r=[("ps", bank)], w=[dkey])
            if self.stage < 0 and int(abs(self.stage) * 10 + 1e-6) - 10 * int(abs(self.stage)) == 3:
                break
            self.proj_tm(P, i, Wa, wak[0], 1536, 768, 0)
            self.cp("act", Vb[0:P, :], self.PS[0:P, 0:768], r=[("ps", 0), ("ps", 1)], w=["Vb"])
            self.proj_tm(P, i, Wa, wak[0], 2304, 768, 2)
            self.act(Gs[0:P, :], self.PS[0:P, 1024:1792], AF.Silu, r=[("ps", 2), ("ps", 3)], w=["Gs"])
            if self.stage < 0 and int(abs(self.stage) * 10 + 1e-6) - 10 * int(abs(self.stage)) == 4:
                break
            self.pbegin(7)
            for c in range(2):
                for d in range(8):
                    self.mm(self.psf(7, P, off=128 * c), 7, Wa[:, d, 3072 + 128 * c:3072 + 128 * c + 128], self.XT[:, d, c0:c0 + P],
                            r=[("XT", i)] + wak, stop=(d == 7))
            self.cp("act", QMT[:, :, 0:P], self.psf(7, 256).rearrange("p (c t) -> p c t", c=2)[:, :, 0:P], r=[("ps", 7)], w=["QMT"])
            if self.stage < 0 and int(abs(self.stage) * 10 + 1e-6) - 10 * int(abs(self.stage)) == 5:
                break
            if i < NT:
                self.pbegin(4, 5)
                for h in range(6):
                    b = 4 if h < 4 else 5
                    self.mm(self.psf(b, 128, off=128 * (h % 4)), b, KT[:, h, :], QT[:, h, :], r=["KT", "QT"], stop=True)
                self.tt("dve", PT, self.PS[:, 2048:2816].rearrange("p (h q) -> p h q", h=6),
                        causal.unsqueeze(1).broadcast_to([128, 6, 128]), ALU.mult, r=[("ps", 4), ("ps", 5), "C0"], w=["PT"])
                self.pbegin(0, 1)
                for h in range(6):
                    b = 0 if h < 4 else 1
                    o = self.psf(b, 128, off=128 * (h % 4))
                    self.mm(o, b, PT[:, h, :], Vb[:, 128 * h:128 * h + 128], r=["PT", "Vb"])
                    self.mm(o, b, QT[:, h, :], Sstb[:, 128 * h:128 * h + 128], r=["QT", "Sstb"], stop=True)
                self.pbegin(2, 3)
                for h in range(6):
                    b = 2 if h < 4 else 3
                    self.mm(self.psf(b, 128, off=128 * (h % 4)), b, Kb[:, 128 * h:128 * h + 128], Vb[:, 128 * h:128 * h + 128],
                            r=["Kb", "Vb"], stop=True)
                self.tt("dve", stmp, self.PS[:, 1024:1792], Sst, ALU.add, r=[("ps", 2), ("ps", 3), "Sst"], w=["Kr"])
                self.tt("pool", Sst, stmp, G128, ALU.mult, r=["Kr", "C0"], w=["Sst"])
                self.cp("pool", Sstb, Sst, r=["Sst"], w=["Sstb"])
                if i == NT - 1 and seg == 1:
                    self.dma("sp", D["srp"].rearrange("h d v -> d h v"), Sst.rearrange("p (h v) -> p h v", h=6), r=["Sst"], w=[], lane="srp")
            else:
                self.dma("sp", Stf.rearrange("p b (h v) -> p b h v", h=6), D["sret"].rearrange("b h d v -> d b h v"), r=[], w=["Stf"], lane="Stf")
                self.cp("act", Stb, Stf, r=["Stf"], w=["Stb"])
                self.pbegin(4)
                for h in range(6):
                    self.mm(self.psf(4, 32, 32, off=32 * h), 4, KT[:, h, 0:32], QT[:, h, 0:32], r=["KT", "QT"], p1=32, stop=True)
                self.tt("dve", PT[0:32, :, 0:32], self.psf(4, 192, 32).rearrange("p (h q) -> p h q", h=6),
                        mask_s.unsqueeze(1).broadcast_to([32, 6, 32]), ALU.mult, r=[("ps", 4), "C0"], w=["PT"])
                blk = self.cst(C0, C0_L, "blk").rearrange("p (b q) -> p b q", b=4)
                self.tt("dve", Qm, QT[:, :, 0:32].unsqueeze(2).broadcast_to([128, 6, 4, 32]),
                        blk.unsqueeze(1).broadcast_to([128, 6, 4, 32]), ALU.mult, r=["QT", "C0"], w=["Qm"])
                rowm = self.cst(C0, C0_L, "rowm", 32)
                self.tt("dve", Kbm, Kb[0:32, :].unsqueeze(1).broadcast_to([32, 4, 768]),
                        rowm.unsqueeze(2).broadcast_to([32, 4, 768]), ALU.mult, r=["Kb", "C0"], w=["Kbm"])
                self.pbegin(0, 1)
                for h in range(6):
                    b = 0 if h < 4 else 1
                    o = self.psf(b, 128, 32, off=128 * (h % 4))
                    self.mm(o, b, PT[0:32, h, 0:32], Vb[0:32, 128 * h:128 * h + 128], r=["PT", "Vb"], p1=32)
                    for bl in range(4):
                        self.mm(o, b, Qm[:, h, bl, :], Stb[:, bl, 128 * h:128 * h + 128], r=["Qm", "Stb"], p1=32, stop=(bl == 3))
                for bl in range(4):
                    self.pbegin(2, 3)
                    for h in range(6):
                        b = 2 if h < 4 else 3
                        self.mm(self.psf(b, 128, off=128 * (h % 4)), b, Kbm[:, bl, 128 * h:128 * h + 128], Vb[0:32, 128 * h:128 * h + 128],
                                r=["Kbm", "Vb"], stop=True)
                    self.tt("dve", stmp, self.PS[:, 1024:1792], Stf[:, bl, :], ALU.add, r=[("ps", 2), ("ps", 3), "Stf"], w=["Kr"])
                    self.tt("pool", Stf[:, bl, :], stmp, G8, ALU.mult, r=["Kr", "C0"], w=["Stf"])
                self.dma("sp", D["srs"].rearrange("b h d v -> d b h v"), Stf.rearrange("p b (h v) -> p b h v", h=6), r=["Stf"], w=[], lane="srs")
            if self.stage < 0 and int(abs(self.stage) * 10 + 1e-6) - 10 * int(abs(self.stage)) == 6:
                break
            self.cp("act", Of[0:P, :], self.PS[0:P, 0:768], r=[("ps", 0), ("ps", 1)], w=["Qr"])
            for h in range(6):
                o6 = gst[0:P, 6 * h:6 * h + 6]
                oi = Of[0:P, 128 * h:128 * h + 128]
                S.add("dve", lambda e, o6=o6, oi=oi: e.bn_stats(o6, oi), r=["Qr"], w=["gst"])
                mv = gst[0:P, 36 + 2 * h:38 + 2 * h]
                S.add("dve", lambda e, o6=o6, mv=mv: e.bn_aggr(mv, o6), r=["gst"], w=["gst"])
            mvv = gst[0:P, 36:48].rearrange("p (h t) -> p h t", t=2)
            ve = gst[0:P, 48:54]
            self.ts("dve", ve, mvv[:, :, 1], LN_EPS, None, ALU.add, None, r=["gst"], w=["gst"])
            rs = gst[0:P, 54:60]
            self.tt("pool", rs, ve, self.cst(C0, C0_L, "nhalf", P).broadcast_to([P, 6]), ALU.pow, r=["gst", "C0"], w=["gst"])
            Ov = Of[0:P, :].rearrange("p (h v) -> p h v", h=6)
            self.tt("dve", Ov, Ov, mvv[:, :, 0].unsqueeze(2).broadcast_to([P, 6, 128]), ALU.subtract, r=["Qr", "gst"], w=["Qr"])
            self.tt("dve", Ov, Ov, rs.unsqueeze(2).broadcast_to([P, 6, 128]), ALU.mult, r=["Qr", "gst"], w=["Qr"])
            self.tt("pool", mixb[0:P, :], Of[0:P, :], Gs[0:P, :], ALU.mult, r=["Qr", "Gs"], w=["mixb"])
            for h in range(6):
                self.tr(self.psb(4, P, off=128 * h), 4, mixb[0:P, 128 * h:128 * h + 128], self.identb[0:P, 0:P], r=["mixb", "CCB"])
            self.cp("act", mixT[:, 0:6, 0:P], self.psb(4, 768).rearrange("p (h t) -> p h t", h=6)[:, :, 0:P], r=[("ps", 4)], w=["mixT"])
            if self.stage < 0 and int(abs(self.stage) * 10 + 1e-6) - 10 * int(abs(self.stage)) == 7:
                break
            if i < NT:
                self.mem_attn(128, QMT, "QMT", self.memKT, self.memV, ["memKT", "memV"], mixT[:, 6:8, :], "mixT", 128, PTm, rL)
            else:
                for bl in range(4):
                    for t in range(2):
                        self.dma("sp", ctile, D["cmkv"][0, bl, 128 * t:128 * t + 128, :], r=[], w=["ctile"], lane="ctile")
                        self.cp("act", ctb, ctile, r=["ctile"], w=["ctb"])
                        for c in range(2):
                            self.tr(self.psb(5, 128, off=128 * c), 5, ctb[:, 128 * c:128 * c + 128], self.identb, r=["ctb", "CCB"])
                        self.cp("dve", mks[:, :, 128 * t:128 * t + 128], self.psb(5, 256).rearrange("p (c t) -> p c t", c=2), r=[("ps", 5)], w=["mks"])
                        self.cp("pool", mvs[:, t, :], ctb[:, 256:512], r=["ctb"], w=["mvs"])
                    self.mem_attn(128, QMT[:, :, 8 * bl:8 * bl + 8], "QMT", mks, mvs, ["mks", "mvs"], mixT[:, 6:8, 8 * bl:8 * bl + 8], "mixT", 8, PTm, rL)
            if self.debug:
                for c in range(8):
                    pass
            if self.stage < 0 and int(abs(self.stage) * 10 + 1e-6) - 10 * int(abs(self.stage)) == 8:
                break
            self.pbegin(0, 1)
            for hf in range(2):
                for c in range(8):
                    self.mm(self.psf(hf, 512, P), hf, mixT[:, c, 0:P], Wo[:, c, 512 * hf:512 * hf + 512], r=["mixT", "Wo"], p1=P, stop=(c == 7))
            z = xi[0:P, :]
            for hf in range(2):
                self.stt("dve", z[:, 512 * hf:512 * hf + 512], z[:, 512 * hf:512 * hf + 512], ALPHA, self.psf(hf, 512, P), ALU.mult, ALU.add,
                         r=[xk, ("ps", hf)], w=[xk])
            self.layernorm(P, z, xk, stat, "stat")
            self.dma("sp", D["x1s"][c0:c0 + P, :], z, r=[xk], w=[("x1s", i)], lane="x1o%d" % (i % 2))
            self.cp("act", xb[0:P, :], z, r=[xk], w=["xb"])
            self.to_XT(P, xb[0:P, :], "xb", i, 6)
        S.barrier()
        A.release(m2)


    def ffn(self, l, with_sample, final):
        nc, S, A, D = self.nc, self.S, self.A, self.D
        m = A.mark()
        hT = A.alloc("hT", [128, NF, 1056], BF16)
        Wi = [A.alloc("Wi", [128, 8, 256], BF16) for _ in range(3)]
        Wo = [A.alloc("Wo2", [128, 1024], BF16) for _ in range(3)]
        sg = [A.alloc("sg", [128, 512], BF16) for _ in range(2)]
        x1t = [A.alloc("x1t", [128, 1024], F32) for _ in range(2)]
        xb2 = A.alloc("xb2", [128, 1024], BF16)
        stat = A.alloc("stat2", [128, 16], F32)
        self.load_ln(D["ln_ffn_g"], D["ln_ffn_b"], l)
        wi_v = D["w_ffn_in"][l].rearrange("(c p) f -> p c f", p=128)
        wo_v = D["w_ffn_out"][l]
        nw = 0
        nwo = 0
        nx = 0
        for half in range(2):
            tiles = list(range(8 * half, 8 * half + 8))
            if half == 1 and with_sample:
                tiles.append(NT)
            col0 = 1024 * half
            pieces = [(0, 512), (512, 512)] + ([(1024, TS)] if (half == 1 and with_sample) else [])
            k = 0
            for f in range(NF):
                wi = Wi[nw % 3]
                wk = ("Wi", nw % 3)
                self.dmas("pool", [(wi[:, :, 0:128], wi_v[:, :, 128 * f:128 * f + 128]),
                                   (wi[:, :, 128:256], wi_v[:, :, FF + 128 * f:FF + 128 * f + 128])], r=[], w=[wk], lane="Wi%d" % (nw % 3))
                nw += 1
                for (po, pn) in pieces:
                    bg, bu = (0, 1) if k % 2 == 0 else (2, 3)
                    sgk = sg[k % 2]
                    k += 1
                    xk = [("XT", (col0 + po) // 128 + q) for q in range((pn + 127) // 128)]
                    for (bank, wc) in ((bg, 0), (bu, 128)):
                        self.pbegin(bank)
                        for d in range(8):
                            self.mm(self.psf(bank, pn), bank, wi[:, d, wc:wc + 128], self.XT[:, d, col0 + po:col0 + po + pn],
                                    r=xk + [wk], stop=(d == 7))
                    self.act(sgk[:, 0:pn], self.psf(bg, pn), AF.Silu, r=[("ps", bg)], w=[("sg", k % 2)])
                    self.tt("dve", hT[:, f, po:po + pn], self.psf(bu, pn), sgk[:, 0:pn], ALU.mult, r=[("ps", bu), ("sg", k % 2)], w=[("hT", f)])
            groups = [tiles[q:q + 3] for q in range(0, len(tiles), 3)]
            for grp in groups:
                self.pbegin(*range(2 * len(grp)))
                for f in range(NF):
                    wo = Wo[nwo % 3]
                    wok = ("Wo2", nwo % 3)
                    self.dma("pool", wo, wo_v[128 * f:128 * f + 128, :], r=[], w=[wok], lane="Wo2%d" % (nwo % 3))
                    nwo += 1
                    for kk, t in enumerate(grp):
                        P = 128 if t < NT else TS
                        lc = (t - 8 * half) * 128
                        for hf in range(2):
                            self.mm(self.psf(2 * kk + hf, 512, P), 2 * kk + hf, hT[:, f, lc:lc + P], wo[:, 512 * hf:512 * hf + 512],
                                    r=[("hT", f), wok], p1=P, stop=(f == NF - 1))
                for kk, t in enumerate(grp):
                    P = 128 if t < NT else TS
                    c0 = 128 * t
                    xt = x1t[nx % 2]
                    xk_ = ("x1t", nx % 2)
                    lane = "x1t%d" % (nx % 2)
                    nx += 1
                    z = xt[0:P, :]
                    self.dma("sp", z, D["x1s"][c0:c0 + P, :], r=[("x1s", t)], w=[xk_], lane=lane)
                    for hf in range(2):
                        self.stt("dve", z[:, 512 * hf:512 * hf + 512], z[:, 512 * hf:512 * hf + 512], ALPHA, self.psf(2 * kk + hf, 512, P),
                                 ALU.mult, ALU.add, r=[xk_, ("ps", 2 * kk + hf)], w=[xk_])
                    self.layernorm(P, z, xk_, stat, "stat2")
                    if final:
                        dst = D["yp"][c0:c0 + P, :] if t < NT else D["ys"]
                        self.dma("sp", dst, z, r=[xk_], w=[], lane=lane + "o")
                    else:
                        self.dma("sp", D["x2s"][c0:c0 + P, :], z, r=[xk_], w=[("x2s", t)], lane=lane + "o")
                        self.cp("act", xb2[0:P, :], z, r=[xk_], w=["xb2"])
                        self.to_XT(P, xb2[0:P, :], "xb2", t, 6)
        S.barrier()
        A.release(m)


    def k_tile(self, P, i, rope_idx, Wg, KTdst, dkey, kout=None):
        c0 = 128 * i
        self.pbegin(0)
        for d in range(8):
            self.mm(self.psf(0, 256, P), 0, self.XT[:, d, c0:c0 + P], Wg[:, d, 0:256], r=[("XT", i), "Wg"], p1=P, stop=(d == 7))
        rc, rs, rk = self.rope_load(self.D["rope1"], rope_idx, 32)
        Kr = self.Kr1[self.kn % 2]
        kk = ("Kr1", self.kn % 2)
        lane = "Kr1%d" % (self.kn % 2)
        self.kn += 1
        self.rope_tm(P, self.psf(0, 256, P), 4, 32, rc[0:P, :], rs[0:P, :], Kr[0:P, :], [("ps", 0), rk], kk, self.tmp1, "rtmp")
        if kout is not None:
            self.dma("sp", kout, Kr[0:P, :], r=[kk], w=[], lane=lane)
        self.cp("act", self.Kb1[0:P, :], Kr[0:P, :], r=[kk], w=["Kb1"])
        for hp in range(2):
            self.tr(self.psb(6, P, off=128 * hp), 6, self.Kb1[0:P, 128 * hp:128 * hp + 128], self.identb[0:P, 0:P], r=["Kb1", "CCB"])
        self.cp("dve", KTdst, self.psb(6, 256).rearrange("p (c t) -> p c t", c=2)[:, :, 0:P], r=[("ps", 6)], w=[dkey])

    def v_block(self, P, cols, xkeys, Wg, Vdst, dkey, vout=None):
        self.pbegin(1)
        for d in range(8):
            self.mm(self.psf(1, 256, P), 1, self.XT[:, d, cols], Wg[:, d, 256:512], r=xkeys + ["Wg"], p1=P, stop=(d == 7))
        self.cp("dve", Vdst, self.psf(1, 256, P), r=[("ps", 1)], w=[dkey])
        if vout is not None:
            Vf = self.Vf1[self.vn % 2]
            vk = ("Vf1", self.vn % 2)
            lane = "Vf1%d" % (self.vn % 2)
            self.vn += 1
            self.cp("dve", Vf[0:P, :], self.psf(1, 256, P), r=[("ps", 1)], w=[vk])
            self.dma("sp", vout, Vf[0:P, :], r=[vk], w=[], lane=lane)

    def l1_alloc_kv(self):
        A = self.A
        self.Kr1 = [A.alloc("Kr1", [128, 256], F32) for _ in range(2)]
        self.Vf1 = [A.alloc("Vf1", [128, 256], F32) for _ in range(2)]
        self.Kb1 = A.alloc("Kb1", [128, 256], BF16)
        self.Vnat = A.alloc("Vnat", [128, 256], BF16)
        self.tmp1 = A.alloc("tmp1", [128, 4, 384], F32)
        self.Wg = A.alloc("Wg", [128, 8, 512], BF16)
        self.kn = 0
        self.vn = 0

    def load_wg(self, g):
        wv = self.D["w_kv"].rearrange("(c p) f -> p c f", p=128)
        self.dmas("pool", [(self.Wg[:, :, 0:256], wv[:, :, 256 * g:256 * g + 256]),
                           (self.Wg[:, :, 256:512], wv[:, :, 768 + 256 * g:768 + 256 * g + 256])], r=[], w=["Wg"], lane="Wg")

    def halo_kv(self):
        S, A, D = self.S, self.A, self.D
        m = A.mark()
        self.l1_alloc_kv()
        KTh = A.alloc("KTh", [128, 2, 2048], BF16)
        Vh = A.alloc("Vh", [128, 16, 256], BF16)
        for g in range(3):
            dil, W = DILS[g], WINS[g]
            self.load_wg(g)
            nt = W // 128
            for q in range(nt):
                i = NT - nt + q
                self.k_tile(128, i, i, self.Wg, KTh[:, :, 128 * q:128 * q + 128], "KTh")
            self.dma("sp", D["hK%d" % g], KTh[:, :, 0:W], r=["KTh"], w=["hK"], lane="hK")
            cl = NT // dil - 1
            for r in range(dil):
                st = 128 * dil * cl + r
                cols = slice(st, st + dil * 127 + 1, dil)
                xkeys = [("XT", dil * cl + q) for q in range(dil)]
                self.v_block(128, cols, xkeys, self.Wg, Vh[:, r, :], "Vh")
            self.dma("sp", D["hV%d" % g], Vh[:, 0:dil, :], r=["Vh"], w=["hV"], lane="hV")
        S.barrier()
        A.release(m)

    def b1_tile(self, P, i, lhs, lkey, Wo, xi, xk, lane, stat):
        D = self.D
        c0 = 128 * i
        self.pbegin(0, 1)
        for hf in range(2):
            for c in range(8):
                self.mm(self.psf(hf, 512, P), hf, lhs(c), Wo[:, c, 512 * hf:512 * hf + 512], r=[lkey, "Wo"], p1=P, stop=(c == 7))
        z = xi[0:P, :]
        for hf in range(2):
            self.stt("dve", z[:, 512 * hf:512 * hf + 512], z[:, 512 * hf:512 * hf + 512], ALPHA, self.psf(hf, 512, P), ALU.mult, ALU.add,
                     r=[xk, ("ps", hf)], w=[xk])
        self.layernorm(P, z, xk, stat, "stat")
        self.dma("sp", D["x1s"][c0:c0 + P, :], z, r=[xk], w=[("x1s", i)], lane=lane)
        self.cp("act", self.xb1[0:P, :], z, r=[xk], w=["xb1"])
        self.to_XT(P, self.xb1[0:P, :], "xb1", i, 6)

    def layer1(self):
        nc, S, A, D = self.nc, self.S, self.A, self.D
        C1 = self.C1 = A.alloc("C1", [128, C1_L.n], F32)
        C1B = A.alloc("C1B", [128, C1_L.n], BF16)
        self.dma("sp", C1, D["c1"], r=[], w=["C1"], lane="C1")
        self.cp("dve", C1B, C1, r=["C1"], w=["C1B"])
        self.mem_kv(1)
        mixT = A.alloc("mixT1", [128, 8, TT], BF16)
        Ltot = A.alloc("Ltot", [128, 2, TT], F32)
        rLs = A.alloc("rLs", [128, 256], F32)
        PTm = A.alloc("PTm1", [128, 1024], BF16)
        m = A.mark()
        Wb = A.alloc("Wb", [128, 8, 1024], BF16)
        wb_v = D["w_in_b"].rearrange("(c p) f -> p c f", p=128)
        self.dmas("pool", [(Wb[:, c, :], wb_v[:, c, :]) for c in range(8)], r=[], w=["Wb"], lane="Wb")
        Qr = A.alloc("Qr1", [128, 768], F32)
        Qb = A.alloc("Qb1", [128, 768], BF16)
        tmpq = A.alloc("tmpq", [128, 4, 384], F32)
        for i in range(NT + 1):
            P = 128 if i < NT else TS
            c0 = 128 * i
            self.proj_tm(P, i, Wb, "Wb", 0, 768, 0)
            rc, rs, rk = self.rope_load(D["rope1"], (16 + i) if i < NT else 32, 32)
            self.rope_tm(P, self.PS[0:P, 0:768], 12, 32, rc[0:P, :], rs[0:P, :], Qr[0:P, :], [("ps", 0), ("ps", 1), rk], "Qr1", tmpq, "rtmp")
            self.cp("pool", Qb[0:P, :], Qr[0:P, :], r=["Qr1"], w=["Qb1"])
            for h in range(6):
                self.tr(self.psb(6, P, off=128 * h), 6, Qb[0:P, 128 * h:128 * h + 128], self.identb[0:P, 0:P], r=["Qb1", "CCB"])
            self.cp("act", mixT[:, 0:6, c0:c0 + P], self.psb(6, 768).rearrange("p (h t) -> p h t", h=6)[:, :, 0:P], r=[("ps", 6)], w=[("mixT", i)])
            self.pbegin(7)
            for c in range(2):
                for d in range(8):
                    self.mm(self.psf(7, P, off=128 * c), 7, Wb[:, d, 768 + 128 * c:768 + 128 * c + 128], self.XT[:, d, c0:c0 + P],
                            r=[("XT", i), "Wb"], stop=(d == 7))
            self.cp("dve", mixT[:, 6:8, c0:c0 + P], self.psf(7, 256).rearrange("p (c t) -> p c t", c=2)[:, :, 0:P], r=[("ps", 7)], w=[("mixm", i)])
        if self.stage == -4.1:
            return
        mks = A.alloc("mks1", [128, 2, 256], BF16)
        mvs = A.alloc("mvs1", [128, 2, 256], BF16)
        ctile = A.alloc("ctile1", [128, 512], F32)
        ctb = A.alloc("ctb1", [128, 512], BF16)
        for i in range(NT):
            c0 = 128 * i
            self.mem_attn(128, mixT[:, 6:8, c0:c0 + 128], ("mixm", i), self.memKT, self.memV, ["memKT", "memV"], mixT[:, 6:8, c0:c0 + 128],
                          ("mixm", i), 128, PTm, rLs)
        for bl in range(4):
            for t in range(2):
                self.dma("sp", ctile, D["cmkv"][1, bl, 128 * t:128 * t + 128, :], r=[], w=["ctile"], lane="ctile")
                self.cp("act", ctb, ctile, r=["ctile"], w=["ctb"])
                for c in range(2):
                    self.tr(self.psb(5, 128, off=128 * c), 5, ctb[:, 128 * c:128 * c + 128], self.identb, r=["ctb", "CCB"])
                self.cp("dve", mks[:, :, 128 * t:128 * t + 128], self.psb(5, 256).rearrange("p (c t) -> p c t", c=2), r=[("ps", 5)], w=["mks"])
                self.cp("pool", mvs[:, t, :], ctb[:, 256:512], r=["ctb"], w=["mvs"])
            cs = T + 8 * bl
            self.mem_attn(128, mixT[:, 6:8, cs:cs + 8], ("mixm", NT), mks, mvs, ["mks", "mvs"], mixT[:, 6:8, cs:cs + 8], ("mixm", NT), 8, PTm, rLs)
        S.barrier()
        A.release(m)
        if self.stage == -4.2:
            return
        m = A.mark()
        self.l1_alloc_kv()
        PTa = [A.alloc("PTa", [128, 544], BF16) for _ in range(4)]
        ctile = A.alloc("ctile3", [128, 512], F32)
        ctb = A.alloc("ctb3", [128, 512], BF16)
        KTn = A.alloc("KTn", [128, 2, 32], BF16)
        Vns = A.alloc("Vns", [128, 256], BF16)
        S.add("pool", lambda e: e.memset(Vns, 0.0), r=[], w=["Vns"])
        for pt_ in PTa:
            S.add("pool", lambda e, pt_=pt_: e.memset(pt_, 0.0), r=[], w=[("PTa", PTa.index(pt_))])
        sb_i = 0
        ob_i = 0
        pa_i = 0
        first = True
        for g in (2, 1, 0):
            dil, W = DILS[g], WINS[g]
            mg = A.mark()
            KT = A.alloc("KTg", [128, 2, W + T], BF16)
            Vb = A.alloc("Vbg", [128, dil + NT, 256], BF16)
            KTs = A.alloc("KTs", [128, 2, W], BF16)
            Vs = A.alloc("Vs", [128, W // 128, 256], BF16)
            self.load_wg(g)
            self.dma("sp", KT[:, :, 0:W], D["hK%d" % g], r=[], w=["KTh"], lane="hKl")
            self.dma("sp", Vb[:, 0:dil, :], D["hV%d" % g], r=[], w=["Vbh"], lane="hVl")
            nlast = W // 128
            for i in range(NT):
                kout = None
                if i >= NT - nlast:
                    q = i - (NT - nlast)
                    kout = D["wp%d" % g][128 * q:128 * q + 128, 0:256]
                self.k_tile(128, i, 16 + i, self.Wg, KT[:, :, W + 128 * i:W + 128 * i + 128], ("KT", i), kout)
            if self.stage == -4.21:
                return
            nc_ = NT // dil
            for c in range(nc_):
                for r in range(dil):
                    st = 128 * dil * c + r
                    cols = slice(st, st + dil * 127 + 1, dil)
                    xkeys = [("XT", dil * c + q) for q in range(dil)]
                    self.v_block(128, cols, xkeys, self.Wg, Vb[:, dil + c * dil + r, :], ("Vb", dil + c * dil + r), None)
            for q in range(nlast):
                i = NT - nlast + q
                self.v_block(128, slice(128 * i, 128 * i + 128), [("XT", i)], self.Wg, self.Vnat, "Vnat", D["wp%d" % g][128 * q:128 * q + 128, 256:512])
            if self.stage == -4.22:
                return
            self.k_tile(TS, NT, 32, self.Wg, KTn, "KTn", None)
            Krs = self.Kr1[(self.kn - 1) % 2]
            krk = ("Kr1", (self.kn - 1) % 2)
            self.v_block(TS, slice(T, T + TS), [("XT", NT)], self.Wg, Vns[0:TS, :], "Vns", None)
            Vfs = self.Vf1[self.vn % 2]
            vfk = ("Vf1", self.vn % 2)
            self.vn += 1
            self.cp("dve", Vfs[0:TS, :], self.psf(1, 256, TS), r=[("ps", 1)], w=[vfk])
            if self.stage == -4.23:
                return
            for bl in range(4):
                self.dma("sp", D["ws%d" % g][bl, W - 8:W, 0:256], Krs[8 * bl:8 * bl + 8, :], r=[krk], w=[], lane="wsk")
                self.dma("sp", D["ws%d" % g][bl, W - 8:W, 256:512], Vfs[8 * bl:8 * bl + 8, :], r=[vfk], w=[], lane="wsv")
                self.dma("pool", D["ws%d" % g][bl, 0:W - 8, :], D["cw%d" % g][bl, 8:W, :], r=[], w=[], lane="wsc")
            if self.stage == -4.3:
                return
            mb4 = self.cst(C1B, C1_L, "mb4")
            mb4h = self.cst(C1B, C1_L, "mb4h")
            for c in range(nc_):
                for r in range(dil):
                    st = 128 * dil * c + r
                    qcols = slice(st, st + dil * 127 + 1, dil)
                    kcur = slice(W + st, W + st + dil * 127 + 1, dil)
                    kprev = slice(st, st + dil * 127 + 1, dil)
                    icur = dil + c * dil + r
                    iprev = icur - dil
                    kkeys = [("KT", dil * c + q) for q in range(dil)] + ([("KT", dil * (c - 1) + q) for q in range(dil)] if c > 0 else ["KTh"])
                    vkeys = [("Vb", icur), (("Vb", iprev) if c > 0 else "Vbh")]
                    qks = [("mq", g, 0, r, c), ("mq", g, 1, r, c)]
                    bo = 4 + (ob_i % 2)
                    ob_i += 1
                    bx = 2 * (sb_i % 2)
                    sb_i += 1
                    pts = []
                    for par in range(2):
                        bank = bx + par
                        ps_ = slice(64 * par, 64 * par + 64)
                        self.pbegin(bank)
                        for hp in range(2):
                            for half, kc in enumerate((kprev, kcur)):
                                self.mm(self.psf(bank, 128, off=(2 * hp + half) * 128), bank, KT[ps_, hp, kc], mixT[ps_, 2 * g + hp, qcols],
                                        r=kkeys + qks, stop=(hp == 1 and half == 1))
                        pt = PTa[pa_i % 4]
                        pk = ("PTa", pa_i % 4)
                        pa_i += 1
                        self.act(pt[:, 0:512], self.psf(bank, 512), AF.Exp, r=[("ps", bank)], w=[pk], scale=0.125)
                        self.tt("pool", pt[:, 0:512], pt[:, 0:512], (mb4h if c == 0 else mb4), ALU.mult, r=[pk, "C1B"], w=[pk])
                        pts.append((pt, pk))
                    self.pbegin(bo)
                    for hp in range(2):
                        for par in range(2):
                            h = 2 * hp + par
                            pt, pk = pts[par]
                            for half, idx in enumerate((iprev, icur)):
                                self.mm(self.psf(bo, 128, off=128 * hp)[64 * par:64 * par + 64, :], bo, Vb[:, idx, 64 * h:64 * h + 64],
                                        pt[:, (2 * hp + half) * 128:(2 * hp + half) * 128 + 128], r=[pk] + vkeys, p0=64 * par, p1=64 * par + 64)
                            for half in range(2):
                                self.mm(self.psf(bo, 128, off=128 * (2 + hp))[64 * par:64 * par + 64, :], bo, self.onesb,
                                        pt[:, (2 * hp + half) * 128:(2 * hp + half) * 128 + 128], r=[pk, "CCB"], p0=64 * par, p1=64 * par + 64,
                                        stop=(hp == 1 and par == 1 and half == 1))
                    self.cp("dve", mixT[:, 2 * g:2 * g + 2, qcols], self.psf(bo, 256).rearrange("p (c q) -> p c q", c=2), r=[("ps", bo)], w=qks)
                    lsrc = self.psf(bo, 256, off=256).rearrange("p (c q) -> p c q", c=2)
                    if first:
                        self.cp("dve", Ltot[:, :, qcols], lsrc, r=[("ps", bo)], w=[("L", r, c)])
                    else:
                        self.tt("dve", Ltot[:, :, qcols], Ltot[:, :, qcols], lsrc, ALU.add, r=[("ps", bo), ("L", r, c)], w=[("L", r, c)])
            if self.stage == -4.4:
                return
            nm = W // 128
            sm = self.cst(C1B, C1_L, "sm%d" % g)
            smn = self.cst(C1B, C1_L, "smn%d" % g).rearrange("p (b x) -> p b x", b=4)
            for bl in range(4):
                for mt in range(nm):
                    self.dma("sp", ctile, D["cw%d" % g][bl, 128 * mt:128 * mt + 128, :], r=[], w=["ctile"], lane="ctile")
                    self.cp("act", ctb, ctile, r=["ctile"], w=["ctb"])
                    for cc_ in range(2):
                        self.tr(self.psb(6, 128, off=128 * cc_), 6, ctb[:, 128 * cc_:128 * cc_ + 128], self.identb, r=["ctb", "CCB"])
                    self.cp("dve", KTs[:, :, 128 * mt:128 * mt + 128], self.psb(6, 256).rearrange("p (c t) -> p c t", c=2), r=[("ps", 6)], w=["KTs"])
                    self.cp("pool", Vs[:, mt, :], ctb[:, 256:512], r=["ctb"], w=["Vs"])
                cs = T + 8 * bl
                pts = []
                for par in range(2):
                    ps_ = slice(64 * par, 64 * par + 64)
                    bS, bN = 2 + par, par
                    self.pbegin(bS, bN)
                    for mt in range(nm):
                        for hp in range(2):
                            self.mm(self.psf(bS, 8, off=16 * mt + 8 * hp), bS, KTs[ps_, hp, 128 * mt:128 * mt + 128], mixT[ps_, 2 * g + hp, cs:cs + 8],
                                    r=["KTs", ("mqs", g)], stop=(mt == nm - 1 and hp == 1))
                    for hp in range(2):
                        self.mm(self.psf(bN, 8, TS, off=8 * hp), bN, KTn[ps_, hp, :], mixT[ps_, 2 * g + hp, cs:cs + 8], r=["KTn", ("mqs", g)], p1=TS, stop=(hp == 1))
                    pt = PTa[pa_i % 4]
                    pk = ("PTa", pa_i % 4)
                    pa_i += 1
                    self.act(pt[:, 0:16 * nm], self.psf(bS, 16 * nm), AF.Exp, r=[("ps", bS)], w=[pk], scale=0.125)
                    self.act(pt[0:TS, 512:528], self.psf(bN, 16, TS), AF.Exp, r=[("ps", bN)], w=[pk], scale=0.125)
                    self.tt("pool", pt[:, 0:16 * nm], pt[:, 0:16 * nm], sm, ALU.mult, r=[pk, "C1B"], w=[pk])
                    self.tt("pool", pt[0:TS, 512:528], pt[0:TS, 512:528], smn[0:TS, bl, :], ALU.mult, r=[pk, "C1B"], w=[pk])
                    pts.append((pt, pk))
                bo = 4 + (ob_i % 2)
                ob_i += 1
                self.pbegin(bo)
                for h in range(4):
                    hp, par = divmod(h, 2)
                    pt, pk = pts[par]
                    for (kind, off) in (("o", 8 * hp), ("l", 16 + 8 * hp)):
                        o = self.psf(bo, 8, off=off)[64 * par:64 * par + 64, :]
                        for mt in range(nm):
                            lhs = Vs[:, mt, 64 * h:64 * h + 64] if kind == "o" else self.onesb
                            self.mm(o, bo, lhs, pt[:, 16 * mt + 8 * hp:16 * mt + 8 * hp + 8], r=[pk, "Vs", "CCB"], p0=64 * par, p1=64 * par + 64)
                        lhs = Vns[:, 64 * h:64 * h + 64] if kind == "o" else self.ones32b
                        self.mm(o, bo, lhs, pt[:, 512 + 8 * hp:512 + 8 * hp + 8], r=[pk, "Vns", "CCB"], p0=64 * par, p1=64 * par + 64, stop=True)
                self.cp("dve", mixT[:, 2 * g:2 * g + 2, cs:cs + 8], self.psf(bo, 16).rearrange("p (c q) -> p c q", c=2), r=[("ps", bo)], w=[("mqs", g)])
                lsrc = self.psf(bo, 16, off=16).rearrange("p (c q) -> p c q", c=2)
                if first:
                    self.cp("dve", Ltot[:, :, cs:cs + 8], lsrc, r=[("ps", bo)], w=[("Ls", bl)])
                else:
                    self.tt("dve", Ltot[:, :, cs:cs + 8], Ltot[:, :, cs:cs + 8], lsrc, ALU.add, r=[("ps", bo), ("Ls", bl)], w=[("Ls", bl)])
            if self.stage == -4.5:
                return
            first = False
            S.barrier()
            A.release(mg)
        A.release(m)
        S.add("dve", lambda e: e.reciprocal(Ltot, Ltot), r=[], w=["Ltot"])
        for ch in range(6):
            self.tt("dve", mixT[:, ch, :], mixT[:, ch, :], Ltot[:, ch % 2, :], ALU.mult, r=["Ltot"], w=[("mixc", ch)])
        if self.debug:
            pass
        S.barrier()
        if self.stage == -4.6:
            return
        m = A.mark()
        Wo = A.alloc("Wo1", [128, 8, 1024], BF16)
        wo_v = D["w_out"][1].rearrange("(c p) f -> p c f", p=128)
        self.dmas("pool", [(Wo[:, c, :], wo_v[:, c, :]) for c in range(8)], r=[], w=["Wo"], lane="Wo")
        self.load_ln(D["ln_mix_g"], D["ln_mix_b"], 1)
        xin = [A.alloc("xin1", [128, 1024], F32) for _ in range(2)]
        self.xb1 = A.alloc("xb1", [128, 1024], BF16)
        stat = A.alloc("stat1", [128, 16], F32)
        for i in range(NT + 1):
            P = 128 if i < NT else TS
            c0 = 128 * i
            xi = xin[i % 2]
            xk = ("xin", i % 2)
            self.dma("sp", xi[0:P, :], D["x2s"][c0:c0 + P, :], r=[], w=[xk], lane="xin%d" % (i % 2))
            self.b1_tile(P, i, (lambda c, c0=c0, P=P: mixT[:, c, c0:c0 + P]), "mixall", Wo, xi, xk, "x1o%d" % (i % 2), stat)
        S.barrier()
        A.release(m)
        if self.stage == -4.7:
            return
        self.ffn(1, True, True)


_CACHE = {}


def _get_prog(stage, debug):
    key = (stage, debug)
    if key not in _CACHE:
        _CACHE[key] = Prog(stage, debug)
    return _CACHE[key]


def _in_maps(inp):
    f = lambda a: np.ascontiguousarray(a, dtype=np.float32)
    maps = []
    for c in range(NCORES):
        b, j = divmod(c, 4)
        cc, c0, c1, rope0, rope1 = build_consts(c)
        xb_ = np.asarray(inp["x_prompt"][b], dtype=np.float32)
        xpad = np.concatenate([np.zeros((3 * T, 1024), np.float32), xb_], 0)
        m = {
            "xp": f(inp["x_prompt"][b, T * j:T * (j + 1)]),
            "xpv": f(xpad[3 * T + T * (j - 1):3 * T + T * j]),
            "xpp": f(xpad[3 * T + T * (j - 3):3 * T + T * (j - 1)]),
            "rope0": rope0, "rope1": rope1,
            "xs": f(inp["x_sample"][4 * c:4 * c + 4].reshape(TS, 1024)),
            "mem": f(inp["mem_prompt"][b]),
            "cmkv": f(inp["cache_mem_kv"][:, 4 * c:4 * c + 4].reshape(2, 4, 256, 512)),
            "sret": f(inp["state_ret"][0, 4 * c:4 * c + 4]),
            "cw0": f(inp["cache_win_kv_g1"][4 * c:4 * c + 4].reshape(4, 128, 512)),
            "cw1": f(inp["cache_win_kv_g2"][4 * c:4 * c + 4].reshape(4, 512, 512)),
            "cw2": f(inp["cache_win_kv_g3"][4 * c:4 * c + 4].reshape(4, 2048, 512)),
            "w_in_a": f(inp["w_in_a"][0]), "w_in_b": f(inp["w_in_b"][0]), "w_out": f(inp["w_out"]),
            "w_kv": f(inp["w_kv_shared"]), "w_mem": f(inp["w_mem_kv"]),
            "ln_mix_g": f(inp["ln_mix_g"]), "ln_mix_b": f(inp["ln_mix_b"]),
            "ln_ffn_g": f(inp["ln_ffn_g"]), "ln_ffn_b": f(inp["ln_ffn_b"]),
            "w_ffn_in": f(inp["w_ffn_in"]), "w_ffn_out": f(inp["w_ffn_out"]),
            "cc": cc, "c0": c0, "c1": c1,
        }
        maps.append(m)
    return maps


def _run(inp, stage=99, debug=False):
    prog = _get_prog(stage, debug)
    maps = [{k: v for k, v in m.items() if k in prog.D} for m in _in_maps(inp)]
    res = run_bass_kernel_spmd(prog.nc, maps, core_ids=list(range(NCORES)))
    return res.results


def kernel(**inp):
    R = _run(inp)
    yp = np.stack([np.concatenate([R[4 * b + j]["yp"] for j in range(4)], 0) for b in range(2)], 0)
    ys = np.concatenate([R[c]["ys"].reshape(4, 8, 1024) for c in range(8)], 0)
    srp = np.stack([R[4 * b + 3]["srp"] for b in range(2)], 0)[None]
    srs = np.concatenate([R[c]["srs"] for c in range(8)], 0)[None]
    mkv = np.stack([np.stack([R[4 * b]["mkv"][l].reshape(256, 2, 4, 64) for b in range(2)], 0) for l in range(2)], 0)
    outs = [yp, ys, srp, srs, mkv]
    for g, W in enumerate(WINS):
        outs.append(np.stack([R[4 * b + 3]["wp%d" % g].reshape(W, 2, 4, 64) for b in range(2)], 0))
    for g, W in enumerate(WINS):
        outs.append(np.concatenate([R[c]["ws%d" % g].reshape(4, W, 2, 4, 64) for c in range(8)], 0))
    return tuple(np.ascontiguousarray(o, dtype=np.float32) for o in outs)
```

```python
import numpy as np
from contextlib import ExitStack
import concourse.bass as bass
import concourse.mybir as mybir
from concourse.bass_utils import run_bass_kernel_spmd

F32 = mybir.dt.float32
BF16 = mybir.dt.bfloat16
AF = mybir.ActivationFunctionType
ALU = mybir.AluOpType
AX = mybir.AxisListType

NEG = -30000.0
ALPHA = 4.0 ** 0.25
LN_EPS = 1e-5
NCORES = 8
T = 2048
NT = 16
TS = 32
TT = T + TS
FF = 2816
NF = 22
DILS = (1, 4, 16)
WINS = (128, 512, 2048)


class Op:
    __slots__ = ("eng", "fn", "sdeps", "is_dma", "lane", "inc", "marked", "count", "idx")


class Sched:
    ENGS = ("pe", "act", "dve", "pool", "sp")

    def __init__(self):
        self.ops = {e: [] for e in self.ENGS}
        self.all = []
        self.lw = {}
        self.rd = {}
        self.floor = None
        self.last = {}
        self.pending_dma = []
        self.lanes = []

    def add(self, eng, fn, r=(), w=(), dma=False, lane=None, inc=16):
        op = Op()
        op.eng, op.fn, op.is_dma, op.lane, op.inc = eng, fn, dma, lane, inc
        op.marked = dma
        op.count = 0
        op.idx = len(self.all)
        if dma and lane not in self.lanes:
            self.lanes.append(lane)
        deps = {}

        def dep(a, kind):
            if (not a.is_dma) and a.eng == eng and (eng == "pe" or kind == "war"):
                return
            deps[a.idx] = a

        for k in r:
            a = self.lw.get(k)
            if a is not None:
                dep(a, "raw")
        for k in w:
            a = self.lw.get(k)
            if a is not None:
                dep(a, "waw")
            for a in self.rd.get(k, ()):
                dep(a, "war")
        if self.floor is not None and not (eng == "pool" and not dma and False):
            deps[self.floor.idx] = self.floor
        for k in w:
            self.lw[k] = op
            self.rd[k] = []
        for k in r:
            if k in w:
                continue
            lst = self.rd.setdefault(k, [])
            if not dma:
                lst[:] = [x for x in lst if x.is_dma or x.eng != eng]
            lst.append(op)
        op.sdeps = list(deps.values())
        for a in op.sdeps:
            a.marked = True
        self.all.append(op)
        self.ops[eng].append(op)
        if dma:
            self.pending_dma.append(op)
        else:
            self.last[eng] = op
        return op

    def barrier(self):
        deps = list(self.last.values()) + list(self.pending_dma)
        op = Op()
        op.eng, op.fn, op.is_dma, op.lane, op.inc = "pool", (lambda e: e.nop()), False, None, 1
        op.marked = False
        op.count = 0
        op.idx = len(self.all)
        op.sdeps = deps
        for a in deps:
            a.marked = True
        self.all.append(op)
        self.ops["pool"].append(op)
        self.last = {"pool": op}
        self.pending_dma = []
        self.floor = op
        self.lw = {}
        self.rd = {}

    def emit(self, nc):
        cnt = {}
        for op in self.all:
            key = op.lane if op.is_dma else op.eng
            if op.marked:
                cnt[key] = cnt.get(key, 0) + (op.inc if op.is_dma else 1)
            op.count = cnt.get(key, 0)
        keys = ["pe", "act", "dve", "pool"] + self.lanes
        with ExitStack() as es:
            sems = {}
            for i, k in enumerate(keys):
                sems[k] = es.enter_context(nc.semaphore("s%d" % i))
            block = es.enter_context(nc.Block())

            def mk(engname):
                def body(e):
                    waited = {}
                    for op in self.ops[engname]:
                        need = {}
                        for a in op.sdeps:
                            key = a.lane if a.is_dma else a.eng
                            if a.count > need.get(key, 0):
                                need[key] = a.count
                        for key, val in need.items():
                            if waited.get(key, 0) < val:
                                e.wait_ge(sems[key], val)
                                waited[key] = val
                        ins = op.fn(e)
                        if op.marked:
                            key = op.lane if op.is_dma else op.eng
                            if isinstance(ins, list):
                                for x in ins:
                                    x.then_inc(sems[key], op.inc // len(ins))
                            else:
                                ins.then_inc(sems[key], op.inc if op.is_dma else 1)
                return body

            block.sync(mk("sp"))
            block.scalar(mk("act"))
            block.vector(mk("dve"))
            block.gpsimd(mk("pool"))
            block.tensor(mk("pe"))


class Arena:
    def __init__(self, nc, base=16512, limit=229376):
        self.nc, self.top, self.limit, self.n = nc, base, limit, 0
        self.peak = base

    def alloc(self, name, shape, dtype):
        esz = 4 if dtype == F32 else 2
        nb = esz
        for s in shape[1:]:
            nb *= s
        nb = (nb + 63) // 64 * 64
        off = self.top
        self.top += nb
        self.peak = max(self.peak, self.top)
        assert self.top <= self.limit, "SBUF overflow at %s: %d" % (name, self.top)
        self.n += 1
        return self.nc.alloc_sbuf_tensor_at("%s_%d" % (name, self.n), list(shape), dtype, offset=off).ap()

    def mark(self):
        return self.top

    def release(self, m):
        self.top = m


def _rope_tab(pos, d):
    inv = (np.float32(10000.0) ** (-(np.arange(0, d, 2, dtype=np.float32)) / np.float32(d))).astype(np.float32)
    ang = (pos.astype(np.float32)[:, None] * inv[None, :]).astype(np.float32)
    return np.cos(ang).astype(np.float32), np.sin(ang).astype(np.float32)


class Cols:
    def __init__(self):
        self.n = 0
        self.m = {}

    def add(self, name, w):
        self.m[name] = (self.n, w)
        self.n += w


def _layout():
    c0 = Cols()
    c0.add("dq", 12); c0.add("dk", 12); c0.add("dec1", 192)
    c0.add("G128", 768); c0.add("G8", 768)
    c0.add("causal", 128); c0.add("mask_s", 32); c0.add("blk", 128); c0.add("rowm", 4)
    c0.add("nhalf", 1)
    c1 = Cols()
    c1.add("mb4", 512); c1.add("mb4h", 512)
    for g in range(3):
        c1.add("sm%d" % g, (WINS[g] // 128) * 16); c1.add("smn%d" % g, 64)
    cc = Cols()
    cc.add("ident", 128); cc.add("ones", 64); cc.add("ones32", 64)
    return cc, c0, c1


CC_L, C0_L, C1_L = _layout()


def build_consts(c):
    b, j = divmod(c, 4)
    h = np.arange(6, dtype=np.float64)
    lg = np.log1p(-np.exp2(-5.0 - h))
    p = np.arange(128)
    cc = np.zeros((128, CC_L.n), np.float32)
    c0 = np.zeros((128, C0_L.n), np.float32)
    c1 = np.zeros((128, C1_L.n), np.float32)

    def put(arr, lay, name, val):
        o, w = lay.m[name]
        arr[:, o:o + w] = np.asarray(val, np.float32).reshape(128, w)

    put(cc, CC_L, "ident", np.eye(128))
    put(cc, CC_L, "ones", np.ones((128, 64)))
    o32 = np.zeros((128, 64)); o32[:32] = 1.0
    put(cc, CC_L, "ones32", o32)
    pos0 = np.zeros((65, 128), np.float32)
    for i in range(64):
        pos0[i] = np.maximum(2048 * (j - 3) + 128 * i + p, 0)
    pos0[64, :32] = 8192 + (p[:32] % 8)
    cs, sn = _rope_tab(pos0.reshape(-1), 128)
    rope0 = np.concatenate([cs.reshape(65, 128, 64), sn.reshape(65, 128, 64)], 2).astype(np.float32)
    pos1 = np.zeros((33, 128), np.float32)
    for i in range(32):
        pos1[i] = np.maximum(2048 * (j - 1) + 128 * i + p, 0)
    pos1[32, :32] = 8192 + (p[:32] % 8)
    cs, sn = _rope_tab(pos1.reshape(-1), 64)
    rope1 = np.concatenate([cs.reshape(33, 128, 32), sn.reshape(33, 128, 32)], 2).astype(np.float32)
    sc = 128.0 ** -0.5
    dq = np.zeros((128, 2, 6)); dk = np.zeros((128, 2, 6))
    dq[:, 0] = np.exp((p[:, None] + 1.0) * lg[None]); dk[:, 0] = np.exp(-(p[:, None] + 1.0) * lg[None]) * sc
    dq[:, 1] = np.exp(((p[:, None] % 8) + 1.0) * lg[None]); dk[:, 1] = np.exp(-((p[:, None] % 8) + 1.0) * lg[None]) * sc
    put(c0, C0_L, "dq", dq); put(c0, C0_L, "dk", dk)
    dec1 = np.zeros((128, 32, 6))
    for i in range(32):
        dec1[:, i] = np.exp((4095.0 - (128 * i + p[:, None])) * lg[None]) * sc
    put(c0, C0_L, "dec1", dec1)
    put(c0, C0_L, "G128", np.broadcast_to(np.repeat(np.exp(128.0 * lg), 128)[None], (128, 768)))
    put(c0, C0_L, "G8", np.broadcast_to(np.repeat(np.exp(8.0 * lg), 128)[None], (128, 768)))
    put(c0, C0_L, "causal", (p[:, None] <= p[None, :]).astype(np.float32))
    ms = np.zeros((128, 32))
    for k in range(32):
        for q in range(32):
            ms[k, q] = 1.0 if (k // 8 == q // 8 and k % 8 <= q % 8) else 0.0
    put(c0, C0_L, "mask_s", ms)
    blk = np.zeros((128, 4, 32))
    for bl in range(4):
        blk[:, bl, 8 * bl:8 * bl + 8] = 1.0
    put(c0, C0_L, "blk", blk)
    rowm = np.zeros((128, 4))
    for bl in range(4):
        rowm[8 * bl:8 * bl + 8, bl] = 1.0
    put(c0, C0_L, "rowm", rowm)
    put(c0, C0_L, "nhalf", np.full((128, 1), -0.5))
    mprev = np.where(p[:, None] >= p[None, :], 1.0, 0.0)
    mcur = np.where(p[:, None] <= p[None, :], 1.0, 0.0)
    put(c1, C1_L, "mb4", np.concatenate([mprev, mcur, mprev, mcur], 1))
    mprevh = mprev * (0.0 if j == 0 else 1.0)
    put(c1, C1_L, "mb4h", np.concatenate([mprevh, mcur, mprevh, mcur], 1))
    for g, dil in enumerate(DILS):
        t = np.arange(8)
        nm = WINS[g] // 128
        mf = (((p[:, None] - t[None]) % dil == 0) & (p[:, None] >= t[None])).astype(np.float32)
        mr = (((p[:, None] - t[None]) % dil == 0)).astype(np.float32)
        sm = np.zeros((128, nm, 2, 8), np.float32)
        for mt in range(nm):
            sm[:, mt, :, :] = (mf if mt == 0 else mr)[:, None, :]
        put(c1, C1_L, "sm%d" % g, sm)
        mn = np.zeros((128, 4, 2, 8), np.float32)
        for r_ in range(32):
            blr, tr = divmod(r_, 8)
            for tq in range(8):
                if tr <= tq and (tq - tr) % dil == 0:
                    mn[r_, blr, :, tq] = 1.0
        put(c1, C1_L, "smn%d" % g, mn)
    return cc, c0, c1, rope0, rope1


class Prog:
    def __init__(self, stage=99, debug=False):
        self.stage = stage
        self.debug = debug
        nc = self.nc = bass.Bass("TRN2", target_bir_lowering=False)
        self.S = Sched()
        self.A = Arena(nc)
        specs = self.specs = {}

        def din(name, shape):
            specs[name] = (list(shape), F32, "ExternalInput")

        def dout(name, shape):
            specs[name] = (list(shape), F32, "ExternalOutput")

        def dscr(name, shape, dt=F32):
            specs[name] = (list(shape), dt, "Internal")

        class LazyD(dict):
            def __missing__(d, name):
                shape, dt, kind = specs[name]
                if kind == "Internal":
                    ap = nc.dram_tensor(name, shape, dt).ap()
                else:
                    ap = nc.dram_tensor(name, shape, dt, kind=kind).ap()
                d[name] = ap
                return ap

        D = self.D = LazyD()
        din("xp", [T, 1024]); din("xpv", [T, 1024]); din("xpp", [2 * T, 1024]); din("xs", [TS, 1024]); din("mem", [256, 1024])
        din("rope0", [65, 128, 128]); din("rope1", [33, 128, 64])
        din("cmkv", [2, 4, 256, 512]); din("sret", [4, 6, 128, 128])
        din("cw0", [4, 128, 512]); din("cw1", [4, 512, 512]); din("cw2", [4, 2048, 512])
        din("w_in_a", [1024, 3328]); din("w_in_b", [1024, 1024]); din("w_out", [2, 1024, 1024])
        din("w_kv", [1024, 1536]); din("w_mem", [2, 1024, 512])
        din("ln_mix_g", [2, 1024]); din("ln_mix_b", [2, 1024]); din("ln_ffn_g", [2, 1024]); din("ln_ffn_b", [2, 1024])
        din("w_ffn_in", [2, 1024, 2 * FF]); din("w_ffn_out", [2, FF, 1024])
        din("cc", [128, CC_L.n]); din("c0", [128, C0_L.n]); din("c1", [128, C1_L.n])
        dout("yp", [T, 1024]); dout("ys", [TS, 1024])
        dout("srp", [6, 128, 128]); dout("srs", [4, 6, 128, 128]); dout("mkv", [2, 256, 512])
        dout("wp0", [128, 512]); dout("wp1", [512, 512]); dout("wp2", [2048, 512])
        dout("ws0", [4, 128, 512]); dout("ws1", [4, 512, 512]); dout("ws2", [4, 2048, 512])
        dscr("x1s", [TT, 1024]); dscr("x2s", [TT, 1024])
        dscr("hK0", [128, 2, 128], BF16); dscr("hK1", [128, 2, 512], BF16); dscr("hK2", [128, 2, 2048], BF16)
        dscr("hV0", [128, 1, 256], BF16); dscr("hV1", [128, 4, 256], BF16); dscr("hV2", [128, 16, 256], BF16)
        dout("dbg_sin", [128, 768])
        if stage > 3:
            for nm_ in list(specs):
                if nm_ != "dbg_sin" or debug:
                    D[nm_]
        self.PS = nc.alloc_psum_tensor("ps", [128, 4096], F32).ap()
        self.PSB = self.PS.bitcast(BF16)
        self.fresh = {}
        self.build()

    def psf(self, b, n=512, p=128, off=0):
        return self.PS[0:p, 512 * b + off:512 * b + off + n]

    def psb(self, b, n=1024, p=128, off=0):
        return self.PSB[0:p, 1024 * b + off:1024 * b + off + n]

    def pbegin(self, *banks):
        for b in banks:
            self.fresh[b] = [True] * 4

    def mm(self, out, bank, lhsT, rhs, r, w=None, p0=0, p1=128, stop=False):
        fr = self.fresh[bank]
        qs = list(range(p0 // 32, (p1 + 31) // 32))
        start = fr[qs[0]]
        for q in qs:
            assert fr[q] == start
            fr[q] = False
        wk = [("ps", bank)] if w is None else w
        self.S.add("pe", lambda e: e.matmul(out, lhsT=lhsT, rhs=rhs, start=start, stop=stop, skip_group_check=True),
                   r=list(r) + ([] if start else wk), w=wk)

    def tr(self, out, bank, in_, ident, r):
        self.S.add("pe", lambda e: e.transpose(out, in_, ident), r=list(r), w=[("ps", bank)])

    def dma(self, q, out, in_, r, w, lane):
        self.S.add(q, lambda e: e.dma_start(out=out, in_=in_), r=r, w=w, dma=True, lane=lane)

    def dmas(self, q, pairs, r, w, lane):
        pairs = list(pairs)
        self.S.add(q, lambda e: [e.dma_start(out=o, in_=i) for (o, i) in pairs], r=r, w=w, dma=True, lane=lane, inc=16 * len(pairs))

    def act(self, out, in_, func, r, w, bias=None, scale=None):
        kw = {}
        if bias is not None:
            kw["bias"] = bias
        if scale is not None:
            kw["scale"] = scale
        self.S.add("act", lambda e: e.activation(out, in_, func, **kw), r=r, w=w)

    def tt(self, eng, out, in0, in1, op, r, w):
        self.S.add(eng, lambda e: e.tensor_tensor(out, in0, in1, op), r=r, w=w)

    def ts(self, eng, out, in0, s1, s2, op0, op1, r, w):
        if s2 is None:
            self.S.add(eng, lambda e: e.tensor_scalar(out, in0, s1, None, op0), r=r, w=w)
        else:
            self.S.add(eng, lambda e: e.tensor_scalar(out, in0, s1, s2, op0, op1), r=r, w=w)

    def stt(self, eng, out, in0, scalar, in1, op0, op1, r, w):
        self.S.add(eng, lambda e: e.scalar_tensor_tensor(out, in0, scalar, in1, op0, op1), r=r, w=w)

    def cp(self, eng, out, in_, r, w):
        if eng == "act":
            self.S.add("act", lambda e: e.copy(out, in_), r=r, w=w)
        else:
            self.S.add(eng, lambda e: e.tensor_copy(out, in_), r=r, w=w)

    def cst(self, blob, lay, name, p=128):
        o, w = lay.m[name]
        return blob[0:p, o:o + w]

    def load_ln(self, g_ap, b_ap, l):
        self.dma("sp", self.lnp[:, 0, :], g_ap[l:l + 1, :].broadcast_to([128, 1024]), r=[], w=["lnp0"], lane="lnp0")
        self.dma("sp", self.lnp[:, 1, :], b_ap[l:l + 1, :].broadcast_to([128, 1024]), r=[], w=["lnp1"], lane="lnp1")

    def layernorm(self, P, z, zkey, stat, skey):
        st6 = stat[0:P, 0:12]
        for hf in range(2):
            o6 = stat[0:P, 6 * hf:6 * hf + 6]
            zi = z[:, 512 * hf:512 * hf + 512]
            self.S.add("dve", lambda e, o6=o6, zi=zi: e.bn_stats(o6, zi), r=[zkey], w=[skey])
        mv = stat[0:P, 12:14]
        self.S.add("dve", lambda e: e.bn_aggr(mv, st6), r=[skey], w=[skey])
        ve = stat[0:P, 14:15]
        self.ts("dve", ve, stat[0:P, 13:14], LN_EPS, None, ALU.add, None, r=[skey], w=[skey])
        rstd = stat[0:P, 15:16]
        self.tt("pool", rstd, ve, self.cst(self.C0, C0_L, "nhalf", P), ALU.pow, r=[skey, "C0"], w=[skey])
        nmr = stat[0:P, 14:15]
        self.stt("dve", nmr, stat[0:P, 12:13], -1.0, rstd, ALU.mult, ALU.mult, r=[skey], w=[skey])
        self.act(z, z, AF.Identity, r=[zkey, skey], w=[zkey], bias=nmr, scale=rstd)
        self.tt("pool", z, z, self.lnp[0:P, 0, :], ALU.mult, r=[zkey, "lnp0"], w=[zkey])
        self.tt("pool", z, z, self.lnp[0:P, 1, :], ALU.add, r=[zkey, "lnp1"], w=[zkey])

    def to_XT(self, P, src_bf, skey, i, bank):
        c0 = 128 * i
        for c in range(8):
            self.tr(self.psb(bank, P, off=128 * c), bank, src_bf[:, 128 * c:128 * c + 128], self.identb[0:P, 0:P], r=[skey, "CCB"])
        src = self.psb(bank, 1024).rearrange("p (c t) -> p c t", c=8)[:, :, 0:P]
        self.cp("dve", self.XT[:, :, c0:c0 + P], src, r=[("ps", bank)], w=[("XT", i)])

    def rope_tm(self, P, src, H, d2, cos, sin, dst, rkeys, wkey, tmp, tkey):
        sv = src.rearrange("p (h t d) -> p h t d", h=H, t=2)
        dv = dst.rearrange("p (h t d) -> p h t d", h=H, t=2)
        lo, hi = sv[:, :, 0, :], sv[:, :, 1, :]
        cb = cos.unsqueeze(1).broadcast_to([P, H, d2])
        sb = sin.unsqueeze(1).broadcast_to([P, H, d2])
        t = [tmp[0:P, k, 0:H * d2].rearrange("p (h d) -> p h d", h=H) for k in range(4)]
        self.tt("dve", t[0], lo, cb, ALU.mult, r=rkeys, w=[(tkey, 0)])
        self.tt("dve", t[1], hi, sb, ALU.mult, r=rkeys, w=[(tkey, 1)])
        self.tt("dve", t[2], hi, cb, ALU.mult, r=rkeys, w=[(tkey, 2)])
        self.tt("dve", t[3], lo, sb, ALU.mult, r=rkeys, w=[(tkey, 3)])
        self.tt("pool", dv[:, :, 0, :], t[0], t[1], ALU.subtract, r=[(tkey, 0), (tkey, 1)], w=[wkey])
        self.tt("pool", dv[:, :, 1, :], t[2], t[3], ALU.add, r=[(tkey, 2), (tkey, 3)], w=[wkey])

    def proj_tm(self, P, i, W, wkey, col0, ncols, bank0):
        c0 = 128 * i
        done = 0
        b = bank0
        while done < ncols:
            n = min(512, ncols - done)
            self.pbegin(b)
            for d in range(8):
                self.mm(self.psf(b, n, P), b, self.XT[:, d, c0:c0 + P], W[:, d, col0 + done:col0 + done + n],
                        r=[("XT", i), wkey], p1=P, stop=(d == 7))
            done += n
            b += 1

    def build(self):
        nc, S, A, D = self.nc, self.S, self.A, self.D
        self.XT = A.alloc("XT", [128, 8, TT], BF16)
        self.CC = A.alloc("CC", [128, CC_L.n], F32)
        self.CCB = A.alloc("CCB", [128, CC_L.n], BF16)
        self.lnp = A.alloc("lnp", [128, 2, 1024], F32)
        self.memKT = A.alloc("memKT", [128, 2, 256], BF16)
        self.memV = A.alloc("memV", [128, 2, 256], BF16)
        self.dma("sp", self.CC, D["cc"], r=[], w=["CC"], lane="CC")
        self.cp("dve", self.CCB, self.CC, r=["CC"], w=["CCB"])
        self.identb = self.cst(self.CCB, CC_L, "ident")
        self.onesb = self.cst(self.CCB, CC_L, "ones")
        self.ones32b = self.cst(self.CCB, CC_L, "ones32")
        if self.stage <= -4:
            self.C0 = A.alloc("C0", [128, C0_L.n], F32)
            self.dma("sp", self.C0, D["c0"], r=[], w=["C0"], lane="C0")
            self.rt = [A.alloc("rt", [128, 128], F32) for _ in range(2)]
            self.rtn = 0
            m_ = A.mark()
            xin_ = A.alloc("xin_", [128, 1024], F32)
            xb_ = A.alloc("xb_", [128, 1024], BF16)
            for i in range(NT + 1):
                P = 128 if i < NT else TS
                src = D["xp"][128 * i:128 * i + 128, :] if i < NT else D["xs"]
                self.dma("sp", xin_[0:P, :], src, r=[], w=["xin_"], lane="xin_")
                self.cp("act", xb_[0:P, :], xin_[0:P, :], r=["xin_"], w=["xb_"])
                self.to_XT(P, xb_[0:P, :], "xb_", i, 6)
            S.barrier()
            A.release(m_)
            self.halo_kv()
            if self.stage != -4.05:
                self.layer1()
            S.barrier()
            S.emit(nc)
            return
        if self.stage <= -2:
            self.C0 = A.alloc("C0", [128, C0_L.n], F32)
            self.dma("sp", self.C0, D["c0"], r=[], w=["C0"], lane="C0")
            self.Sst = A.alloc("Sst", [128, 768], F32)
            self.Sstb = A.alloc("Sstb", [128, 768], BF16)
            self.rt = [A.alloc("rt", [128, 128], F32) for _ in range(2)]
            self.rtn = 0
            S.add("pool", lambda e: e.memset(self.Sst, 0.0), r=[], w=["Sst"])
            S.add("pool", lambda e: e.memset(self.Sstb, 0.0), r=[], w=["Sstb"])
            self.mem_kv(0)
            self.l0_pass2(1)
            S.barrier()
            S.emit(nc)
            return
        if self.stage < 0:
            self.C0 = A.alloc("C0", [128, C0_L.n], F32)
            self.dma("sp", self.C0, D["c0"], r=[], w=["C0"], lane="C0")
            self.mem_kv(0)
            S.barrier()
            S.emit(nc)
            return
        self.layer0()
        if self.stage > 3:
            S.barrier()
            self.layer1()
        S.barrier()
        S.emit(nc)

    def mem_kv(self, l, stat_bank=6):
        S, A, D = self.S, self.A, self.D
        st = self.stage
        m = A.mark()
        Wm = A.alloc("Wm", [128, 8, 512], BF16)
        memT = A.alloc("memT", [128, 8, 256], BF16)
        mtile = A.alloc("mtile", [128, 1024], F32)
        mtb = A.alloc("mtb", [128, 1024], BF16)
        mo = A.alloc("mo", [128, 512], F32)
        wm_v = D["w_mem"][l].rearrange("(c p) f -> p c f", p=128)
        self.dmas("pool", [(Wm[:, c, :], wm_v[:, c, :]) for c in range(8)], r=[], w=["Wm"], lane="Wm")
        for t in range(2):
            self.dma("sp", mtile, D["mem"][128 * t:128 * t + 128, :], r=[], w=["mtile"], lane="mtile")
            self.cp("act", mtb, mtile, r=["mtile"], w=["mtb"])
            if st == -1.1:
                continue
            for c in range(8):
                self.tr(self.psb(7, 128, off=128 * c), 7, mtb[:, 128 * c:128 * c + 128], self.identb, r=["mtb", "CCB"])
            self.cp("dve", memT[:, :, 128 * t:128 * t + 128], self.psb(7, 1024).rearrange("p (c t) -> p c t", c=2 * 4),
                    r=[("ps", 7)], w=["memT"])
        if st in (-1.1, -1.2):
            S.barrier(); A.release(m); return
        for t in range(2):
            self.pbegin(6)
            for d in range(8):
                self.mm(self.psf(6), 6, memT[:, d, 128 * t:128 * t + 128], Wm[:, d, :], r=["memT", "Wm"], stop=(d == 7))
            if st == -1.25:
                continue
            self.cp("dve", mo, self.psf(6), r=[("ps", 6)], w=["mo"])
            if st == -1.3:
                continue
            self.cp("dve", self.memV[:, t, :], self.psf(6, 256, off=256), r=[("ps", 6)], w=["memV"])
            if st == -1.4:
                continue
            if st == -1.51 and t == 1:
                continue
            if st == -1.52:
                self.dma("sp", D["dbg_sin"][:, 0:512], mo, r=["mo"], w=[], lane="mo")
                continue
            if st == -1.53:
                self.dma("pool", D["mkv"][l, 128 * t:128 * t + 128, :], mo, r=["mo"], w=[], lane="mo")
                continue
            self.dma("sp", D["mkv"][l, 128 * t:128 * t + 128, :], mo, r=["mo"], w=[], lane="mo")
        if st in (-1.25, -1.3, -1.4, -1.5, -1.51, -1.52, -1.53, -1.54):
            S.barrier(); A.release(m); return
        for c in range(2):
            self.pbegin(7)
            for d in range(8):
                self.mm(self.psf(7, 256), 7, Wm[:, d, 128 * c:128 * c + 128], memT[:, d, :], r=["memT", "Wm"], stop=(d == 7))
            self.cp("dve", self.memKT[:, c, :], self.psf(7, 256), r=[("ps", 7)], w=["memKT"])
        S.barrier()
        A.release(m)

    def mem_attn(self, P, QMT, qkey, KT, V, kvkeys, dst, dkey, n_q, PTm, rL, banks=(2, 3, 7)):
        bs0, bs1, bo = banks
        self.pbegin(bs0, bs1)
        for h in range(4):
            c, par = divmod(h, 2)
            bank = bs0 if par == 0 else bs1
            for mt in range(2):
                slot = (2 * c + mt) * n_q
                self.mm(self.psf(bank, n_q, off=slot), bank, KT[64 * par:64 * par + 64, c, 128 * mt:128 * mt + 128],
                        QMT[64 * par:64 * par + 64, c, :], r=[qkey] + kvkeys, stop=True)
        for k, bank in enumerate((bs0, bs1)):
            self.act(PTm[:, 4 * k * n_q:(4 * k + 4) * n_q], self.psf(bank, 4 * n_q), AF.Exp, r=[("ps", bank)], w=["PTm"], scale=0.125)
        self.pbegin(bo)
        for h in range(4):
            c, par = divmod(h, 2)
            for mt in range(2):
                ix = 4 * par + 2 * c + mt
                rhs = PTm[:, ix * n_q:(ix + 1) * n_q]
                self.mm(self.psf(bo, n_q, off=c * n_q)[64 * par:64 * par + 64, :], bo, V[:, mt, 64 * h:64 * h + 64], rhs,
                        r=["PTm"] + kvkeys, p0=64 * par, p1=64 * par + 64)
            for mt in range(2):
                ix = 4 * par + 2 * c + mt
                rhs = PTm[:, ix * n_q:(ix + 1) * n_q]
                self.mm(self.psf(bo, n_q, off=(2 + c) * n_q)[64 * par:64 * par + 64, :], bo, self.onesb, rhs,
                        r=["PTm", "CCB"], p0=64 * par, p1=64 * par + 64, stop=True)
        S = self.S
        rl = rL[:, 0:2 * n_q]
        S.add("dve", lambda e: e.reciprocal(rl, self.psf(bo, 2 * n_q, off=2 * n_q)), r=[("ps", bo)], w=["rL"])
        self.tt("dve", dst, self.psf(bo, 2 * n_q).rearrange("p (c q) -> p c q", c=2), rl.rearrange("p (c q) -> p c q", c=2),
                ALU.mult, r=[("ps", bo), "rL"], w=[dkey])

    def layer0(self):
        nc, S, A, D = self.nc, self.S, self.A, self.D
        C0 = self.C0 = A.alloc("C0", [128, C0_L.n], F32)
        self.dma("sp", C0, D["c0"], r=[], w=["C0"], lane="C0")
        Sst = self.Sst = A.alloc("Sst", [128, 768], F32)
        Sstb = self.Sstb = A.alloc("Sstb", [128, 768], BF16)
        self.rt = [A.alloc("rt", [128, 128], F32) for _ in range(2)]
        self.rtn = 0
        dec1 = self.cst(C0, C0_L, "dec1").rearrange("p (i h) -> p i h", i=32)
        wa_v = D["w_in_a"].rearrange("(c p) f -> p c f", p=128)

        m1 = A.mark()
        Wkv = A.alloc("Wkv0", [128, 8, 1536], BF16)
        self.dmas("pool", [(Wkv[:, c, :], wa_v[:, c, 768:2304]) for c in range(8)], r=[], w=["Wkv0"], lane="Wkv0")
        xin = [A.alloc("xin", [128, 1024], F32) for _ in range(2)]
        xb = A.alloc("xb", [128, 1024], BF16)
        Kb = A.alloc("Kb", [128, 768], BF16)
        Vb = A.alloc("Vb", [128, 768], BF16)
        tmp = A.alloc("ropetmp", [128, 4, 384], F32)
        self.pbegin(4, 5)
        NP1 = 2 * NT
        for i in range(NP1):
            xi = xin[i % 2]
            xk = ("xin", i % 2)
            self.dma("sp", xi, D["xpp"][128 * i:128 * i + 128, :], r=[], w=[xk], lane="xin%d" % (i % 2))
            rc, rs, rk = self.rope_load(D["rope0"], i, 64)
            self.cp("act", xb, xi, r=[xk], w=["xb"])
            ti = i % 2
            self.to_XT(128, xb, "xb", ti, 6)
            c0 = 128 * ti
            for (col0, b0) in ((0, 0), (768, 2)):
                for (off, n, b) in ((0, 512, b0), (512, 256, b0 + 1)):
                    self.pbegin(b)
                    for d in range(8):
                        self.mm(self.psf(b, n), b, self.XT[:, d, c0:c0 + 128], Wkv[:, d, col0 + off:col0 + off + n],
                                r=[("XT", ti), "Wkv0"], stop=(d == 7))
            self.rope_tm(128, self.PS[:, 0:768], 6, 64, rc, rs, Kb, [("ps", 0), ("ps", 1), rk], "Kb", tmp, "rtmp")
            self.tt("dve", Vb.rearrange("p (h v) -> p h v", h=6), self.PS[:, 1024:1792].rearrange("p (h v) -> p h v", h=6),
                    dec1[:, i, :].unsqueeze(2).broadcast_to([128, 6, 128]), ALU.mult, r=[("ps", 2), ("ps", 3), "C0"], w=["Vb"])
            for h in range(6):
                b = 4 if h < 4 else 5
                self.mm(self.psf(b, 128, off=128 * (h % 4)), b, Kb[:, 128 * h:128 * h + 128], Vb[:, 128 * h:128 * h + 128],
                        r=["Kb", "Vb"], stop=(i == NP1 - 1))
        self.cp("dve", Sst[:, 0:512], self.psf(4), r=[("ps", 4)], w=["Sst"])
        self.cp("dve", Sst[:, 512:768], self.psf(5, 256), r=[("ps", 5)], w=["Sst"])
        self.cp("dve", Sstb, Sst, r=["Sst"], w=["Sstb"])
        if self.debug:
            self.dma("sp", D["dbg_sin"], Sst, r=["Sst"], w=[], lane="dbg")
        if self.stage <= 1:
            return
        S.barrier()
        A.release(m1)
        self.mem_kv(0)
        for seg in range(2):
            self.l0_pass2(seg)
            if self.stage <= 2 and seg == 1:
                return
            self.ffn(0, seg == 1, False)
            if seg == 0:
                self.halo_kv()
            if self.stage <= 2.5 and seg == 0:
                return

    def rope_load(self, tab, idx, d2):
        k = self.rtn % 2
        self.rtn += 1
        rt = self.rt[k]
        self.dma("sp", rt[:, 0:2 * d2], tab[idx], r=[], w=[("rt", k)], lane="rt%d" % k)
        return rt[:, 0:d2], rt[:, d2:2 * d2], ("rt", k)

    def l0_pass2(self, seg):
        nc, S, A, D = self.nc, self.S, self.A, self.D
        C0 = self.C0
        Sst, Sstb = self.Sst, self.Sstb
        dq = self.cst(C0, C0_L, "dq").rearrange("p (v h) -> p v h", v=2)
        dk = self.cst(C0, C0_L, "dk").rearrange("p (v h) -> p v h", v=2)
        wa_v = D["w_in_a"].rearrange("(c p) f -> p c f", p=128)
        m2 = A.mark()
        Wa = A.alloc("Wa", [128, 8, 3328], BF16)
        self.dmas("pool", [(Wa[:, c, :], wa_v[:, c, :]) for c in range(8)], r=[], w=["Wa"], lane="Wa")
        Wo = A.alloc("Wo", [128, 8, 1024], BF16)
        wo_v = D["w_out"][0].rearrange("(c p) f -> p c f", p=128)
        self.dmas("pool", [(Wo[:, c, :], wo_v[:, c, :]) for c in range(8)], r=[], w=["Wo"], lane="Wo")
        wak = ["Wa"]
        self.load_ln(D["ln_mix_g"], D["ln_mix_b"], 0)
        xin = [A.alloc("xin", [128, 1024], F32) for _ in range(2)]
        xb = A.alloc("xb", [128, 1024], BF16)
        tmp = A.alloc("ropetmp", [128, 4, 384], F32)
        Qr = A.alloc("Qr", [128, 768], F32)
        Kr = A.alloc("Kr", [128, 768], F32)
        Qb = A.alloc("Qb", [128, 768], BF16)
        Kb = A.alloc("Kb", [128, 768], BF16)
        Vb = A.alloc("Vb", [128, 768], BF16)
        Gs = A.alloc("Gs", [128, 768], F32)
        QT = A.alloc("QT", [128, 6, 128], BF16)
        KT = A.alloc("KT", [128, 6, 128], BF16)
        PT = A.alloc("PT", [128, 6, 128], BF16)
        Of = Qr
        gst = A.alloc("gst", [128, 64], F32)
        mixb = A.alloc("mixb", [128, 768], BF16)
        mixT = A.alloc("mixT", [128, 8, 128], BF16)
        QMT = A.alloc("QMT", [128, 2, 128], BF16)
        PTm = A.alloc("PTm", [128, 1024], BF16)
        rL = A.alloc("rL", [128, 256], F32)
        stat = A.alloc("stat", [128, 16], F32)
        stmp = Kr
        causal = self.cst(C0, C0_L, "causal")
        mask_s = self.cst(C0, C0_L, "mask_s", 32)
        G128 = self.cst(C0, C0_L, "G128")
        G8 = self.cst(C0, C0_L, "G8")
        Stf = A.alloc("Stf", [128, 4, 768], F32)
        Stb = A.alloc("Stbf", [128, 4, 768], BF16)
        Qm = A.alloc("Qm", [128, 6, 4, 32], BF16)
        Kbm = A.alloc("Kbm", [32, 4, 768], BF16)
        mks = A.alloc("mks", [128, 2, 256], BF16)
        mvs = A.alloc("mvs", [128, 2, 256], BF16)
        ctile = A.alloc("ctile", [128, 512], F32)
        ctb = A.alloc("ctb", [128, 512], BF16)

        tl = range(NT + (1 if seg == 1 else 0))
        if self.stage <= -2:
            tl = [0] if self.stage > -3 else [NT]
        for i in tl:
            P = 128 if i < NT else TS
            v = 0 if i < NT else 1
            c0 = 128 * i
            xi = xin[i % 2]
            xk = ("xin", i % 2)
            xsrc = D["xp"] if seg == 1 else D["xpv"]
            src = xsrc[128 * i:128 * i + 128, :] if i < NT else D["xs"]
            self.dma("sp", xi[0:P, :], src, r=[], w=[xk], lane="xin%d" % (i % 2))
            rc, rs, rk = self.rope_load(D["rope0"], (32 + 16 * seg + i) if i < NT else 64, 64)
            rc, rs = rc[0:P, :], rs[0:P, :]
            self.cp("act", xb[0:P, :], xi[0:P, :], r=[xk], w=["xb"])
            self.to_XT(P, xb[0:P, :], "xb", i, 6)
            self.proj_tm(P, i, Wa, wak[0], 0, 768, 0)
            self.rope_tm(P, self.PS[0:P, 0:768], 6, 64, rc, rs, Qr[0:P, :], [("ps", 0), ("ps", 1), rk], "Qr", tmp, "rtmp")
            self.tt("pool", Qb[0:P, :].rearrange("p (h v) -> p h v", h=6), Qr[0:P, :].rearrange("p (h v) -> p h v", h=6),
                    dq[0:P, v, :].unsqueeze(2).broadcast_to([P, 6, 128]), ALU.mult, r=["Qr", "C0"], w=["Qb"])
            if self.stage < 0 and int(abs(self.stage) * 10 + 1e-6) - 10 * int(abs(self.stage)) == 1:
                break
            self.proj_tm(P, i, Wa, wak[0], 768, 768, 2)
            self.rope_tm(P, self.PS[0:P, 1024:1792], 6, 64, rc, rs, Kr[0:P, :], [("ps", 2), ("ps", 3), rk], "Kr", tmp, "rtmp")
            self.tt("pool", Kb[0:P, :].rearrange("p (h v) -> p h v", h=6), Kr[0:P, :].rearrange("p (h v) -> p h v", h=6),
                    dk[0:P, v, :].unsqueeze(2).broadcast_to([P, 6, 128]), ALU.mult, r=["Kr", "C0"], w=["Kb"])
            if self.stage < 0 and int(abs(self.stage) * 10 + 1e-6) - 10 * int(abs(self.stage)) == 2:
                break
            for (srcb, skey, dstT, dkey, bank) in ((Qb, "Qb", QT, "QT", 4), (Kb, "Kb", KT, "KT", 5)):
                for h in range(6):
                    self.tr(self.psb(bank, P, off=128 * h), bank, srcb[0:P, 128 * h:128 * h + 128], self.identb[0:P, 0:P], r=[skey, "CCB"])
                self.cp("dve", dstT[:, :, 0:P], self.psb(bank, 768).rearrange("p (h t) -> p h t", h=6)[:, :, 0:P], r=[("ps", bank)], w=[dkey])
            if self.stage < 0 and int(abs(self.stage) * 10 + 1e-6) - 10 * int(abs(self.stage)) == 3:
                break
            self.proj_tm(P, i, Wa, wak[0], 1536, 768, 0)
            self.cp("act", Vb[0:P, :], self.PS[0:P, 0:768], r=[("ps", 0), ("ps", 1)], w=["Vb"])
            self.proj_tm(P, i, Wa, wak[0], 2304, 768, 2)
            self.act(Gs[0:P, :], self.PS[0:P, 1024:1792], AF.Silu, r=[("ps", 2), ("ps", 3)], w=["Gs"])
            if self.stage < 0 and int(abs(self.stage) * 10 + 1e-6) - 10 * int(abs(self.stage)) == 4:
                break
            self.pbegin(7)
            for c in range(2):
                for d in range(8):
                    self.mm(self.psf(7, P, off=128 * c), 7, Wa[:, d, 3072 + 128 * c:3072 + 128 * c + 128], self.XT[:, d, c0:c0 + P],
                            r=[("XT", i)] + wak, stop=(d == 7))
            self.cp("act", QMT[:, :, 0:P], self.psf(7, 256).rearrange("p (c t) -> p c t", c=2)[:, :, 0:P], r=[("ps", 7)], w=["QMT"])
            if self.stage < 0 and int(abs(self.stage) * 10 + 1e-6) - 10 * int(abs(self.stage)) == 5:
                break
            if i < NT:
                self.pbegin(4, 5)
                for h in range(6):
                    b = 4 if h < 4 else 5
                    self.mm(self.psf(b, 128, off=128 * (h % 4)), b, KT[:, h, :], QT[:, h, :], r=["KT", "QT"], stop=True)
                self.tt("dve", PT, self.PS[:, 2048:2816].rearrange("p (h q) -> p h q", h=6),
                        causal.unsqueeze(1).broadcast_to([128, 6, 128]), ALU.mult, r=[("ps", 4), ("ps", 5), "C0"], w=["PT"])
                self.pbegin(0, 1)
                for h in range(6):
                    b = 0 if h < 4 else 1
                    o = self.psf(b, 128, off=128 * (h % 4))
                    self.mm(o, b, PT[:, h, :], Vb[:, 128 * h:128 * h + 128], r=["PT", "Vb"])
                    self.mm(o, b, QT[:, h, :], Sstb[:, 128 * h:128 * h + 128], r=["QT", "Sstb"], stop=True)
                self.pbegin(2, 3)
                for h in range(6):
                    b = 2 if h < 4 else 3
                    self.mm(self.psf(b, 128, off=128 * (h % 4)), b, Kb[:, 128 * h:128 * h + 128], Vb[:, 128 * h:128 * h + 128],
                            r=["Kb", "Vb"], stop=True)
                self.tt("dve", stmp, self.PS[:, 1024:1792], Sst, ALU.add, r=[("ps", 2), ("ps", 3), "Sst"], w=["Kr"])
                self.tt("pool", Sst, stmp, G128, ALU.mult, r=["Kr", "C0"], w=["Sst"])
                self.cp("pool", Sstb, Sst, r=["Sst"], w=["Sstb"])
                if i == NT - 1 and seg == 1:
                    self.dma("sp", D["srp"].rearrange("h d v -> d h v"), Sst.rearrange("p (h v) -> p h v", h=6), r=["Sst"], w=[], lane="srp")
            else:
                self.dma("sp", Stf.rearrange("p b (h v) -> p b h v", h=6), D["sret"].rearrange("b h d v -> d b h v"), r=[], w=["Stf"], lane="Stf")
                self.cp("act", Stb, Stf, r=["Stf"], w=["Stb"])
                self.pbegin(4)
                for h in range(6):
                    self.mm(self.psf(4, 32, 32, off=32 * h), 4, KT[:, h, 0:32], QT[:, h, 0:32], r=["KT", "QT"], p1=32, stop=True)
                self.tt("dve", PT[0:32, :, 0:32], self.psf(4, 192, 32).rearrange("p (h q) -> p h q", h=6),
                        mask_s.unsqueeze(1).broadcast_to([32, 6, 32]), ALU.mult, r=[("ps", 4), "C0"], w=["PT"])
                blk = self.cst(C0, C0_L, "blk").rearrange("p (b q) -> p b q", b=4)
                self.tt("dve", Qm, QT[:, :, 0:32].unsqueeze(2).broadcast_to([128, 6, 4, 32]),
                        blk.unsqueeze(1).broadcast_to([128, 6, 4, 32]), ALU.mult, r=["QT", "C0"], w=["Qm"])
                rowm = self.cst(C0, C0_L, "rowm", 32)
                self.tt("dve", Kbm, Kb[0:32, :].unsqueeze(1).broadcast_to([32, 4, 768]),
                        rowm.unsqueeze(2).broadcast_to([32, 4, 768]), ALU.mult, r=["Kb", "C0"], w=["Kbm"])
                self.pbegin(0, 1)
                for h in range(6):
                    b = 0 if h < 4 else 1
                    o = self.psf(b, 128, 32, off=128 * (h % 4))
                    self.mm(o, b, PT[0:32, h, 0:32], Vb[0:32, 128 * h:128 * h + 128], r=["PT", "Vb"], p1=32)
                    for bl in range(4):
                        self.mm(o, b, Qm[:, h, bl, :], Stb[:, bl, 128 * h:128 * h + 128], r=["Qm", "Stb"], p1=32, stop=(bl == 3))
                for bl in range(4):
                    self.pbegin(2, 3)
                    for h in range(6):
                        b = 2 if h < 4 else 3
                        self.mm(self.psf(b, 128, off=128 * (h % 4)), b, Kbm[:, bl, 128 * h:128 * h + 128], Vb[0:32, 128 * h:128 * h + 128],
                                r=["Kbm", "Vb"], stop=True)
                    self.tt("dve", stmp, self.PS[:, 1024:1792], Stf[:, bl, :], ALU.add, r=[("ps", 2), ("ps", 3), "Stf"], w=["Kr"])
                    self.tt("pool", Stf[:, bl, :], stmp, G8, ALU.mult, r=["Kr", "C0"], w=["Stf"])
                self.dma("sp", D["srs"].rearrange("b h d v -> d b h v"), Stf.rearrange("p b (h v) -> p b h v", h=6), r=["Stf"], w=[], lane="srs")
            if self.stage < 0 and int(abs(self.stage) * 10 + 1e-6) - 10 * int(abs(self.stage)) == 6:
                break
            self.cp("act", Of[0:P, :], self.PS[0:P, 0:768], r=[("ps", 0), ("ps", 1)], w=["Qr"])
            for h in range(6):
                o6 = gst[0:P, 6 * h:6 * h + 6]
                oi = Of[0:P, 128 * h:128 * h + 128]
                S.add("dve", lambda e, o6=o6, oi=oi: e.bn_stats(o6, oi), r=["Qr"], w=["gst"])
                mv = gst[0:P, 36 + 2 * h:38 + 2 * h]
                S.add("dve", lambda e, o6=o6, mv=mv: e.bn_aggr(mv, o6), r=["gst"], w=["gst"])
            mvv = gst[0:P, 36:48].rearrange("p (h t) -> p h t", t=2)
            ve = gst[0:P, 48:54]
            self.ts("dve", ve, mvv[:, :, 1], LN_EPS, None, ALU.add, None, r=["gst"], w=["gst"])
            rs = gst[0:P, 54:60]
            self.tt("pool", rs, ve, self.cst(C0, C0_L, "nhalf", P).broadcast_to([P, 6]), ALU.pow, r=["gst", "C0"], w=["gst"])
            Ov = Of[0:P, :].rearrange("p (h v) -> p h v", h=6)
            self.tt("dve", Ov, Ov, mvv[:, :, 0].unsqueeze(2).broadcast_to([P, 6, 128]), ALU.subtract, r=["Qr", "gst"], w=["Qr"])
            self.tt("dve", Ov, Ov, rs.unsqueeze(2).broadcast_to([P, 6, 128]), ALU.mult, r=["Qr", "gst"], w=["Qr"])
            self.tt("pool", mixb[0:P, :], Of[0:P, :], Gs[0:P, :], ALU.mult, r=["Qr", "Gs"], w=["mixb"])
            for h in range(6):
                self.tr(self.psb(4, P, off=128 * h), 4, mixb[0:P, 128 * h:128 * h + 128], self.identb[0:P, 0:P], r=["mixb", "CCB"])
            self.cp("act", mixT[:, 0:6, 0:P], self.psb(4, 768).rearrange("p (h t) -> p h t", h=6)[:, :, 0:P], r=[("ps", 4)], w=["mixT"])
            if self.stage < 0 and int(abs(self.stage) * 10 + 1e-6) - 10 * int(abs(self.stage)) == 7:
                break
            if i < NT:
                self.mem_attn(128, QMT, "QMT", self.memKT, self.memV, ["memKT", "memV"], mixT[:, 6:8, :], "mixT", 128, PTm, rL)
            else:
                for bl in range(4):
                    for t in range(2):
                        self.dma("sp", ctile, D["cmkv"][0, bl, 128 * t:128 * t + 128, :], r=[], w=["ctile"], lane="ctile")
                        self.cp("act", ctb, ctile, r=["ctile"], w=["ctb"])
                        for c in range(2):
                            self.tr(self.psb(5, 128, off=128 * c), 5, ctb[:, 128 * c:128 * c + 128], self.identb, r=["ctb", "CCB"])
                        self.cp("dve", mks[:, :, 128 * t:128 * t + 128], self.psb(5, 256).rearrange("p (c t) -> p c t", c=2), r=[("ps", 5)], w=["mks"])
                        self.cp("pool", mvs[:, t, :], ctb[:, 256:512], r=["ctb"], w=["mvs"])
                    self.mem_attn(128, QMT[:, :, 8 * bl:8 * bl + 8], "QMT", mks, mvs, ["mks", "mvs"], mixT[:, 6:8, 8 * bl:8 * bl + 8], "mixT", 8, PTm, rL)
            if self.debug:
                for c in range(8):
                    pass
            if self.stage < 0 and int(abs(self.stage) * 10 + 1e-6) - 10 * int(abs(self.stage)) == 8:
                break
            self.pbegin(0, 1)
            for hf in range(2):
                for c in range(8):
                    self.mm(self.psf(hf, 512, P), hf, mixT[:, c, 0:P], Wo[:, c, 512 * hf:512 * hf + 512], r=["mixT", "Wo"], p1=P, stop=(c == 7))
            z = xi[0:P, :]
            for hf in range(2):
                self.stt("dve", z[:, 512 * hf:512 * hf + 512], z[:, 512 * hf:512 * hf + 512], ALPHA, self.psf(hf, 512, P), ALU.mult, ALU.add,
                         r=[xk, ("ps", hf)], w=[xk])
            self.layernorm(P, z, xk, stat, "stat")
            self.dma("sp", D["x1s"][c0:c0 + P, :], z, r=[xk], w=[("x1s", i)], lane="x1o%d" % (i % 2))
            self.cp("act", xb[0:P, :], z, r=[xk], w=["xb"])
            self.to_XT(P, xb[0:P, :], "xb", i, 6)
        S.barrier()
        A.release(m2)


    def ffn(self, l, with_sample, final):
        nc, S, A, D = self.nc, self.S, self.A, self.D
        m = A.mark()
        hT = A.alloc("hT", [128, NF, 1056], BF16)
        Wi = [A.alloc("Wi", [128, 8, 256], BF16) for _ in range(3)]
        WoR = A.alloc("WoR", [128, NF, 1024], BF16)
        sg = [A.alloc("sg", [128, 512], BF16) for _ in range(2)]
        x1t = [A.alloc("x1t", [128, 1024], F32) for _ in range(2)]
        xb2 = A.alloc("xb2", [128, 1024], BF16)
        stat = A.alloc("stat2", [128, 16], F32)
        self.load_ln(D["ln_ffn_g"], D["ln_ffn_b"], l)
        wi_v = D["w_ffn_in"][l].rearrange("(c p) f -> p c f", p=128)
        wo_v = D["w_ffn_out"][l]
        nw = 0
        nwo = 0
        nx = 0
        for half in range(2):
            tiles = list(range(8 * half, 8 * half + 8))
            if half == 1 and with_sample:
                tiles.append(NT)
            col0 = 1024 * half
            pieces = [(0, 512), (512, 512)] + ([(1024, TS)] if (half == 1 and with_sample) else [])
            k = 0
            for f in range(NF):
                wi = Wi[nw % 3]
                wk = ("Wi", nw % 3)
                self.dmas("pool", [(wi[:, :, 0:128], wi_v[:, :, 128 * f:128 * f + 128]),
                                   (wi[:, :, 128:256], wi_v[:, :, FF + 128 * f:FF + 128 * f + 128])], r=[], w=[wk], lane="Wi%d" % (nw % 3))
                nw += 1
                if half == 0 and f == 2:
                    self.dmas("pool", [(WoR[:, ff, :], wo_v[128 * ff:128 * ff + 128, :]) for ff in range(NF)], r=[], w=["WoR"], lane="WoR")
                for (po, pn) in pieces:
                    bg, bu = (0, 1) if k % 2 == 0 else (2, 3)
                    sgk = sg[k % 2]
                    k += 1
                    xk = [("XT", (col0 + po) // 128 + q) for q in range((pn + 127) // 128)]
                    for (bank, wc) in ((bg, 0), (bu, 128)):
                        self.pbegin(bank)
                        for d in range(8):
                            self.mm(self.psf(bank, pn), bank, wi[:, d, wc:wc + 128], self.XT[:, d, col0 + po:col0 + po + pn],
                                    r=xk + [wk], stop=(d == 7))
                    self.act(sgk[:, 0:pn], self.psf(bg, pn), AF.Silu, r=[("ps", bg)], w=[("sg", k % 2)])
                    self.tt("dve", hT[:, f, po:po + pn], self.psf(bu, pn), sgk[:, 0:pn], ALU.mult, r=[("ps", bu), ("sg", k % 2)], w=[("hT", f)])
            groups = [tiles[q:q + 3] for q in range(0, len(tiles), 3)]
            for grp in groups:
                self.pbegin(*range(2 * len(grp)))
                for f in range(NF):
                    wo = WoR[:, f, :]
                    wok = "WoR"
                    for kk, t in enumerate(grp):
                        P = 128 if t < NT else TS
                        lc = (t - 8 * half) * 128
                        for hf in range(2):
                            self.mm(self.psf(2 * kk + hf, 512, P), 2 * kk + hf, hT[:, f, lc:lc + P], wo[:, 512 * hf:512 * hf + 512],
                                    r=[("hT", f), wok], p1=P, stop=(f == NF - 1))
                for kk, t in enumerate(grp):
                    P = 128 if t < NT else TS
                    c0 = 128 * t
                    xt = x1t[nx % 2]
                    xk_ = ("x1t", nx % 2)
                    lane = "x1t%d" % (nx % 2)
                    nx += 1
                    z = xt[0:P, :]
                    self.dma("sp", z, D["x1s"][c0:c0 + P, :], r=[("x1s", t)], w=[xk_], lane=lane)
                    for hf in range(2):
                        self.stt("dve", z[:, 512 * hf:512 * hf + 512], z[:, 512 * hf:512 * hf + 512], ALPHA, self.psf(2 * kk + hf, 512, P),
                                 ALU.mult, ALU.add, r=[xk_, ("ps", 2 * kk + hf)], w=[xk_])
                    self.layernorm(P, z, xk_, stat, "stat2")
                    if final:
                        dst = D["yp"][c0:c0 + P, :] if t < NT else D["ys"]
                        self.dma("sp", dst, z, r=[xk_], w=[], lane=lane + "o")
                    else:
                        self.dma("sp", D["x2s"][c0:c0 + P, :], z, r=[xk_], w=[("x2s", t)], lane=lane + "o")
                        self.cp("act", xb2[0:P, :], z, r=[xk_], w=["xb2"])
                        self.to_XT(P, xb2[0:P, :], "xb2", t, 6)
        S.barrier()
        A.release(m)


    def k_tile(self, P, i, rope_idx, Wg, KTdst, dkey, kout=None):
        c0 = 128 * i
        self.pbegin(0)
        for d in range(8):
            self.mm(self.psf(0, 256, P), 0, self.XT[:, d, c0:c0 + P], Wg[:, d, 0:256], r=[("XT", i), "Wg"], p1=P, stop=(d == 7))
        rc, rs, rk = self.rope_load(self.D["rope1"], rope_idx, 32)
        Kr = self.Kr1[self.kn % 2]
        kk = ("Kr1", self.kn % 2)
        lane = "Kr1%d" % (self.kn % 2)
        self.kn += 1
        self.rope_tm(P, self.psf(0, 256, P), 4, 32, rc[0:P, :], rs[0:P, :], Kr[0:P, :], [("ps", 0), rk], kk, self.tmp1, "rtmp")
        if kout is not None:
            self.dma("sp", kout, Kr[0:P, :], r=[kk], w=[], lane=lane)
        self.cp("act", self.Kb1[0:P, :], Kr[0:P, :], r=[kk], w=["Kb1"])
        for hp in range(2):
            self.tr(self.psb(6, P, off=128 * hp), 6, self.Kb1[0:P, 128 * hp:128 * hp + 128], self.identb[0:P, 0:P], r=["Kb1", "CCB"])
        self.cp("dve", KTdst, self.psb(6, 256).rearrange("p (c t) -> p c t", c=2)[:, :, 0:P], r=[("ps", 6)], w=[dkey])

    def v_block(self, P, cols, xkeys, Wg, Vdst, dkey, vout=None):
        self.pbegin(1)
        for d in range(8):
            self.mm(self.psf(1, 256, P), 1, self.XT[:, d, cols], Wg[:, d, 256:512], r=xkeys + ["Wg"], p1=P, stop=(d == 7))
        self.cp("dve", Vdst, self.psf(1, 256, P), r=[("ps", 1)], w=[dkey])
        if vout is not None:
            Vf = self.Vf1[self.vn % 2]
            vk = ("Vf1", self.vn % 2)
            lane = "Vf1%d" % (self.vn % 2)
            self.vn += 1
            self.cp("dve", Vf[0:P, :], self.psf(1, 256, P), r=[("ps", 1)], w=[vk])
            self.dma("sp", vout, Vf[0:P, :], r=[vk], w=[], lane=lane)

    def l1_alloc_kv(self):
        A = self.A
        self.Kr1 = [A.alloc("Kr1", [128, 256], F32) for _ in range(2)]
        self.Vf1 = [A.alloc("Vf1", [128, 256], F32) for _ in range(2)]
        self.Kb1 = A.alloc("Kb1", [128, 256], BF16)
        self.Vnat = A.alloc("Vnat", [128, 256], BF16)
        self.tmp1 = A.alloc("tmp1", [128, 4, 384], F32)
        self.Wg = A.alloc("Wg", [128, 8, 512], BF16)
        self.kn = 0
        self.vn = 0

    def load_wg(self, g):
        wv = self.D["w_kv"].rearrange("(c p) f -> p c f", p=128)
        self.dmas("pool", [(self.Wg[:, :, 0:256], wv[:, :, 256 * g:256 * g + 256]),
                           (self.Wg[:, :, 256:512], wv[:, :, 768 + 256 * g:768 + 256 * g + 256])], r=[], w=["Wg"], lane="Wg")

    def halo_kv(self):
        S, A, D = self.S, self.A, self.D
        m = A.mark()
        self.l1_alloc_kv()
        KTh = A.alloc("KTh", [128, 2, 2048], BF16)
        Vh = A.alloc("Vh", [128, 16, 256], BF16)
        for g in range(3):
            dil, W = DILS[g], WINS[g]
            self.load_wg(g)
            nt = W // 128
            for q in range(nt):
                i = NT - nt + q
                self.k_tile(128, i, i, self.Wg, KTh[:, :, 128 * q:128 * q + 128], "KTh")
            self.dma("sp", D["hK%d" % g], KTh[:, :, 0:W], r=["KTh"], w=["hK"], lane="hK")
            cl = NT // dil - 1
            for r in range(dil):
                st = 128 * dil * cl + r
                cols = slice(st, st + dil * 127 + 1, dil)
                xkeys = [("XT", dil * cl + q) for q in range(dil)]
                self.v_block(128, cols, xkeys, self.Wg, Vh[:, r, :], "Vh")
            self.dma("sp", D["hV%d" % g], Vh[:, 0:dil, :], r=["Vh"], w=["hV"], lane="hV")
        S.barrier()
        A.release(m)

    def b1_tile(self, P, i, lhs, lkey, Wo, xi, xk, lane, stat):
        D = self.D
        c0 = 128 * i
        self.pbegin(0, 1)
        for hf in range(2):
            for c in range(8):
                self.mm(self.psf(hf, 512, P), hf, lhs(c), Wo[:, c, 512 * hf:512 * hf + 512], r=[lkey, "Wo"], p1=P, stop=(c == 7))
        z = xi[0:P, :]
        for hf in range(2):
            self.stt("dve", z[:, 512 * hf:512 * hf + 512], z[:, 512 * hf:512 * hf + 512], ALPHA, self.psf(hf, 512, P), ALU.mult, ALU.add,
                     r=[xk, ("ps", hf)], w=[xk])
        self.layernorm(P, z, xk, stat, "stat")
        self.dma("sp", D["x1s"][c0:c0 + P, :], z, r=[xk], w=[("x1s", i)], lane=lane)
        self.cp("act", self.xb1[0:P, :], z, r=[xk], w=["xb1"])
        self.to_XT(P, self.xb1[0:P, :], "xb1", i, 6)

    def layer1(self):
        nc, S, A, D = self.nc, self.S, self.A, self.D
        ml = A.mark()
        C1 = self.C1 = A.alloc("C1", [128, C1_L.n], F32)
        C1B = A.alloc("C1B", [128, C1_L.n], BF16)
        self.dma("sp", C1, D["c1"], r=[], w=["C1"], lane="C1")
        self.cp("dve", C1B, C1, r=["C1"], w=["C1B"])
        self.mem_kv(1)
        mixT = A.alloc("mixT1", [128, 8, TT], BF16)
        Ltot = A.alloc("Ltot", [128, 2, TT], F32)
        rLs = A.alloc("rLs", [128, 256], F32)
        PTm = A.alloc("PTm1", [128, 1024], BF16)
        m = A.mark()
        Wb = A.alloc("Wb", [128, 8, 1024], BF16)
        wb_v = D["w_in_b"].rearrange("(c p) f -> p c f", p=128)
        self.dmas("pool", [(Wb[:, c, :], wb_v[:, c, :]) for c in range(8)], r=[], w=["Wb"], lane="Wb")
        Qr = A.alloc("Qr1", [128, 768], F32)
        Qb = A.alloc("Qb1", [128, 768], BF16)
        tmpq = A.alloc("tmpq", [128, 4, 384], F32)
        for i in range(NT + 1):
            P = 128 if i < NT else TS
            c0 = 128 * i
            self.proj_tm(P, i, Wb, "Wb", 0, 768, 0)
            rc, rs, rk = self.rope_load(D["rope1"], (16 + i) if i < NT else 32, 32)
            self.rope_tm(P, self.PS[0:P, 0:768], 12, 32, rc[0:P, :], rs[0:P, :], Qr[0:P, :], [("ps", 0), ("ps", 1), rk], "Qr1", tmpq, "rtmp")
            self.cp("pool", Qb[0:P, :], Qr[0:P, :], r=["Qr1"], w=["Qb1"])
            for h in range(6):
                self.tr(self.psb(6, P, off=128 * h), 6, Qb[0:P, 128 * h:128 * h + 128], self.identb[0:P, 0:P], r=["Qb1", "CCB"])
            self.cp("act", mixT[:, 0:6, c0:c0 + P], self.psb(6, 768).rearrange("p (h t) -> p h t", h=6)[:, :, 0:P], r=[("ps", 6)], w=[("mixT", i)])
            self.pbegin(7)
            for c in range(2):
                for d in range(8):
                    self.mm(self.psf(7, P, off=128 * c), 7, Wb[:, d, 768 + 128 * c:768 + 128 * c + 128], self.XT[:, d, c0:c0 + P],
                            r=[("XT", i), "Wb"], stop=(d == 7))
            self.cp("dve", mixT[:, 6:8, c0:c0 + P], self.psf(7, 256).rearrange("p (c t) -> p c t", c=2)[:, :, 0:P], r=[("ps", 7)], w=[("mixm", i)])
        if self.stage == -4.1:
            return
        mks = A.alloc("mks1", [128, 2, 256], BF16)
        mvs = A.alloc("mvs1", [128, 2, 256], BF16)
        ctile = A.alloc("ctile1", [128, 512], F32)
        ctb = A.alloc("ctb1", [128, 512], BF16)
        for i in range(NT):
            c0 = 128 * i
            self.mem_attn(128, mixT[:, 6:8, c0:c0 + 128], ("mixm", i), self.memKT, self.memV, ["memKT", "memV"], mixT[:, 6:8, c0:c0 + 128],
                          ("mixm", i), 128, PTm, rLs)
        for bl in range(4):
            for t in range(2):
                self.dma("sp", ctile, D["cmkv"][1, bl, 128 * t:128 * t + 128, :], r=[], w=["ctile"], lane="ctile")
                self.cp("act", ctb, ctile, r=["ctile"], w=["ctb"])
                for c in range(2):
                    self.tr(self.psb(5, 128, off=128 * c), 5, ctb[:, 128 * c:128 * c + 128], self.identb, r=["ctb", "CCB"])
                self.cp("dve", mks[:, :, 128 * t:128 * t + 128], self.psb(5, 256).rearrange("p (c t) -> p c t", c=2), r=[("ps", 5)], w=["mks"])
                self.cp("pool", mvs[:, t, :], ctb[:, 256:512], r=["ctb"], w=["mvs"])
            cs = T + 8 * bl
            self.mem_attn(128, mixT[:, 6:8, cs:cs + 8], ("mixm", NT), mks, mvs, ["mks", "mvs"], mixT[:, 6:8, cs:cs + 8], ("mixm", NT), 8, PTm, rLs)
        S.barrier()
        A.release(m)
        if self.stage == -4.2:
            return
        m = A.mark()
        self.l1_alloc_kv()
        PTa = [A.alloc("PTa", [128, 544], BF16) for _ in range(4)]
        ctile = A.alloc("ctile3", [128, 512], F32)
        ctb = A.alloc("ctb3", [128, 512], BF16)
        KTn = A.alloc("KTn", [128, 2, 32], BF16)
        Vns = A.alloc("Vns", [128, 256], BF16)
        S.add("pool", lambda e: e.memset(Vns, 0.0), r=[], w=["Vns"])
        for pt_ in PTa:
            S.add("pool", lambda e, pt_=pt_: e.memset(pt_, 0.0), r=[], w=[("PTa", PTa.index(pt_))])
        sb_i = 0
        ob_i = 0
        pa_i = 0
        first = True
        for g in (2, 1, 0):
            dil, W = DILS[g], WINS[g]
            mg = A.mark()
            KT = A.alloc("KTg", [128, 2, W + T], BF16)
            Vb = A.alloc("Vbg", [128, dil + NT, 256], BF16)
            KTs = A.alloc("KTs", [128, 2, W], BF16)
            Vs = A.alloc("Vs", [128, W // 128, 256], BF16)
            self.load_wg(g)
            self.dma("sp", KT[:, :, 0:W], D["hK%d" % g], r=[], w=["KTh"], lane="hKl")
            self.dma("sp", Vb[:, 0:dil, :], D["hV%d" % g], r=[], w=["Vbh"], lane="hVl")
            nlast = W // 128
            for i in range(NT):
                kout = None
                if i >= NT - nlast:
                    q = i - (NT - nlast)
                    kout = D["wp%d" % g][128 * q:128 * q + 128, 0:256]
                self.k_tile(128, i, 16 + i, self.Wg, KT[:, :, W + 128 * i:W + 128 * i + 128], ("KT", i), kout)
            if self.stage == -4.21:
                return
            nc_ = NT // dil
            for c in range(nc_):
                for r in range(dil):
                    st = 128 * dil * c + r
                    cols = slice(st, st + dil * 127 + 1, dil)
                    xkeys = [("XT", dil * c + q) for q in range(dil)]
                    self.v_block(128, cols, xkeys, self.Wg, Vb[:, dil + c * dil + r, :], ("Vb", dil + c * dil + r), None)
            for q in range(nlast):
                i = NT - nlast + q
                self.v_block(128, slice(128 * i, 128 * i + 128), [("XT", i)], self.Wg, self.Vnat, "Vnat", D["wp%d" % g][128 * q:128 * q + 128, 256:512])
            if self.stage == -4.22:
                return
            self.k_tile(TS, NT, 32, self.Wg, KTn, "KTn", None)
            Krs = self.Kr1[(self.kn - 1) % 2]
            krk = ("Kr1", (self.kn - 1) % 2)
            self.v_block(TS, slice(T, T + TS), [("XT", NT)], self.Wg, Vns[0:TS, :], "Vns", None)
            Vfs = self.Vf1[self.vn % 2]
            vfk = ("Vf1", self.vn % 2)
            self.vn += 1
            self.cp("dve", Vfs[0:TS, :], self.psf(1, 256, TS), r=[("ps", 1)], w=[vfk])
            if self.stage == -4.23:
                return
            for bl in range(4):
                self.dma("sp", D["ws%d" % g][bl, W - 8:W, 0:256], Krs[8 * bl:8 * bl + 8, :], r=[krk], w=[], lane="wsk")
                self.dma("sp", D["ws%d" % g][bl, W - 8:W, 256:512], Vfs[8 * bl:8 * bl + 8, :], r=[vfk], w=[], lane="wsv")
                self.dma("pool", D["ws%d" % g][bl, 0:W - 8, :], D["cw%d" % g][bl, 8:W, :], r=[], w=[], lane="wsc")
            if self.stage == -4.3:
                return
            mb4 = self.cst(C1B, C1_L, "mb4")
            mb4h = self.cst(C1B, C1_L, "mb4h")
            for c in range(nc_):
                for r in range(dil):
                    st = 128 * dil * c + r
                    qcols = slice(st, st + dil * 127 + 1, dil)
                    kcur = slice(W + st, W + st + dil * 127 + 1, dil)
                    kprev = slice(st, st + dil * 127 + 1, dil)
                    icur = dil + c * dil + r
                    iprev = icur - dil
                    kkeys = [("KT", dil * c + q) for q in range(dil)] + ([("KT", dil * (c - 1) + q) for q in range(dil)] if c > 0 else ["KTh"])
                    vkeys = [("Vb", icur), (("Vb", iprev) if c > 0 else "Vbh")]
                    qks = [("mq", g, 0, r, c), ("mq", g, 1, r, c)]
                    bo = 4 + (ob_i % 2)
                    ob_i += 1
                    bx = 2 * (sb_i % 2)
                    sb_i += 1
                    pts = []
                    for par in range(2):
                        bank = bx + par
                        ps_ = slice(64 * par, 64 * par + 64)
                        self.pbegin(bank)
                        for hp in range(2):
                            for half, kc in enumerate((kprev, kcur)):
                                self.mm(self.psf(bank, 128, off=(2 * hp + half) * 128), bank, KT[ps_, hp, kc], mixT[ps_, 2 * g + hp, qcols],
                                        r=kkeys + qks, stop=(hp == 1 and half == 1))
                        pt = PTa[pa_i % 4]
                        pk = ("PTa", pa_i % 4)
                        pa_i += 1
                        self.act(pt[:, 0:512], self.psf(bank, 512), AF.Exp, r=[("ps", bank)], w=[pk], scale=0.125)
                        self.tt("pool", pt[:, 0:512], pt[:, 0:512], (mb4h if c == 0 else mb4), ALU.mult, r=[pk, "C1B"], w=[pk])
                        pts.append((pt, pk))
                    self.pbegin(bo)
                    for hp in range(2):
                        for par in range(2):
                            h = 2 * hp + par
                            pt, pk = pts[par]
                            for half, idx in enumerate((iprev, icur)):
                                self.mm(self.psf(bo, 128, off=128 * hp)[64 * par:64 * par + 64, :], bo, Vb[:, idx, 64 * h:64 * h + 64],
                                        pt[:, (2 * hp + half) * 128:(2 * hp + half) * 128 + 128], r=[pk] + vkeys, p0=64 * par, p1=64 * par + 64)
                            for half in range(2):
                                self.mm(self.psf(bo, 128, off=128 * (2 + hp))[64 * par:64 * par + 64, :], bo, self.onesb,
                                        pt[:, (2 * hp + half) * 128:(2 * hp + half) * 128 + 128], r=[pk, "CCB"], p0=64 * par, p1=64 * par + 64,
                                        stop=(hp == 1 and par == 1 and half == 1))
                    self.cp("dve", mixT[:, 2 * g:2 * g + 2, qcols], self.psf(bo, 256).rearrange("p (c q) -> p c q", c=2), r=[("ps", bo)], w=qks)
                    lsrc = self.psf(bo, 256, off=256).rearrange("p (c q) -> p c q", c=2)
                    if first:
                        self.cp("dve", Ltot[:, :, qcols], lsrc, r=[("ps", bo)], w=[("L", r, c)])
                    else:
                        self.tt("dve", Ltot[:, :, qcols], Ltot[:, :, qcols], lsrc, ALU.add, r=[("ps", bo), ("L", r, c)], w=[("L", r, c)])
            if self.stage == -4.4:
                return
            nm = W // 128
            sm = self.cst(C1B, C1_L, "sm%d" % g)
            smn = self.cst(C1B, C1_L, "smn%d" % g).rearrange("p (b x) -> p b x", b=4)
            for bl in range(4):
                for mt in range(nm):
                    self.dma("sp", ctile, D["cw%d" % g][bl, 128 * mt:128 * mt + 128, :], r=[], w=["ctile"], lane="ctile")
                    self.cp("act", ctb, ctile, r=["ctile"], w=["ctb"])
                    for cc_ in range(2):
                        self.tr(self.psb(6, 128, off=128 * cc_), 6, ctb[:, 128 * cc_:128 * cc_ + 128], self.identb, r=["ctb", "CCB"])
                    self.cp("dve", KTs[:, :, 128 * mt:128 * mt + 128], self.psb(6, 256).rearrange("p (c t) -> p c t", c=2), r=[("ps", 6)], w=["KTs"])
                    self.cp("pool", Vs[:, mt, :], ctb[:, 256:512], r=["ctb"], w=["Vs"])
                cs = T + 8 * bl
                pts = []
                for par in range(2):
                    ps_ = slice(64 * par, 64 * par + 64)
                    bS, bN = 2 + par, par
                    self.pbegin(bS, bN)
                    for mt in range(nm):
                        for hp in range(2):
                            self.mm(self.psf(bS, 8, off=16 * mt + 8 * hp), bS, KTs[ps_, hp, 128 * mt:128 * mt + 128], mixT[ps_, 2 * g + hp, cs:cs + 8],
                                    r=["KTs", ("mqs", g)], stop=(mt == nm - 1 and hp == 1))
                    for hp in range(2):
                        self.mm(self.psf(bN, 8, TS, off=8 * hp), bN, KTn[ps_, hp, :], mixT[ps_, 2 * g + hp, cs:cs + 8], r=["KTn", ("mqs", g)], p1=TS, stop=(hp == 1))
                    pt = PTa[pa_i % 4]
                    pk = ("PTa", pa_i % 4)
                    pa_i += 1
                    self.act(pt[:, 0:16 * nm], self.psf(bS, 16 * nm), AF.Exp, r=[("ps", bS)], w=[pk], scale=0.125)
                    self.act(pt[0:TS, 512:528], self.psf(bN, 16, TS), AF.Exp, r=[("ps", bN)], w=[pk], scale=0.125)
                    self.tt("pool", pt[:, 0:16 * nm], pt[:, 0:16 * nm], sm, ALU.mult, r=[pk, "C1B"], w=[pk])
                    self.tt("pool", pt[0:TS, 512:528], pt[0:TS, 512:528], smn[0:TS, bl, :], ALU.mult, r=[pk, "C1B"], w=[pk])
                    pts.append((pt, pk))
                bo = 4 + (ob_i % 2)
                ob_i += 1
                self.pbegin(bo)
                for h in range(4):
                    hp, par = divmod(h, 2)
                    pt, pk = pts[par]
                    for (kind, off) in (("o", 8 * hp), ("l", 16 + 8 * hp)):
                        o = self.psf(bo, 8, off=off)[64 * par:64 * par + 64, :]
                        for mt in range(nm):
                            lhs = Vs[:, mt, 64 * h:64 * h + 64] if kind == "o" else self.onesb
                            self.mm(o, bo, lhs, pt[:, 16 * mt + 8 * hp:16 * mt + 8 * hp + 8], r=[pk, "Vs", "CCB"], p0=64 * par, p1=64 * par + 64)
                        lhs = Vns[:, 64 * h:64 * h + 64] if kind == "o" else self.ones32b
                        self.mm(o, bo, lhs, pt[:, 512 + 8 * hp:512 + 8 * hp + 8], r=[pk, "Vns", "CCB"], p0=64 * par, p1=64 * par + 64, stop=True)
                self.cp("dve", mixT[:, 2 * g:2 * g + 2, cs:cs + 8], self.psf(bo, 16).rearrange("p (c q) -> p c q", c=2), r=[("ps", bo)], w=[("mqs", g)])
                lsrc = self.psf(bo, 16, off=16).rearrange("p (c q) -> p c q", c=2)
                if first:
                    self.cp("dve", Ltot[:, :, cs:cs + 8], lsrc, r=[("ps", bo)], w=[("Ls", bl)])
                else:
                    self.tt("dve", Ltot[:, :, cs:cs + 8], Ltot[:, :, cs:cs + 8], lsrc, ALU.add, r=[("ps", bo), ("Ls", bl)], w=[("Ls", bl)])
            if self.stage == -4.5:
                return
            first = False
            S.barrier()
            A.release(mg)
        A.release(m)
        S.add("dve", lambda e: e.reciprocal(Ltot, Ltot), r=[], w=["Ltot"])
        for ch in range(6):
            self.tt("dve", mixT[:, ch, :], mixT[:, ch, :], Ltot[:, ch % 2, :], ALU.mult, r=["Ltot"], w=[("mixc", ch)])
        if self.debug:
            pass
        S.barrier()
        if self.stage == -4.6:
            return
        m = A.mark()
        Wo = A.alloc("Wo1", [128, 8, 1024], BF16)
        wo_v = D["w_out"][1].rearrange("(c p) f -> p c f", p=128)
        self.dmas("pool", [(Wo[:, c, :], wo_v[:, c, :]) for c in range(8)], r=[], w=["Wo"], lane="Wo")
        self.load_ln(D["ln_mix_g"], D["ln_mix_b"], 1)
        xin = [A.alloc("xin1", [128, 1024], F32) for _ in range(2)]
        self.xb1 = A.alloc("xb1", [128, 1024], BF16)
        stat = A.alloc("stat1", [128, 16], F32)
        for i in range(NT + 1):
            P = 128 if i < NT else TS
            c0 = 128 * i
            xi = xin[i % 2]
            xk = ("xin", i % 2)
            self.dma("sp", xi[0:P, :], D["x2s"][c0:c0 + P, :], r=[], w=[xk], lane="xin%d" % (i % 2))
            self.b1_tile(P, i, (lambda c, c0=c0, P=P: mixT[:, c, c0:c0 + P]), "mixall", Wo, xi, xk, "x1o%d" % (i % 2), stat)
        S.barrier()
        A.release(m)
        if self.stage == -4.7:
            return
        A.release(ml)
        self.ffn(1, True, True)


_CACHE = {}


def _get_prog(stage, debug):
    key = (stage, debug)
    if key not in _CACHE:
        _CACHE[key] = Prog(stage, debug)
    return _CACHE[key]


def _in_maps(inp):
    f = lambda a: np.ascontiguousarray(a, dtype=np.float32)
    maps = []
    for c in range(NCORES):
        b, j = divmod(c, 4)
        cc, c0, c1, rope0, rope1 = build_consts(c)
        xb_ = np.asarray(inp["x_prompt"][b], dtype=np.float32)
        xpad = np.concatenate([np.zeros((3 * T, 1024), np.float32), xb_], 0)
        m = {
            "xp": f(inp["x_prompt"][b, T * j:T * (j + 1)]),
            "xpv": f(xpad[3 * T + T * (j - 1):3 * T + T * j]),
            "xpp": f(xpad[3 * T + T * (j - 3):3 * T + T * (j - 1)]),
            "rope0": rope0, "rope1": rope1,
            "xs": f(inp["x_sample"][4 * c:4 * c + 4].reshape(TS, 1024)),
            "mem": f(inp["mem_prompt"][b]),
            "cmkv": f(inp["cache_mem_kv"][:, 4 * c:4 * c + 4].reshape(2, 4, 256, 512)),
            "sret": f(inp["state_ret"][0, 4 * c:4 * c + 4]),
            "cw0": f(inp["cache_win_kv_g1"][4 * c:4 * c + 4].reshape(4, 128, 512)),
            "cw1": f(inp["cache_win_kv_g2"][4 * c:4 * c + 4].reshape(4, 512, 512)),
            "cw2": f(inp["cache_win_kv_g3"][4 * c:4 * c + 4].reshape(4, 2048, 512)),
            "w_in_a": f(inp["w_in_a"][0]), "w_in_b": f(inp["w_in_b"][0]), "w_out": f(inp["w_out"]),
            "w_kv": f(inp["w_kv_shared"]), "w_mem": f(inp["w_mem_kv"]),
            "ln_mix_g": f(inp["ln_mix_g"]), "ln_mix_b": f(inp["ln_mix_b"]),
            "ln_ffn_g": f(inp["ln_ffn_g"]), "ln_ffn_b": f(inp["ln_ffn_b"]),
            "w_ffn_in": f(inp["w_ffn_in"]), "w_ffn_out": f(inp["w_ffn_out"]),
            "cc": cc, "c0": c0, "c1": c1,
        }
        maps.append(m)
    return maps


def _run(inp, stage=99, debug=False):
    prog = _get_prog(stage, debug)
    maps = [{k: v for k, v in m.items() if k in prog.D} for m in _in_maps(inp)]
    res = run_bass_kernel_spmd(prog.nc, maps, core_ids=list(range(NCORES)))
    return res.results


def kernel(**inp):
    R = _run(inp)
    yp = np.stack([np.concatenate([R[4 * b + j]["yp"] for j in range(4)], 0) for b in range(2)], 0)
    ys = np.concatenate([R[c]["ys"].reshape(4, 8, 1024) for c in range(8)], 0)
    srp = np.stack([R[4 * b + 3]["srp"] for b in range(2)], 0)[None]
    srs = np.concatenate([R[c]["srs"] for c in range(8)], 0)[None]
    mkv = np.stack([np.stack([R[4 * b]["mkv"][l].reshape(256, 2, 4, 64) for b in range(2)], 0) for l in range(2)], 0)
    outs = [yp, ys, srp, srs, mkv]
    for g, W in enumerate(WINS):
        outs.append(np.stack([R[4 * b + 3]["wp%d" % g].reshape(W, 2, 4, 64) for b in range(2)], 0))
    for g, W in enumerate(WINS):
        outs.append(np.concatenate([R[c]["ws%d" % g].reshape(4, W, 2, 4, 64) for c in range(8)], 0))
    return tuple(np.ascontiguousarray(o, dtype=np.float32) for o in outs)
```

```python
import numpy as np
from contextlib import ExitStack
import concourse.bass as bass
import concourse.mybir as mybir
from concourse.bass_utils import run_bass_kernel_spmd

F32 = mybir.dt.float32
BF16 = mybir.dt.bfloat16
AF = mybir.ActivationFunctionType
ALU = mybir.AluOpType
AX = mybir.AxisListType

NEG = -30000.0
ALPHA = 4.0 ** 0.25
LN_EPS = 1e-5
NCORES = 8
T = 2048
NT = 16
TS = 32
TT = T + TS
FF = 2816
NF = 22
DILS = (1, 4, 16)
WINS = (128, 512, 2048)


class Op:
    __slots__ = ("eng", "fn", "sdeps", "is_dma", "lane", "inc", "marked", "count", "idx")


class Sched:
    ENGS = ("pe", "act", "dve", "pool", "sp")

    def __init__(self):
        self.ops = {e: [] for e in self.ENGS}
        self.all = []
        self.lw = {}
        self.rd = {}
        self.floor = None
        self.last = {}
        self.pending_dma = []
        self.lanes = []

    def add(self, eng, fn, r=(), w=(), dma=False, lane=None, inc=16):
        op = Op()
        op.eng, op.fn, op.is_dma, op.lane, op.inc = eng, fn, dma, lane, inc
        op.marked = dma
        op.count = 0
        op.idx = len(self.all)
        if dma and lane not in self.lanes:
            self.lanes.append(lane)
        deps = {}

        def dep(a, kind):
            if (not a.is_dma) and a.eng == eng and (eng == "pe" or kind == "war"):
                return
            deps[a.idx] = a

        for k in r:
            a = self.lw.get(k)
            if a is not None:
                dep(a, "raw")
        for k in w:
            a = self.lw.get(k)
            if a is not None:
                dep(a, "waw")
            for a in self.rd.get(k, ()):
                dep(a, "war")
        if self.floor is not None and not (eng == "pool" and not dma and False):
            deps[self.floor.idx] = self.floor
        for k in w:
            self.lw[k] = op
            self.rd[k] = []
        for k in r:
            if k in w:
                continue
            lst = self.rd.setdefault(k, [])
            if not dma:
                lst[:] = [x for x in lst if x.is_dma or x.eng != eng]
            lst.append(op)
        op.sdeps = list(deps.values())
        for a in op.sdeps:
            a.marked = True
        self.all.append(op)
        self.ops[eng].append(op)
        if dma:
            self.pending_dma.append(op)
        else:
            self.last[eng] = op
        return op

    def barrier(self):
        deps = list(self.last.values()) + list(self.pending_dma)
        op = Op()
        op.eng, op.fn, op.is_dma, op.lane, op.inc = "pool", (lambda e: e.nop()), False, None, 1
        op.marked = False
        op.count = 0
        op.idx = len(self.all)
        op.sdeps = deps
        for a in deps:
            a.marked = True
        self.all.append(op)
        self.ops["pool"].append(op)
        self.last = {"pool": op}
        self.pending_dma = []
        self.floor = op
        self.lw = {}
        self.rd = {}

    def emit(self, nc):
        cnt = {}
        for op in self.all:
            key = op.lane if op.is_dma else op.eng
            if op.marked:
                cnt[key] = cnt.get(key, 0) + (op.inc if op.is_dma else 1)
            op.count = cnt.get(key, 0)
        keys = ["pe", "act", "dve", "pool"] + self.lanes
        with ExitStack() as es:
            sems = {}
            for i, k in enumerate(keys):
                sems[k] = es.enter_context(nc.semaphore("s%d" % i))
            block = es.enter_context(nc.Block())

            def mk(engname):
                def body(e):
                    waited = {}
                    for op in self.ops[engname]:
                        need = {}
                        for a in op.sdeps:
                            key = a.lane if a.is_dma else a.eng
                            if a.count > need.get(key, 0):
                                need[key] = a.count
                        for key, val in need.items():
                            if waited.get(key, 0) < val:
                                e.wait_ge(sems[key], val)
                                waited[key] = val
                        ins = op.fn(e)
                        if op.marked:
                            key = op.lane if op.is_dma else op.eng
                            if isinstance(ins, list):
                                for x in ins:
                                    x.then_inc(sems[key], op.inc // len(ins))
                            else:
                                ins.then_inc(sems[key], op.inc if op.is_dma else 1)
                return body

            block.sync(mk("sp"))
            block.scalar(mk("act"))
            block.vector(mk("dve"))
            block.gpsimd(mk("pool"))
            block.tensor(mk("pe"))


class Arena:
    def __init__(self, nc, base=16512, limit=229376):
        self.nc, self.top, self.limit, self.n = nc, base, limit, 0
        self.peak = base

    def alloc(self, name, shape, dtype):
        esz = 4 if dtype == F32 else 2
        nb = esz
        for s in shape[1:]:
            nb *= s
        nb = (nb + 63) // 64 * 64
        off = self.top
        self.top += nb
        self.peak = max(self.peak, self.top)
        assert self.top <= self.limit, "SBUF overflow at %s: %d" % (name, self.top)
        self.n += 1
        return self.nc.alloc_sbuf_tensor_at("%s_%d" % (name, self.n), list(shape), dtype, offset=off).ap()

    def mark(self):
        return self.top

    def release(self, m):
        self.top = m


def _rope_tab(pos, d):
    inv = (np.float32(10000.0) ** (-(np.arange(0, d, 2, dtype=np.float32)) / np.float32(d))).astype(np.float32)
    ang = (pos.astype(np.float32)[:, None] * inv[None, :]).astype(np.float32)
    return np.cos(ang).astype(np.float32), np.sin(ang).astype(np.float32)


class Cols:
    def __init__(self):
        self.n = 0
        self.m = {}

    def add(self, name, w):
        self.m[name] = (self.n, w)
        self.n += w


def _layout():
    c0 = Cols()
    c0.add("dq", 12); c0.add("dk", 12); c0.add("dec1", 192)
    c0.add("G128", 768); c0.add("G8", 768)
    c0.add("causal", 128); c0.add("mask_s", 32); c0.add("blk", 128); c0.add("rowm", 4)
    c0.add("nhalf", 1)
    c1 = Cols()
    c1.add("mb4", 512); c1.add("mb4h", 512)
    for g in range(3):
        c1.add("sm%d" % g, (WINS[g] // 128) * 16); c1.add("smn%d" % g, 64)
    cc = Cols()
    cc.add("ident", 128); cc.add("ones", 64); cc.add("ones32", 64)
    return cc, c0, c1


CC_L, C0_L, C1_L = _layout()


def build_consts(c):
    b, j = divmod(c, 4)
    h = np.arange(6, dtype=np.float64)
    lg = np.log1p(-np.exp2(-5.0 - h))
    p = np.arange(128)
    cc = np.zeros((128, CC_L.n), np.float32)
    c0 = np.zeros((128, C0_L.n), np.float32)
    c1 = np.zeros((128, C1_L.n), np.float32)

    def put(arr, lay, name, val):
        o, w = lay.m[name]
        arr[:, o:o + w] = np.asarray(val, np.float32).reshape(128, w)

    put(cc, CC_L, "ident", np.eye(128))
    put(cc, CC_L, "ones", np.ones((128, 64)))
    o32 = np.zeros((128, 64)); o32[:32] = 1.0
    put(cc, CC_L, "ones32", o32)
    pos0 = np.zeros((65, 128), np.float32)
    for i in range(64):
        pos0[i] = np.maximum(2048 * (j - 3) + 128 * i + p, 0)
    pos0[64, :32] = 8192 + (p[:32] % 8)
    cs, sn = _rope_tab(pos0.reshape(-1), 128)
    rope0 = np.concatenate([cs.reshape(65, 128, 64), sn.reshape(65, 128, 64)], 2).astype(np.float32)
    pos1 = np.zeros((33, 128), np.float32)
    for i in range(32):
        pos1[i] = np.maximum(2048 * (j - 1) + 128 * i + p, 0)
    pos1[32, :32] = 8192 + (p[:32] % 8)
    cs, sn = _rope_tab(pos1.reshape(-1), 64)
    rope1 = np.concatenate([cs.reshape(33, 128, 32), sn.reshape(33, 128, 32)], 2).astype(np.float32)
    sc = 128.0 ** -0.5
    dq = np.zeros((128, 2, 6)); dk = np.zeros((128, 2, 6))
    dq[:, 0] = np.exp((p[:, None] + 1.0) * lg[None]); dk[:, 0] = np.exp(-(p[:, None] + 1.0) * lg[None]) * sc
    dq[:, 1] = np.exp(((p[:, None] % 8) + 1.0) * lg[None]); dk[:, 1] = np.exp(-((p[:, None] % 8) + 1.0) * lg[None]) * sc
    put(c0, C0_L, "dq", dq); put(c0, C0_L, "dk", dk)
    dec1 = np.zeros((128, 32, 6))
    for i in range(32):
        dec1[:, i] = np.exp((4095.0 - (128 * i + p[:, None])) * lg[None]) * sc
    put(c0, C0_L, "dec1", dec1)
    put(c0, C0_L, "G128", np.broadcast_to(np.repeat(np.exp(128.0 * lg), 128)[None], (128, 768)))
    put(c0, C0_L, "G8", np.broadcast_to(np.repeat(np.exp(8.0 * lg), 128)[None], (128, 768)))
    put(c0, C0_L, "causal", (p[:, None] <= p[None, :]).astype(np.float32))
    ms = np.zeros((128, 32))
    for k in range(32):
        for q in range(32):
            ms[k, q] = 1.0 if (k // 8 == q // 8 and k % 8 <= q % 8) else 0.0
    put(c0, C0_L, "mask_s", ms)
    blk = np.zeros((128, 4, 32))
    for bl in range(4):
        blk[:, bl, 8 * bl:8 * bl + 8] = 1.0
    put(c0, C0_L, "blk", blk)
    rowm = np.zeros((128, 4))
    for bl in range(4):
        rowm[8 * bl:8 * bl + 8, bl] = 1.0
    put(c0, C0_L, "rowm", rowm)
    put(c0, C0_L, "nhalf", np.full((128, 1), -0.5))
    mprev = np.where(p[:, None] >= p[None, :], 1.0, 0.0)
    mcur = np.where(p[:, None] <= p[None, :], 1.0, 0.0)
    put(c1, C1_L, "mb4", np.concatenate([mprev, mcur, mprev, mcur], 1))
    mprevh = mprev * (0.0 if j == 0 else 1.0)
    put(c1, C1_L, "mb4h", np.concatenate([mprevh, mcur, mprevh, mcur], 1))
    for g, dil in enumerate(DILS):
        t = np.arange(8)
        nm = WINS[g] // 128
        mf = (((p[:, None] - t[None]) % dil == 0) & (p[:, None] >= t[None])).astype(np.float32)
        mr = (((p[:, None] - t[None]) % dil == 0)).astype(np.float32)
        sm = np.zeros((128, nm, 2, 8), np.float32)
        for mt in range(nm):
            sm[:, mt, :, :] = (mf if mt == 0 else mr)[:, None, :]
        put(c1, C1_L, "sm%d" % g, sm)
        mn = np.zeros((128, 4, 2, 8), np.float32)
        for r_ in range(32):
            blr, tr = divmod(r_, 8)
            for tq in range(8):
                if tr <= tq and (tq - tr) % dil == 0:
                    mn[r_, blr, :, tq] = 1.0
        put(c1, C1_L, "smn%d" % g, mn)
    return cc, c0, c1, rope0, rope1


class Prog:
    def __init__(self, stage=99, debug=False):
        self.stage = stage
        self.debug = debug
        nc = self.nc = bass.Bass("TRN2", target_bir_lowering=False)
        self.S = Sched()
        self.A = Arena(nc)
        specs = self.specs = {}

        def din(name, shape):
            specs[name] = (list(shape), F32, "ExternalInput")

        def dout(name, shape):
            specs[name] = (list(shape), F32, "ExternalOutput")

        def dscr(name, shape, dt=F32):
            specs[name] = (list(shape), dt, "Internal")

        class LazyD(dict):
            def __missing__(d, name):
                shape, dt, kind = specs[name]
                if kind == "Internal":
                    ap = nc.dram_tensor(name, shape, dt).ap()
                else:
                    ap = nc.dram_tensor(name, shape, dt, kind=kind).ap()
                d[name] = ap
                return ap

        D = self.D = LazyD()
        din("xp", [T, 1024]); din("xpv", [T, 1024]); din("xpp", [2 * T, 1024]); din("xs", [TS, 1024]); din("mem", [256, 1024])
        din("rope0", [65, 128, 128]); din("rope1", [33, 128, 64])
        din("cmkv", [2, 4, 256, 512]); din("sret", [4, 6, 128, 128])
        din("cw0", [4, 128, 512]); din("cw1", [4, 512, 512]); din("cw2", [4, 2048, 512])
        din("w_in_a", [1024, 3328]); din("w_in_b", [1024, 1024]); din("w_out", [2, 1024, 1024])
        din("w_kv", [1024, 1536]); din("w_mem", [2, 1024, 512])
        din("ln_mix_g", [2, 1024]); din("ln_mix_b", [2, 1024]); din("ln_ffn_g", [2, 1024]); din("ln_ffn_b", [2, 1024])
        din("w_ffn_in", [2, 1024, 2 * FF]); din("w_ffn_out", [2, FF, 1024])
        din("cc", [128, CC_L.n]); din("c0", [128, C0_L.n]); din("c1", [128, C1_L.n])
        dout("yp", [T, 1024]); dout("ys", [TS, 1024])
        dout("srp", [6, 128, 128]); dout("srs", [4, 6, 128, 128]); dout("mkv", [2, 256, 512])
        dout("wp0", [128, 512]); dout("wp1", [512, 512]); dout("wp2", [2048, 512])
        dout("ws0", [4, 128, 512]); dout("ws1", [4, 512, 512]); dout("ws2", [4, 2048, 512])
        dscr("x1s", [TT, 1024]); dscr("x2s", [TT, 1024])
        dscr("hK0", [128, 2, 128], BF16); dscr("hK1", [128, 2, 512], BF16); dscr("hK2", [128, 2, 2048], BF16)
        dscr("hV0", [128, 1, 256], BF16); dscr("hV1", [128, 4, 256], BF16); dscr("hV2", [128, 16, 256], BF16)
        dout("dbg_sin", [128, 768])
        if stage > 3:
            for nm_ in list(specs):
                if nm_ != "dbg_sin" or debug:
                    D[nm_]
        self.PS = nc.alloc_psum_tensor("ps", [128, 4096], F32).ap()
        self.PSB = self.PS.bitcast(BF16)
        self.fresh = {}
        self.build()

    def psf(self, b, n=512, p=128, off=0):
        return self.PS[0:p, 512 * b + off:512 * b + off + n]

    def psb(self, b, n=1024, p=128, off=0):
        return self.PSB[0:p, 1024 * b + off:1024 * b + off + n]

    def pbegin(self, *banks):
        for b in banks:
            self.fresh[b] = [True] * 4

    def mm(self, out, bank, lhsT, rhs, r, w=None, p0=0, p1=128, stop=False):
        fr = self.fresh[bank]
        qs = list(range(p0 // 32, (p1 + 31) // 32))
        start = fr[qs[0]]
        for q in qs:
            assert fr[q] == start
            fr[q] = False
        wk = [("ps", bank)] if w is None else w
        self.S.add("pe", lambda e: e.matmul(out, lhsT=lhsT, rhs=rhs, start=start, stop=stop, skip_group_check=True),
                   r=list(r) + ([] if start else wk), w=wk)

    def tr(self, out, bank, in_, ident, r):
        self.S.add("pe", lambda e: e.transpose(out, in_, ident), r=list(r), w=[("ps", bank)])

    def dma(self, q, out, in_, r, w, lane):
        self.S.add(q, lambda e: e.dma_start(out=out, in_=in_), r=r, w=w, dma=True, lane=lane)

    def dmas(self, q, pairs, r, w, lane):
        pairs = list(pairs)
        self.S.add(q, lambda e: [e.dma_start(out=o, in_=i) for (o, i) in pairs], r=r, w=w, dma=True, lane=lane, inc=16 * len(pairs))

    def act(self, out, in_, func, r, w, bias=None, scale=None):
        kw = {}
        if bias is not None:
            kw["bias"] = bias
        if scale is not None:
            kw["scale"] = scale
        self.S.add("act", lambda e: e.activation(out, in_, func, **kw), r=r, w=w)

    def tt(self, eng, out, in0, in1, op, r, w):
        self.S.add(eng, lambda e: e.tensor_tensor(out, in0, in1, op), r=r, w=w)

    def ts(self, eng, out, in0, s1, s2, op0, op1, r, w):
        if s2 is None:
            self.S.add(eng, lambda e: e.tensor_scalar(out, in0, s1, None, op0), r=r, w=w)
        else:
            self.S.add(eng, lambda e: e.tensor_scalar(out, in0, s1, s2, op0, op1), r=r, w=w)

    def stt(self, eng, out, in0, scalar, in1, op0, op1, r, w):
        self.S.add(eng, lambda e: e.scalar_tensor_tensor(out, in0, scalar, in1, op0, op1), r=r, w=w)

    def cp(self, eng, out, in_, r, w):
        if eng == "act":
            self.S.add("act", lambda e: e.copy(out, in_), r=r, w=w)
        else:
            self.S.add(eng, lambda e: e.tensor_copy(out, in_), r=r, w=w)

    def cst(self, blob, lay, name, p=128):
        o, w = lay.m[name]
        return blob[0:p, o:o + w]

    def load_ln(self, g_ap, b_ap, l):
        self.dma("sp", self.lnp[:, 0, :], g_ap[l:l + 1, :].broadcast_to([128, 1024]), r=[], w=["lnp0"], lane="lnp0")
        self.dma("sp", self.lnp[:, 1, :], b_ap[l:l + 1, :].broadcast_to([128, 1024]), r=[], w=["lnp1"], lane="lnp1")

    def layernorm(self, P, z, zkey, stat, skey):
        st6 = stat[0:P, 0:12]
        for hf in range(2):
            o6 = stat[0:P, 6 * hf:6 * hf + 6]
            zi = z[:, 512 * hf:512 * hf + 512]
            self.S.add("dve", lambda e, o6=o6, zi=zi: e.bn_stats(o6, zi), r=[zkey], w=[skey])
        mv = stat[0:P, 12:14]
        self.S.add("dve", lambda e: e.bn_aggr(mv, st6), r=[skey], w=[skey])
        ve = stat[0:P, 14:15]
        self.ts("dve", ve, stat[0:P, 13:14], LN_EPS, None, ALU.add, None, r=[skey], w=[skey])
        rstd = stat[0:P, 15:16]
        self.tt("pool", rstd, ve, self.cst(self.C0, C0_L, "nhalf", P), ALU.pow, r=[skey, "C0"], w=[skey])
        nmr = stat[0:P, 14:15]
        self.stt("dve", nmr, stat[0:P, 12:13], -1.0, rstd, ALU.mult, ALU.mult, r=[skey], w=[skey])
        self.act(z, z, AF.Identity, r=[zkey, skey], w=[zkey], bias=nmr, scale=rstd)
        self.tt("pool", z, z, self.lnp[0:P, 0, :], ALU.mult, r=[zkey, "lnp0"], w=[zkey])
        self.tt("pool", z, z, self.lnp[0:P, 1, :], ALU.add, r=[zkey, "lnp1"], w=[zkey])

    def to_XT(self, P, src_bf, skey, i, bank):
        c0 = 128 * i
        for c in range(8):
            self.tr(self.psb(bank, P, off=128 * c), bank, src_bf[:, 128 * c:128 * c + 128], self.identb[0:P, 0:P], r=[skey, "CCB"])
        src = self.psb(bank, 1024).rearrange("p (c t) -> p c t", c=8)[:, :, 0:P]
        self.cp("dve", self.XT[:, :, c0:c0 + P], src, r=[("ps", bank)], w=[("XT", i)])

    def rope_tm(self, P, src, H, d2, cos, sin, dst, rkeys, wkey, tmp, tkey):
        sv = src.rearrange("p (h t d) -> p h t d", h=H, t=2)
        dv = dst.rearrange("p (h t d) -> p h t d", h=H, t=2)
        lo, hi = sv[:, :, 0, :], sv[:, :, 1, :]
        cb = cos.unsqueeze(1).broadcast_to([P, H, d2])
        sb = sin.unsqueeze(1).broadcast_to([P, H, d2])
        t = [tmp[0:P, k, 0:H * d2].rearrange("p (h d) -> p h d", h=H) for k in range(4)]
        self.tt("dve", t[0], lo, cb, ALU.mult, r=rkeys, w=[(tkey, 0)])
        self.tt("dve", t[1], hi, sb, ALU.mult, r=rkeys, w=[(tkey, 1)])
        self.tt("dve", t[2], hi, cb, ALU.mult, r=rkeys, w=[(tkey, 2)])
        self.tt("dve", t[3], lo, sb, ALU.mult, r=rkeys, w=[(tkey, 3)])
        self.tt("pool", dv[:, :, 0, :], t[0], t[1], ALU.subtract, r=[(tkey, 0), (tkey, 1)], w=[wkey])
        self.tt("pool", dv[:, :, 1, :], t[2], t[3], ALU.add, r=[(tkey, 2), (tkey, 3)], w=[wkey])

    def proj_tm(self, P, i, W, wkey, col0, ncols, bank0):
        c0 = 128 * i
        done = 0
        b = bank0
        while done < ncols:
            n = min(512, ncols - done)
            self.pbegin(b)
            for d in range(8):
                self.mm(self.psf(b, n, P), b, self.XT[:, d, c0:c0 + P], W[:, d, col0 + done:col0 + done + n],
                        r=[("XT", i), wkey], p1=P, stop=(d == 7))
            done += n
            b += 1

    def build(self):
        nc, S, A, D = self.nc, self.S, self.A, self.D
        self.XT = A.alloc("XT", [128, 8, TT], BF16)
        self.CC = A.alloc("CC", [128, CC_L.n], F32)
        self.CCB = A.alloc("CCB", [128, CC_L.n], BF16)
        self.lnp = A.alloc("lnp", [128, 2, 1024], F32)
        self.memKT = A.alloc("memKT", [128, 2, 256], BF16)
        self.memV = A.alloc("memV", [128, 2, 256], BF16)
        self.dma("sp", self.CC, D["cc"], r=[], w=["CC"], lane="CC")
        self.cp("dve", self.CCB, self.CC, r=["CC"], w=["CCB"])
        self.identb = self.cst(self.CCB, CC_L, "ident")
        self.onesb = self.cst(self.CCB, CC_L, "ones")
        self.ones32b = self.cst(self.CCB, CC_L, "ones32")
        if self.stage <= -4:
            self.C0 = A.alloc("C0", [128, C0_L.n], F32)
            self.dma("sp", self.C0, D["c0"], r=[], w=["C0"], lane="C0")
            self.rt = [A.alloc("rt", [128, 128], F32) for _ in range(2)]
            self.rtn = 0
            m_ = A.mark()
            xin_ = A.alloc("xin_", [128, 1024], F32)
            xb_ = A.alloc("xb_", [128, 1024], BF16)
            for i in range(NT + 1):
                P = 128 if i < NT else TS
                src = D["xp"][128 * i:128 * i + 128, :] if i < NT else D["xs"]
                self.dma("sp", xin_[0:P, :], src, r=[], w=["xin_"], lane="xin_")
                self.cp("act", xb_[0:P, :], xin_[0:P, :], r=["xin_"], w=["xb_"])
                self.to_XT(P, xb_[0:P, :], "xb_", i, 6)
            S.barrier()
            A.release(m_)
            self.halo_kv()
            if self.stage != -4.05:
                self.layer1()
            S.barrier()
            S.emit(nc)
            return
        if self.stage <= -2:
            self.C0 = A.alloc("C0", [128, C0_L.n], F32)
            self.dma("sp", self.C0, D["c0"], r=[], w=["C0"], lane="C0")
            self.Sst = A.alloc("Sst", [128, 768], F32)
            self.Sstb = A.alloc("Sstb", [128, 768], BF16)
            self.rt = [A.alloc("rt", [128, 128], F32) for _ in range(2)]
            self.rtn = 0
            S.add("pool", lambda e: e.memset(self.Sst, 0.0), r=[], w=["Sst"])
            S.add("pool", lambda e: e.memset(self.Sstb, 0.0), r=[], w=["Sstb"])
            self.mem_kv(0)
            self.l0_pass2(1)
            S.barrier()
            S.emit(nc)
            return
        if self.stage < 0:
            self.C0 = A.alloc("C0", [128, C0_L.n], F32)
            self.dma("sp", self.C0, D["c0"], r=[], w=["C0"], lane="C0")
            self.mem_kv(0)
            S.barrier()
            S.emit(nc)
            return
        self.layer0()
        if self.stage > 3:
            S.barrier()
            self.layer1()
        S.barrier()
        S.emit(nc)

    def mem_kv(self, l, stat_bank=6):
        S, A, D = self.S, self.A, self.D
        st = self.stage
        m = A.mark()
        Wm = A.alloc("Wm", [128, 8, 512], BF16)
        memT = A.alloc("memT", [128, 8, 256], BF16)
        mtile = A.alloc("mtile", [128, 1024], F32)
        mtb = A.alloc("mtb", [128, 1024], BF16)
        mo = A.alloc("mo", [128, 512], F32)
        wm_v = D["w_mem"][l].rearrange("(c p) f -> p c f", p=128)
        self.dmas("pool", [(Wm[:, c, :], wm_v[:, c, :]) for c in range(8)], r=[], w=["Wm"], lane="Wm")
        for t in range(2):
            self.dma("sp", mtile, D["mem"][128 * t:128 * t + 128, :], r=[], w=["mtile"], lane="mtile")
            self.cp("act", mtb, mtile, r=["mtile"], w=["mtb"])
            if st == -1.1:
                continue
            for c in range(8):
                self.tr(self.psb(7, 128, off=128 * c), 7, mtb[:, 128 * c:128 * c + 128], self.identb, r=["mtb", "CCB"])
            self.cp("dve", memT[:, :, 128 * t:128 * t + 128], self.psb(7, 1024).rearrange("p (c t) -> p c t", c=2 * 4),
                    r=[("ps", 7)], w=["memT"])
        if st in (-1.1, -1.2):
            S.barrier(); A.release(m); return
        for t in range(2):
            self.pbegin(6)
            for d in range(8):
                self.mm(self.psf(6), 6, memT[:, d, 128 * t:128 * t + 128], Wm[:, d, :], r=["memT", "Wm"], stop=(d == 7))
            if st == -1.25:
                continue
            self.cp("dve", mo, self.psf(6), r=[("ps", 6)], w=["mo"])
            if st == -1.3:
                continue
            self.cp("dve", self.memV[:, t, :], self.psf(6, 256, off=256), r=[("ps", 6)], w=["memV"])
            if st == -1.4:
                continue
            if st == -1.51 and t == 1:
                continue
            if st == -1.52:
                self.dma("sp", D["dbg_sin"][:, 0:512], mo, r=["mo"], w=[], lane="mo")
                continue
            if st == -1.53:
                self.dma("pool", D["mkv"][l, 128 * t:128 * t + 128, :], mo, r=["mo"], w=[], lane="mo")
                continue
            self.dma("sp", D["mkv"][l, 128 * t:128 * t + 128, :], mo, r=["mo"], w=[], lane="mo")
        if st in (-1.25, -1.3, -1.4, -1.5, -1.51, -1.52, -1.53, -1.54):
            S.barrier(); A.release(m); return
        for c in range(2):
            self.pbegin(7)
            for d in range(8):
                self.mm(self.psf(7, 256), 7, Wm[:, d, 128 * c:128 * c + 128], memT[:, d, :], r=["memT", "Wm"], stop=(d == 7))
            self.cp("dve", self.memKT[:, c, :], self.psf(7, 256), r=[("ps", 7)], w=["memKT"])
        S.barrier()
        A.release(m)

    def mem_attn(self, P, QMT, qkey, KT, V, kvkeys, dst, dkey, n_q, PTm, rL, banks=(2, 3, 7)):
        bs0, bs1, bo = banks
        self.pbegin(bs0, bs1)
        for h in range(4):
            c, par = divmod(h, 2)
            bank = bs0 if par == 0 else bs1
            for mt in range(2):
                slot = (2 * c + mt) * n_q
                self.mm(self.psf(bank, n_q, off=slot), bank, KT[64 * par:64 * par + 64, c, 128 * mt:128 * mt + 128],
                        QMT[64 * par:64 * par + 64, c, :], r=[qkey] + kvkeys, stop=True)
        for k, bank in enumerate((bs0, bs1)):
            self.act(PTm[:, 4 * k * n_q:(4 * k + 4) * n_q], self.psf(bank, 4 * n_q), AF.Exp, r=[("ps", bank)], w=["PTm"], scale=0.125)
        self.pbegin(bo)
        for h in range(4):
            c, par = divmod(h, 2)
            for mt in range(2):
                ix = 4 * par + 2 * c + mt
                rhs = PTm[:, ix * n_q:(ix + 1) * n_q]
                self.mm(self.psf(bo, n_q, off=c * n_q)[64 * par:64 * par + 64, :], bo, V[:, mt, 64 * h:64 * h + 64], rhs,
                        r=["PTm"] + kvkeys, p0=64 * par, p1=64 * par + 64)
            for mt in range(2):
                ix = 4 * par + 2 * c + mt
                rhs = PTm[:, ix * n_q:(ix + 1) * n_q]
                self.mm(self.psf(bo, n_q, off=(2 + c) * n_q)[64 * par:64 * par + 64, :], bo, self.onesb, rhs,
                        r=["PTm", "CCB"], p0=64 * par, p1=64 * par + 64, stop=True)
        S = self.S
        rl = rL[:, 0:2 * n_q]
        S.add("dve", lambda e: e.reciprocal(rl, self.psf(bo, 2 * n_q, off=2 * n_q)), r=[("ps", bo)], w=["rL"])
        self.tt("dve", dst, self.psf(bo, 2 * n_q).rearrange("p (c q) -> p c q", c=2), rl.rearrange("p (c q) -> p c q", c=2),
                ALU.mult, r=[("ps", bo), "rL"], w=[dkey])

    def layer0(self):
        nc, S, A, D = self.nc, self.S, self.A, self.D
        C0 = self.C0 = A.alloc("C0", [128, C0_L.n], F32)
        self.dma("sp", C0, D["c0"], r=[], w=["C0"], lane="C0")
        Sst = self.Sst = A.alloc("Sst", [128, 768], F32)
        Sstb = self.Sstb = A.alloc("Sstb", [128, 768], BF16)
        self.rt = [A.alloc("rt", [128, 128], F32) for _ in range(2)]
        self.rtn = 0
        dec1 = self.cst(C0, C0_L, "dec1").rearrange("p (i h) -> p i h", i=32)
        wa_v = D["w_in_a"].rearrange("(c p) f -> p c f", p=128)

        m1 = A.mark()
        Wkv = A.alloc("Wkv0", [128, 8, 1536], BF16)
        self.dmas("pool", [(Wkv[:, c, :], wa_v[:, c, 768:2304]) for c in range(8)], r=[], w=["Wkv0"], lane="Wkv0")
        xin = [A.alloc("xin", [128, 1024], F32) for _ in range(2)]
        xb = A.alloc("xb", [128, 1024], BF16)
        Kb = A.alloc("Kb", [128, 768], BF16)
        Vb = A.alloc("Vb", [128, 768], BF16)
        tmp = A.alloc("ropetmp", [128, 4, 384], F32)
        self.pbegin(4, 5)
        NP1 = 2 * NT
        for i in range(NP1):
            xi = xin[i % 2]
            xk = ("xin", i % 2)
            self.dma("sp", xi, D["xpp"][128 * i:128 * i + 128, :], r=[], w=[xk], lane="xin%d" % (i % 2))
            rc, rs, rk = self.rope_load(D["rope0"], i, 64)
            self.cp("act", xb, xi, r=[xk], w=["xb"])
            ti = i % 2
            self.to_XT(128, xb, "xb", ti, 6)
            c0 = 128 * ti
            for (col0, b0) in ((0, 0), (768, 2)):
                for (off, n, b) in ((0, 512, b0), (512, 256, b0 + 1)):
                    self.pbegin(b)
                    for d in range(8):
                        self.mm(self.psf(b, n), b, self.XT[:, d, c0:c0 + 128], Wkv[:, d, col0 + off:col0 + off + n],
                                r=[("XT", ti), "Wkv0"], stop=(d == 7))
            self.rope_tm(128, self.PS[:, 0:768], 6, 64, rc, rs, Kb, [("ps", 0), ("ps", 1), rk], "Kb", tmp, "rtmp")
            self.tt("dve", Vb.rearrange("p (h v) -> p h v", h=6), self.PS[:, 1024:1792].rearrange("p (h v) -> p h v", h=6),
                    dec1[:, i, :].unsqueeze(2).broadcast_to([128, 6, 128]), ALU.mult, r=[("ps", 2), ("ps", 3), "C0"], w=["Vb"])
            for h in range(6):
                b = 4 if h < 4 else 5
                self.mm(self.psf(b, 128, off=128 * (h % 4)), b, Kb[:, 128 * h:128 * h + 128], Vb[:, 128 * h:128 * h + 128],
                        r=["Kb", "Vb"], stop=(i == NP1 - 1))
        self.cp("dve", Sst[:, 0:512], self.psf(4), r=[("ps", 4)], w=["Sst"])
        self.cp("dve", Sst[:, 512:768], self.psf(5, 256), r=[("ps", 5)], w=["Sst"])
        self.cp("dve", Sstb, Sst, r=["Sst"], w=["Sstb"])
        if self.debug:
            self.dma("sp", D["dbg_sin"], Sst, r=["Sst"], w=[], lane="dbg")
        if self.stage <= 1:
            return
        S.barrier()
        A.release(m1)
        self.mem_kv(0)
        for seg in range(2):
            self.l0_pass2(seg)
            if self.stage <= 2 and seg == 1:
                return
            self.ffn(0, seg == 1, False)
            if seg == 0:
                self.halo_kv()
            if self.stage <= 2.5 and seg == 0:
                return

    def rope_load(self, tab, idx, d2):
        k = self.rtn % 2
        self.rtn += 1
        rt = self.rt[k]
        self.dma("sp", rt[:, 0:2 * d2], tab[idx], r=[], w=[("rt", k)], lane="rt%d" % k)
        return rt[:, 0:d2], rt[:, d2:2 * d2], ("rt", k)

    def l0_pass2(self, seg):
        nc, S, A, D = self.nc, self.S, self.A, self.D
        C0 = self.C0
        Sst, Sstb = self.Sst, self.Sstb
        dq = self.cst(C0, C0_L, "dq").rearrange("p (v h) -> p v h", v=2)
        dk = self.cst(C0, C0_L, "dk").rearrange("p (v h) -> p v h", v=2)
        wa_v = D["w_in_a"].rearrange("(c p) f -> p c f", p=128)
        m2 = A.mark()
        Wa = A.alloc("Wa", [128, 8, 3328], BF16)
        self.dmas("pool", [(Wa[:, c, :], wa_v[:, c, :]) for c in range(8)], r=[], w=["Wa"], lane="Wa")
        Wo = A.alloc("Wo", [128, 8, 1024], BF16)
        wo_v = D["w_out"][0].rearrange("(c p) f -> p c f", p=128)
        self.dmas("pool", [(Wo[:, c, :], wo_v[:, c, :]) for c in range(8)], r=[], w=["Wo"], lane="Wo")
        wak = ["Wa"]
        self.load_ln(D["ln_mix_g"], D["ln_mix_b"], 0)
        xin = [A.alloc("xin", [128, 1024], F32) for _ in range(2)]
        xb = A.alloc("xb", [128, 1024], BF16)
        tmp = A.alloc("ropetmp", [128, 4, 384], F32)
        Qr = A.alloc("Qr", [128, 768], F32)
        Kr = A.alloc("Kr", [128, 768], F32)
        Qb = A.alloc("Qb", [128, 768], BF16)
        Kb = A.alloc("Kb", [128, 768], BF16)
        Vb = A.alloc("Vb", [128, 768], BF16)
        Gs = A.alloc("Gs", [128, 768], F32)
        QT = A.alloc("QT", [128, 6, 128], BF16)
        KT = A.alloc("KT", [128, 6, 128], BF16)
        PT = A.alloc("PT", [128, 6, 128], BF16)
        Of = Qr
        gst = A.alloc("gst", [128, 64], F32)
        mixb = A.alloc("mixb", [128, 768], BF16)
        mixT = A.alloc("mixT", [128, 8, 128], BF16)
        QMT = A.alloc("QMT", [128, 2, 128], BF16)
        PTm = A.alloc("PTm", [128, 1024], BF16)
        rL = A.alloc("rL", [128, 256], F32)
        stat = A.alloc("stat", [128, 16], F32)
        stmp = Kr
        causal = self.cst(C0, C0_L, "causal")
        mask_s = self.cst(C0, C0_L, "mask_s", 32)
        G128 = self.cst(C0, C0_L, "G128")
        G8 = self.cst(C0, C0_L, "G8")
        Stf = A.alloc("Stf", [128, 4, 768], F32)
        Stb = A.alloc("Stbf", [128, 4, 768], BF16)
        Qm = A.alloc("Qm", [128, 6, 4, 32], BF16)
        Kbm = A.alloc("Kbm", [32, 4, 768], BF16)
        mks = A.alloc("mks", [128, 2, 256], BF16)
        mvs = A.alloc("mvs", [128, 2, 256], BF16)
        ctile = A.alloc("ctile", [128, 512], F32)
        ctb = A.alloc("ctb", [128, 512], BF16)

        tl = range(NT + (1 if seg == 1 else 0))
        if self.stage <= -2:
            tl = [0] if self.stage > -3 else [NT]
        for i in tl:
            P = 128 if i < NT else TS
            v = 0 if i < NT else 1
            c0 = 128 * i
            xi = xin[i % 2]
            xk = ("xin", i % 2)
            xsrc = D["xp"] if seg == 1 else D["xpv"]
            src = xsrc[128 * i:128 * i + 128, :] if i < NT else D["xs"]
            self.dma("sp", xi[0:P, :], src, r=[], w=[xk], lane="xin%d" % (i % 2))
            rc, rs, rk = self.rope_load(D["rope0"], (32 + 16 * seg + i) if i < NT else 64, 64)
            rc, rs = rc[0:P, :], rs[0:P, :]
            self.cp("act", xb[0:P, :], xi[0:P, :], r=[xk], w=["xb"])
            self.to_XT(P, xb[0:P, :], "xb", i, 6)
            self.proj_tm(P, i, Wa, wak[0], 0, 768, 0)
            self.rope_tm(P, self.PS[0:P, 0:768], 6, 64, rc, rs, Qr[0:P, :], [("ps", 0), ("ps", 1), rk], "Qr", tmp, "rtmp")
            self.tt("pool", Qb[0:P, :].rearrange("p (h v) -> p h v", h=6), Qr[0:P, :].rearrange("p (h v) -> p h v", h=6),
                    dq[0:P, v, :].unsqueeze(2).broadcast_to([P, 6, 128]), ALU.mult, r=["Qr", "C0"], w=["Qb"])
            if self.stage < 0 and int(abs(self.stage) * 10 + 1e-6) - 10 * int(abs(self.stage)) == 1:
                break
            self.proj_tm(P, i, Wa, wak[0], 768, 768, 2)
            self.rope_tm(P, self.PS[0:P, 1024:1792], 6, 64, rc, rs, Kr[0:P, :], [("ps", 2), ("ps", 3), rk], "Kr", tmp, "rtmp")
            self.tt("pool", Kb[0:P, :].rearrange("p (h v) -> p h v", h=6), Kr[0:P, :].rearrange("p (h v) -> p h v", h=6),
                    dk[0:P, v, :].unsqueeze(2).broadcast_to([P, 6, 128]), ALU.mult, r=["Kr", "C0"], w=["Kb"])
            if self.stage < 0 and int(abs(self.stage) * 10 + 1e-6) - 10 * int(abs(self.stage)) == 2:
                break
            for (srcb, skey, dstT, dkey, bank) in ((Qb, "Qb", QT, "QT", 4), (Kb, "Kb", KT, "KT", 5)):
                for h in range(6):
                    self.tr(self.psb(bank, P, off=128 * h), bank, srcb[0:P, 128 * h:128 * h + 128], self.identb[0:P, 0:P], r=[skey, "CCB"])
                self.cp("dve", dstT[:, :, 0:P], self.psb(bank, 768).rearrange("p (h t) -> p h t", h=6)[:, :, 0:P], r=[("ps", bank)], w=[dkey])
            if self.stage < 0 and int(abs(self.stage) * 10 + 1e-6) - 10 * int(abs(self.stage)) == 3:
                break
            self.proj_tm(P, i, Wa, wak[0], 1536, 768, 0)
            self.cp("act", Vb[0:P, :], self.PS[0:P, 0:768], r=[("ps", 0), ("ps", 1)], w=["Vb"])
            self.proj_tm(P, i, Wa, wak[0], 2304, 768, 2)
            self.act(Gs[0:P, :], self.PS[0:P, 1024:1792], AF.Silu, r=[("ps", 2), ("ps", 3)], w=["Gs"])
            if self.stage < 0 and int(abs(self.stage) * 10 + 1e-6) - 10 * int(abs(self.stage)) == 4:
                break
            self.pbegin(7)
            for c in range(2):
                for d in range(8):
                    self.mm(self.psf(7, P, off=128 * c), 7, Wa[:, d, 3072 + 128 * c:3072 + 128 * c + 128], self.XT[:, d, c0:c0 + P],
                            r=[("XT", i)] + wak, stop=(d == 7))
            self.cp("act", QMT[:, :, 0:P], self.psf(7, 256).rearrange("p (c t) -> p c t", c=2)[:, :, 0:P], r=[("ps", 7)], w=["QMT"])
            if self.stage < 0 and int(abs(self.stage) * 10 + 1e-6) - 10 * int(abs(self.stage)) == 5:
                break
            if i < NT:
                self.pbegin(4, 5)
                for h in range(6):
                    b = 4 if h < 4 else 5
                    self.mm(self.psf(b, 128, off=128 * (h % 4)), b, KT[:, h, :], QT[:, h, :], r=["KT", "QT"], stop=True)
                self.tt("dve", PT, self.PS[:, 2048:2816].rearrange("p (h q) -> p h q", h=6),
                        causal.unsqueeze(1).broadcast_to([128, 6, 128]), ALU.mult, r=[("ps", 4), ("ps", 5), "C0"], w=["PT"])
                self.pbegin(0, 1)
                for h in range(6):
                    b = 0 if h < 4 else 1
                    o = self.psf(b, 128, off=128 * (h % 4))
                    self.mm(o, b, PT[:, h, :], Vb[:, 128 * h:128 * h + 128], r=["PT", "Vb"])
                    self.mm(o, b, QT[:, h, :], Sstb[:, 128 * h:128 * h + 128], r=["QT", "Sstb"], stop=True)
                self.pbegin(2, 3)
                for h in range(6):
                    b = 2 if h < 4 else 3
                    self.mm(self.psf(b, 128, off=128 * (h % 4)), b, Kb[:, 128 * h:128 * h + 128], Vb[:, 128 * h:128 * h + 128],
                            r=["Kb", "Vb"], stop=True)
                self.tt("dve", stmp, self.PS[:, 1024:1792], Sst, ALU.add, r=[("ps", 2), ("ps", 3), "Sst"], w=["Kr"])
                self.tt("pool", Sst, stmp, G128, ALU.mult, r=["Kr", "C0"], w=["Sst"])
                self.cp("pool", Sstb, Sst, r=["Sst"], w=["Sstb"])
                if i == NT - 1 and seg == 1:
                    self.dma("sp", D["srp"].rearrange("h d v -> d h v"), Sst.rearrange("p (h v) -> p h v", h=6), r=["Sst"], w=[], lane="srp")
            else:
                self.dma("sp", Stf.rearrange("p b (h v) -> p b h v", h=6), D["sret"].rearrange("b h d v -> d b h v"), r=[], w=["Stf"], lane="Stf")
                self.cp("act", Stb, Stf, r=["Stf"], w=["Stb"])
                self.pbegin(4)
                for h in range(6):
                    self.mm(self.psf(4, 32, 32, off=32 * h), 4, KT[:, h, 0:32], QT[:, h, 0:32], r=["KT", "QT"], p1=32, stop=True)
                self.tt("dve", PT[0:32, :, 0:32], self.psf(4, 192, 32).rearrange("p (h q) -> p h q", h=6),
                        mask_s.unsqueeze(1).broadcast_to([32, 6, 32]), ALU.mult, r=[("ps", 4), "C0"], w=["PT"])
                blk = self.cst(C0, C0_L, "blk").rearrange("p (b q) -> p b q", b=4)
                self.tt("dve", Qm, QT[:, :, 0:32].unsqueeze(2).broadcast_to([128, 6, 4, 32]),
                        blk.unsqueeze(1).broadcast_to([128, 6, 4, 32]), ALU.mult, r=["QT", "C0"], w=["Qm"])
                rowm = self.cst(C0, C0_L, "rowm", 32)
                self.tt("dve", Kbm, Kb[0:32, :].unsqueeze(1).broadcast_to([32, 4, 768]),
                        rowm.unsqueeze(2).broadcast_to([32, 4, 768]), ALU.mult, r=["Kb", "C0"], w=["Kbm"])
                self.pbegin(0, 1)
                for h in range(6):
                    b = 0 if h < 4 else 1
                    o = self.psf(b, 128, 32, off=128 * (h % 4))
                    self.mm(o, b, PT[0:32, h, 0:32], Vb[0:32, 128 * h:128 * h + 128], r=["PT", "Vb"], p1=32)
                    for bl in range(4):
                        self.mm(o, b, Qm[:, h, bl, :], Stb[:, bl, 128 * h:128 * h + 128], r=["Qm", "Stb"], p1=32, stop=(bl == 3))
                for bl in range(4):
                    self.pbegin(2, 3)
                    for h in range(6):
                        b = 2 if h < 4 else 3
                        self.mm(self.psf(b, 128, off=128 * (h % 4)), b, Kbm[:, bl, 128 * h:128 * h + 128], Vb[0:32, 128 * h:128 * h + 128],
                                r=["Kbm", "Vb"], stop=True)
                    self.tt("dve", stmp, self.PS[:, 1024:1792], Stf[:, bl, :], ALU.add, r=[("ps", 2), ("ps", 3), "Stf"], w=["Kr"])
                    self.tt("pool", Stf[:, bl, :], stmp, G8, ALU.mult, r=["Kr", "C0"], w=["Stf"])
                self.dma("sp", D["srs"].rearrange("b h d v -> d b h v"), Stf.rearrange("p b (h v) -> p b h v", h=6), r=["Stf"], w=[], lane="srs")
            if self.stage < 0 and int(abs(self.stage) * 10 + 1e-6) - 10 * int(abs(self.stage)) == 6:
                break
            self.cp("act", Of[0:P, :], self.PS[0:P, 0:768], r=[("ps", 0), ("ps", 1)], w=["Qr"])
            for h in range(6):
                o6 = gst[0:P, 6 * h:6 * h + 6]
                oi = Of[0:P, 128 * h:128 * h + 128]
                S.add("dve", lambda e, o6=o6, oi=oi: e.bn_stats(o6, oi), r=["Qr"], w=["gst"])
                mv = gst[0:P, 36 + 2 * h:38 + 2 * h]
                S.add("dve", lambda e, o6=o6, mv=mv: e.bn_aggr(mv, o6), r=["gst"], w=["gst"])
            mvv = gst[0:P, 36:48].rearrange("p (h t) -> p h t", t=2)
            ve = gst[0:P, 48:54]
            self.ts("dve", ve, mvv[:, :, 1], LN_EPS, None, ALU.add, None, r=["gst"], w=["gst"])
            rs = gst[0:P, 54:60]
            self.tt("pool", rs, ve, self.cst(C0, C0_L, "nhalf", P).broadcast_to([P, 6]), ALU.pow, r=["gst", "C0"], w=["gst"])
            Ov = Of[0:P, :].rearrange("p (h v) -> p h v", h=6)
            self.tt("dve", Ov, Ov, mvv[:, :, 0].unsqueeze(2).broadcast_to([P, 6, 128]), ALU.subtract, r=["Qr", "gst"], w=["Qr"])
            self.tt("dve", Ov, Ov, rs.unsqueeze(2).broadcast_to([P, 6, 128]), ALU.mult, r=["Qr", "gst"], w=["Qr"])
            self.tt("pool", mixb[0:P, :], Of[0:P, :], Gs[0:P, :], ALU.mult, r=["Qr", "Gs"], w=["mixb"])
            for h in range(6):
                self.tr(self.psb(4, P, off=128 * h), 4, mixb[0:P, 128 * h:128 * h + 128], self.identb[0:P, 0:P], r=["mixb", "CCB"])
            self.cp("act", mixT[:, 0:6, 0:P], self.psb(4, 768).rearrange("p (h t) -> p h t", h=6)[:, :, 0:P], r=[("ps", 4)], w=["mixT"])
            if self.stage < 0 and int(abs(self.stage) * 10 + 1e-6) - 10 * int(abs(self.stage)) == 7:
                break
            if i < NT:
                self.mem_attn(128, QMT, "QMT", self.memKT, self.memV, ["memKT", "memV"], mixT[:, 6:8, :], "mixT", 128, PTm, rL)
            else:
                for bl in range(4):
                    for t in range(2):
                        self.dma("sp", ctile, D["cmkv"][0, bl, 128 * t:128 * t + 128, :], r=[], w=["ctile"], lane="ctile")
                        self.cp("act", ctb, ctile, r=["ctile"], w=["ctb"])
                        for c in range(2):
                            self.tr(self.psb(5, 128, off=128 * c), 5, ctb[:, 128 * c:128 * c + 128], self.identb, r=["ctb", "CCB"])
                        self.cp("dve", mks[:, :, 128 * t:128 * t + 128], self.psb(5, 256).rearrange("p (c t) -> p c t", c=2), r=[("ps", 5)], w=["mks"])
                        self.cp("pool", mvs[:, t, :], ctb[:, 256:512], r=["ctb"], w=["mvs"])
                    self.mem_attn(128, QMT[:, :, 8 * bl:8 * bl + 8], "QMT", mks, mvs, ["mks", "mvs"], mixT[:, 6:8, 8 * bl:8 * bl + 8], "mixT", 8, PTm, rL)
            if self.debug:
                for c in range(8):
                    pass
            if self.stage < 0 and int(abs(self.stage) * 10 + 1e-6) - 10 * int(abs(self.stage)) == 8:
                break
            self.pbegin(0, 1)
            for hf in range(2):
                for c in range(8):
                    self.mm(self.psf(hf, 512, P), hf, mixT[:, c, 0:P], Wo[:, c, 512 * hf:512 * hf + 512], r=["mixT", "Wo"], p1=P, stop=(c == 7))
            z = xi[0:P, :]
            for hf in range(2):
                self.stt("dve", z[:, 512 * hf:512 * hf + 512], z[:, 512 * hf:512 * hf + 512], ALPHA, self.psf(hf, 512, P), ALU.mult, ALU.add,
                         r=[xk, ("ps", hf)], w=[xk])
            self.layernorm(P, z, xk, stat, "stat")
            self.dma("sp", D["x1s"][c0:c0 + P, :], z, r=[xk], w=[("x1s", i)], lane="x1o%d" % (i % 2))
            self.cp("act", xb[0:P, :], z, r=[xk], w=["xb"])
            self.to_XT(P, xb[0:P, :], "xb", i, 6)
        S.barrier()
        A.release(m2)


    def ffn(self, l, with_sample, final):
        nc, S, A, D = self.nc, self.S, self.A, self.D
        m = A.mark()
        hT = A.alloc("hT", [128, NF, 1056], BF16)
        Wi = [A.alloc("Wi", [128, 8, 256], BF16) for _ in range(3)]
        WoR = A.alloc("WoR", [128, NF, 1024], BF16)
        sg = [A.alloc("sg", [128, 512], BF16) for _ in range(3)]
        x1t = [A.alloc("x1t", [128, 1024], F32) for _ in range(2)]
        xb2 = A.alloc("xb2", [128, 1024], BF16)
        stat = A.alloc("stat2", [128, 16], F32)
        self.load_ln(D["ln_ffn_g"], D["ln_ffn_b"], l)
        wi_v = D["w_ffn_in"][l].rearrange("(c p) f -> p c f", p=128)
        wo_v = D["w_ffn_out"][l]
        nw = 0
        nwo = 0
        nx = 0
        for half in range(2):
            tiles = list(range(8 * half, 8 * half + 8))
            if half == 1 and with_sample:
                tiles.append(NT)
            col0 = 1024 * half
            pieces = [(0, 512), (512, 512)] + ([(1024, TS)] if (half == 1 and with_sample) else [])
            k = 0
            for f in range(NF):
                wi = Wi[nw % 3]
                wk = ("Wi", nw % 3)
                self.dmas("pool", [(wi[:, :, 0:128], wi_v[:, :, 128 * f:128 * f + 128]),
                                   (wi[:, :, 128:256], wi_v[:, :, FF + 128 * f:FF + 128 * f + 128])], r=[], w=[wk], lane="Wi%d" % (nw % 3))
                nw += 1
                if half == 0 and f == 2:
                    self.dmas("pool", [(WoR[:, ff, :], wo_v[128 * ff:128 * ff + 128, :]) for ff in range(NF)], r=[], w=["WoR"], lane="WoR")
                for (po, pn) in pieces:
                    bg, bu = ((0, 1), (2, 3), (4, 5))[k % 3]
                    sgk = sg[k % 3]
                    k += 1
                    xk = [("XT", (col0 + po) // 128 + q) for q in range((pn + 127) // 128)]
                    for (bank, wc) in ((bg, 0), (bu, 128)):
                        self.pbegin(bank)
                        for d in range(8):
                            self.mm(self.psf(bank, pn), bank, wi[:, d, wc:wc + 128], self.XT[:, d, col0 + po:col0 + po + pn],
                                    r=xk + [wk], stop=(d == 7))
                    self.act(sgk[:, 0:pn], self.psf(bg, pn), AF.Silu, r=[("ps", bg)], w=[("sg", k % 3)])
                    self.tt("dve", hT[:, f, po:po + pn], self.psf(bu, pn), sgk[:, 0:pn], ALU.mult, r=[("ps", bu), ("sg", k % 3)], w=[("hT", f)])
            groups = [tiles[q:q + 3] for q in range(0, len(tiles), 3)]
            for grp in groups:
                self.pbegin(*range(2 * len(grp)))
                for f in range(NF):
                    wo = WoR[:, f, :]
                    wok = "WoR"
                    for kk, t in enumerate(grp):
                        P = 128 if t < NT else TS
                        lc = (t - 8 * half) * 128
                        for hf in range(2):
                            self.mm(self.psf(2 * kk + hf, 512, P), 2 * kk + hf, hT[:, f, lc:lc + P], wo[:, 512 * hf:512 * hf + 512],
                                    r=[("hT", f), wok], p1=P, stop=(f == NF - 1))
                for kk, t in enumerate(grp):
                    P = 128 if t < NT else TS
                    c0 = 128 * t
                    xt = x1t[nx % 2]
                    xk_ = ("x1t", nx % 2)
                    lane = "x1t%d" % (nx % 2)
                    nx += 1
                    z = xt[0:P, :]
                    self.dma("sp", z, D["x1s"][c0:c0 + P, :], r=[("x1s", t)], w=[xk_], lane=lane)
                    for hf in range(2):
                        self.stt("dve", z[:, 512 * hf:512 * hf + 512], z[:, 512 * hf:512 * hf + 512], ALPHA, self.psf(2 * kk + hf, 512, P),
                                 ALU.mult, ALU.add, r=[xk_, ("ps", 2 * kk + hf)], w=[xk_])
                    self.layernorm(P, z, xk_, stat, "stat2")
                    if final:
                        dst = D["yp"][c0:c0 + P, :] if t < NT else D["ys"]
                        self.dma("sp", dst, z, r=[xk_], w=[], lane=lane + "o")
                    else:
                        self.dma("sp", D["x2s"][c0:c0 + P, :], z, r=[xk_], w=[("x2s", t)], lane=lane + "o")
                        self.cp("act", xb2[0:P, :], z, r=[xk_], w=["xb2"])
                        self.to_XT(P, xb2[0:P, :], "xb2", t, 6)
        S.barrier()
        A.release(m)


    def k_tile(self, P, i, rope_idx, Wg, KTdst, dkey, kout=None):
        c0 = 128 * i
        self.pbegin(0)
        for d in range(8):
            self.mm(self.psf(0, 256, P), 0, self.XT[:, d, c0:c0 + P], Wg[:, d, 0:256], r=[("XT", i), "Wg"], p1=P, stop=(d == 7))
        rc, rs, rk = self.rope_load(self.D["rope1"], rope_idx, 32)
        Kr = self.Kr1[self.kn % 2]
        kk = ("Kr1", self.kn % 2)
        lane = "Kr1%d" % (self.kn % 2)
        self.kn += 1
        self.rope_tm(P, self.psf(0, 256, P), 4, 32, rc[0:P, :], rs[0:P, :], Kr[0:P, :], [("ps", 0), rk], kk, self.tmp1, "rtmp")
        if kout is not None:
            self.dma("sp", kout, Kr[0:P, :], r=[kk], w=[], lane=lane)
        self.cp("act", self.Kb1[0:P, :], Kr[0:P, :], r=[kk], w=["Kb1"])
        for hp in range(2):
            self.tr(self.psb(6, P, off=128 * hp), 6, self.Kb1[0:P, 128 * hp:128 * hp + 128], self.identb[0:P, 0:P], r=["Kb1", "CCB"])
        self.cp("dve", KTdst, self.psb(6, 256).rearrange("p (c t) -> p c t", c=2)[:, :, 0:P], r=[("ps", 6)], w=[dkey])

    def v_block(self, P, cols, xkeys, Wg, Vdst, dkey, vout=None):
        self.pbegin(1)
        for d in range(8):
            self.mm(self.psf(1, 256, P), 1, self.XT[:, d, cols], Wg[:, d, 256:512], r=xkeys + ["Wg"], p1=P, stop=(d == 7))
        self.cp("dve", Vdst, self.psf(1, 256, P), r=[("ps", 1)], w=[dkey])
        if vout is not None:
            Vf = self.Vf1[self.vn % 2]
            vk = ("Vf1", self.vn % 2)
            lane = "Vf1%d" % (self.vn % 2)
            self.vn += 1
            self.cp("dve", Vf[0:P, :], self.psf(1, 256, P), r=[("ps", 1)], w=[vk])
            self.dma("sp", vout, Vf[0:P, :], r=[vk], w=[], lane=lane)

    def l1_alloc_kv(self):
        A = self.A
        self.Kr1 = [A.alloc("Kr1", [128, 256], F32) for _ in range(2)]
        self.Vf1 = [A.alloc("Vf1", [128, 256], F32) for _ in range(2)]
        self.Kb1 = A.alloc("Kb1", [128, 256], BF16)
        self.Vnat = A.alloc("Vnat", [128, 256], BF16)
        self.tmp1 = A.alloc("tmp1", [128, 4, 384], F32)
        self.Wg = A.alloc("Wg", [128, 8, 512], BF16)
        self.kn = 0
        self.vn = 0

    def load_wg(self, g):
        wv = self.D["w_kv"].rearrange("(c p) f -> p c f", p=128)
        self.dmas("pool", [(self.Wg[:, :, 0:256], wv[:, :, 256 * g:256 * g + 256]),
                           (self.Wg[:, :, 256:512], wv[:, :, 768 + 256 * g:768 + 256 * g + 256])], r=[], w=["Wg"], lane="Wg")

    def halo_kv(self):
        S, A, D = self.S, self.A, self.D
        m = A.mark()
        self.l1_alloc_kv()
        KTh = A.alloc("KTh", [128, 2, 2048], BF16)
        Vh = A.alloc("Vh", [128, 16, 256], BF16)
        for g in range(3):
            dil, W = DILS[g], WINS[g]
            self.load_wg(g)
            nt = W // 128
            for q in range(nt):
                i = NT - nt + q
                self.k_tile(128, i, i, self.Wg, KTh[:, :, 128 * q:128 * q + 128], "KTh")
            self.dma("sp", D["hK%d" % g], KTh[:, :, 0:W], r=["KTh"], w=["hK"], lane="hK")
            cl = NT // dil - 1
            for r in range(dil):
                st = 128 * dil * cl + r
                cols = slice(st, st + dil * 127 + 1, dil)
                xkeys = [("XT", dil * cl + q) for q in range(dil)]
                self.v_block(128, cols, xkeys, self.Wg, Vh[:, r, :], "Vh")
            self.dma("sp", D["hV%d" % g], Vh[:, 0:dil, :], r=["Vh"], w=["hV"], lane="hV")
        S.barrier()
        A.release(m)

    def b1_tile(self, P, i, lhs, lkey, Wo, xi, xk, lane, stat):
        D = self.D
        c0 = 128 * i
        self.pbegin(0, 1)
        for hf in range(2):
            for c in range(8):
                self.mm(self.psf(hf, 512, P), hf, lhs(c), Wo[:, c, 512 * hf:512 * hf + 512], r=[lkey, "Wo"], p1=P, stop=(c == 7))
        z = xi[0:P, :]
        for hf in range(2):
            self.stt("dve", z[:, 512 * hf:512 * hf + 512], z[:, 512 * hf:512 * hf + 512], ALPHA, self.psf(hf, 512, P), ALU.mult, ALU.add,
                     r=[xk, ("ps", hf)], w=[xk])
        self.layernorm(P, z, xk, stat, "stat")
        self.dma("sp", D["x1s"][c0:c0 + P, :], z, r=[xk], w=[("x1s", i)], lane=lane)
        self.cp("act", self.xb1[0:P, :], z, r=[xk], w=["xb1"])
        self.to_XT(P, self.xb1[0:P, :], "xb1", i, 6)

    def layer1(self):
        nc, S, A, D = self.nc, self.S, self.A, self.D
        ml = A.mark()
        C1 = self.C1 = A.alloc("C1", [128, C1_L.n], F32)
        C1B = A.alloc("C1B", [128, C1_L.n], BF16)
        self.dma("sp", C1, D["c1"], r=[], w=["C1"], lane="C1")
        self.cp("dve", C1B, C1, r=["C1"], w=["C1B"])
        self.mem_kv(1)
        mixT = A.alloc("mixT1", [128, 8, TT], BF16)
        Ltot = A.alloc("Ltot", [128, 2, TT], F32)
        rLs = A.alloc("rLs", [128, 256], F32)
        PTm = A.alloc("PTm1", [128, 1024], BF16)
        m = A.mark()
        Wb = A.alloc("Wb", [128, 8, 1024], BF16)
        wb_v = D["w_in_b"].rearrange("(c p) f -> p c f", p=128)
        self.dmas("pool", [(Wb[:, c, :], wb_v[:, c, :]) for c in range(8)], r=[], w=["Wb"], lane="Wb")
        Qr = A.alloc("Qr1", [128, 768], F32)
        Qb = A.alloc("Qb1", [128, 768], BF16)
        tmpq = A.alloc("tmpq", [128, 4, 384], F32)
        for i in range(NT + 1):
            P = 128 if i < NT else TS
            c0 = 128 * i
            self.proj_tm(P, i, Wb, "Wb", 0, 768, 0)
            rc, rs, rk = self.rope_load(D["rope1"], (16 + i) if i < NT else 32, 32)
            self.rope_tm(P, self.PS[0:P, 0:768], 12, 32, rc[0:P, :], rs[0:P, :], Qr[0:P, :], [("ps", 0), ("ps", 1), rk], "Qr1", tmpq, "rtmp")
            self.cp("pool", Qb[0:P, :], Qr[0:P, :], r=["Qr1"], w=["Qb1"])
            for h in range(6):
                self.tr(self.psb(6, P, off=128 * h), 6, Qb[0:P, 128 * h:128 * h + 128], self.identb[0:P, 0:P], r=["Qb1", "CCB"])
            self.cp("act", mixT[:, 0:6, c0:c0 + P], self.psb(6, 768).rearrange("p (h t) -> p h t", h=6)[:, :, 0:P], r=[("ps", 6)], w=[("mixT", i)])
            self.pbegin(7)
            for c in range(2):
                for d in range(8):
                    self.mm(self.psf(7, P, off=128 * c), 7, Wb[:, d, 768 + 128 * c:768 + 128 * c + 128], self.XT[:, d, c0:c0 + P],
                            r=[("XT", i), "Wb"], stop=(d == 7))
            self.cp("dve", mixT[:, 6:8, c0:c0 + P], self.psf(7, 256).rearrange("p (c t) -> p c t", c=2)[:, :, 0:P], r=[("ps", 7)], w=[("mixm", i)])
        if self.stage == -4.1:
            return
        mks = A.alloc("mks1", [128, 2, 256], BF16)
        mvs = A.alloc("mvs1", [128, 2, 256], BF16)
        ctile = A.alloc("ctile1", [128, 512], F32)
        ctb = A.alloc("ctb1", [128, 512], BF16)
        for i in range(NT):
            c0 = 128 * i
            self.mem_attn(128, mixT[:, 6:8, c0:c0 + 128], ("mixm", i), self.memKT, self.memV, ["memKT", "memV"], mixT[:, 6:8, c0:c0 + 128],
                          ("mixm", i), 128, PTm, rLs)
        for bl in range(4):
            for t in range(2):
                self.dma("sp", ctile, D["cmkv"][1, bl, 128 * t:128 * t + 128, :], r=[], w=["ctile"], lane="ctile")
                self.cp("act", ctb, ctile, r=["ctile"], w=["ctb"])
                for c in range(2):
                    self.tr(self.psb(5, 128, off=128 * c), 5, ctb[:, 128 * c:128 * c + 128], self.identb, r=["ctb", "CCB"])
                self.cp("dve", mks[:, :, 128 * t:128 * t + 128], self.psb(5, 256).rearrange("p (c t) -> p c t", c=2), r=[("ps", 5)], w=["mks"])
                self.cp("pool", mvs[:, t, :], ctb[:, 256:512], r=["ctb"], w=["mvs"])
            cs = T + 8 * bl
            self.mem_attn(128, mixT[:, 6:8, cs:cs + 8], ("mixm", NT), mks, mvs, ["mks", "mvs"], mixT[:, 6:8, cs:cs + 8], ("mixm", NT), 8, PTm, rLs)
        S.barrier()
        A.release(m)
        if self.stage == -4.2:
            return
        m = A.mark()
        self.l1_alloc_kv()
        PTa = [A.alloc("PTa", [128, 544], BF16) for _ in range(4)]
        ctile = A.alloc("ctile3", [128, 512], F32)
        ctb = A.alloc("ctb3", [128, 512], BF16)
        KTn = A.alloc("KTn", [128, 2, 32], BF16)
        Vns = A.alloc("Vns", [128, 256], BF16)
        S.add("pool", lambda e: e.memset(Vns, 0.0), r=[], w=["Vns"])
        for pt_ in PTa:
            S.add("pool", lambda e, pt_=pt_: e.memset(pt_, 0.0), r=[], w=[("PTa", PTa.index(pt_))])
        sb_i = 0
        ob_i = 0
        pa_i = 0
        first = True
        for g in (2, 1, 0):
            dil, W = DILS[g], WINS[g]
            mg = A.mark()
            KT = A.alloc("KTg", [128, 2, W + T], BF16)
            Vb = A.alloc("Vbg", [128, dil + NT, 256], BF16)
            KTs = A.alloc("KTs", [128, 2, W], BF16)
            Vs = A.alloc("Vs", [128, W // 128, 256], BF16)
            self.load_wg(g)
            self.dma("sp", KT[:, :, 0:W], D["hK%d" % g], r=[], w=["KTh"], lane="hKl")
            self.dma("sp", Vb[:, 0:dil, :], D["hV%d" % g], r=[], w=["Vbh"], lane="hVl")
            nlast = W // 128
            for i in range(NT):
                kout = None
                if i >= NT - nlast:
                    q = i - (NT - nlast)
                    kout = D["wp%d" % g][128 * q:128 * q + 128, 0:256]
                self.k_tile(128, i, 16 + i, self.Wg, KT[:, :, W + 128 * i:W + 128 * i + 128], ("KT", i), kout)
            if self.stage == -4.21:
                return
            nc_ = NT // dil
            for c in range(nc_):
                for r in range(dil):
                    st = 128 * dil * c + r
                    cols = slice(st, st + dil * 127 + 1, dil)
                    xkeys = [("XT", dil * c + q) for q in range(dil)]
                    self.v_block(128, cols, xkeys, self.Wg, Vb[:, dil + c * dil + r, :], ("Vb", dil + c * dil + r), None)
            for q in range(nlast):
                i = NT - nlast + q
                self.v_block(128, slice(128 * i, 128 * i + 128), [("XT", i)], self.Wg, self.Vnat, "Vnat", D["wp%d" % g][128 * q:128 * q + 128, 256:512])
            if self.stage == -4.22:
                return
            self.k_tile(TS, NT, 32, self.Wg, KTn, "KTn", None)
            Krs = self.Kr1[(self.kn - 1) % 2]
            krk = ("Kr1", (self.kn - 1) % 2)
            self.v_block(TS, slice(T, T + TS), [("XT", NT)], self.Wg, Vns[0:TS, :], "Vns", None)
            Vfs = self.Vf1[self.vn % 2]
            vfk = ("Vf1", self.vn % 2)
            self.vn += 1
            self.cp("dve", Vfs[0:TS, :], self.psf(1, 256, TS), r=[("ps", 1)], w=[vfk])
            if self.stage == -4.23:
                return
            for bl in range(4):
                self.dma("sp", D["ws%d" % g][bl, W - 8:W, 0:256], Krs[8 * bl:8 * bl + 8, :], r=[krk], w=[], lane="wsk")
                self.dma("sp", D["ws%d" % g][bl, W - 8:W, 256:512], Vfs[8 * bl:8 * bl + 8, :], r=[vfk], w=[], lane="wsv")
                self.dma("pool", D["ws%d" % g][bl, 0:W - 8, :], D["cw%d" % g][bl, 8:W, :], r=[], w=[], lane="wsc")
            if self.stage == -4.3:
                return
            mb4 = self.cst(C1B, C1_L, "mb4")
            mb4h = self.cst(C1B, C1_L, "mb4h")
            for c in range(nc_):
                for r in range(dil):
                    st = 128 * dil * c + r
                    qcols = slice(st, st + dil * 127 + 1, dil)
                    kcur = slice(W + st, W + st + dil * 127 + 1, dil)
                    kprev = slice(st, st + dil * 127 + 1, dil)
                    icur = dil + c * dil + r
                    iprev = icur - dil
                    kkeys = [("KT", dil * c + q) for q in range(dil)] + ([("KT", dil * (c - 1) + q) for q in range(dil)] if c > 0 else ["KTh"])
                    vkeys = [("Vb", icur), (("Vb", iprev) if c > 0 else "Vbh")]
                    qks = [("mq", g, 0, r, c), ("mq", g, 1, r, c)]
                    bo = 4 + (ob_i % 2)
                    ob_i += 1
                    bx = 2 * (sb_i % 2)
                    sb_i += 1
                    pts = []
                    for par in range(2):
                        bank = bx + par
                        ps_ = slice(64 * par, 64 * par + 64)
                        self.pbegin(bank)
                        for hp in range(2):
                            for half, kc in enumerate((kprev, kcur)):
                                self.mm(self.psf(bank, 128, off=(2 * hp + half) * 128), bank, KT[ps_, hp, kc], mixT[ps_, 2 * g + hp, qcols],
                                        r=kkeys + qks, stop=(hp == 1 and half == 1))
                        pt = PTa[pa_i % 4]
                        pk = ("PTa", pa_i % 4)
                        pa_i += 1
                        self.act(pt[:, 0:512], self.psf(bank, 512), AF.Exp, r=[("ps", bank)], w=[pk], scale=0.125)
                        self.tt("pool", pt[:, 0:512], pt[:, 0:512], (mb4h if c == 0 else mb4), ALU.mult, r=[pk, "C1B"], w=[pk])
                        pts.append((pt, pk))
                    self.pbegin(bo)
                    for hp in range(2):
                        for par in range(2):
                            h = 2 * hp + par
                            pt, pk = pts[par]
                            for half, idx in enumerate((iprev, icur)):
                                self.mm(self.psf(bo, 128, off=128 * hp)[64 * par:64 * par + 64, :], bo, Vb[:, idx, 64 * h:64 * h + 64],
                                        pt[:, (2 * hp + half) * 128:(2 * hp + half) * 128 + 128], r=[pk] + vkeys, p0=64 * par, p1=64 * par + 64)
                            for half in range(2):
                                self.mm(self.psf(bo, 128, off=128 * (2 + hp))[64 * par:64 * par + 64, :], bo, self.onesb,
                                        pt[:, (2 * hp + half) * 128:(2 * hp + half) * 128 + 128], r=[pk, "CCB"], p0=64 * par, p1=64 * par + 64,
                                        stop=(hp == 1 and par == 1 and half == 1))
                    self.cp("dve", mixT[:, 2 * g:2 * g + 2, qcols], self.psf(bo, 256).rearrange("p (c q) -> p c q", c=2), r=[("ps", bo)], w=qks)
                    lsrc = self.psf(bo, 256, off=256).rearrange("p (c q) -> p c q", c=2)
                    if first:
                        self.cp("dve", Ltot[:, :, qcols], lsrc, r=[("ps", bo)], w=[("L", r, c)])
                    else:
                        self.tt("dve", Ltot[:, :, qcols], Ltot[:, :, qcols], lsrc, ALU.add, r=[("ps", bo), ("L", r, c)], w=[("L", r, c)])
            if self.stage == -4.4:
                return
            nm = W // 128
            sm = self.cst(C1B, C1_L, "sm%d" % g)
            smn = self.cst(C1B, C1_L, "smn%d" % g).rearrange("p (b x) -> p b x", b=4)
            for bl in range(4):
                for mt in range(nm):
                    self.dma("sp", ctile, D["cw%d" % g][bl, 128 * mt:128 * mt + 128, :], r=[], w=["ctile"], lane="ctile")
                    self.cp("act", ctb, ctile, r=["ctile"], w=["ctb"])
                    for cc_ in range(2):
                        self.tr(self.psb(6, 128, off=128 * cc_), 6, ctb[:, 128 * cc_:128 * cc_ + 128], self.identb, r=["ctb", "CCB"])
                    self.cp("dve", KTs[:, :, 128 * mt:128 * mt + 128], self.psb(6, 256).rearrange("p (c t) -> p c t", c=2), r=[("ps", 6)], w=["KTs"])
                    self.cp("pool", Vs[:, mt, :], ctb[:, 256:512], r=["ctb"], w=["Vs"])
                cs = T + 8 * bl
                pts = []
                for par in range(2):
                    ps_ = slice(64 * par, 64 * par + 64)
                    bS, bN = 2 + par, par
                    self.pbegin(bS, bN)
                    for mt in range(nm):
                        for hp in range(2):
                            self.mm(self.psf(bS, 8, off=16 * mt + 8 * hp), bS, KTs[ps_, hp, 128 * mt:128 * mt + 128], mixT[ps_, 2 * g + hp, cs:cs + 8],
                                    r=["KTs", ("mqs", g)], stop=(mt == nm - 1 and hp == 1))
                    for hp in range(2):
                        self.mm(self.psf(bN, 8, TS, off=8 * hp), bN, KTn[ps_, hp, :], mixT[ps_, 2 * g + hp, cs:cs + 8], r=["KTn", ("mqs", g)], p1=TS, stop=(hp == 1))
                    pt = PTa[pa_i % 4]
                    pk = ("PTa", pa_i % 4)
                    pa_i += 1
                    self.act(pt[:, 0:16 * nm], self.psf(bS, 16 * nm), AF.Exp, r=[("ps", bS)], w=[pk], scale=0.125)
                    self.act(pt[0:TS, 512:528], self.psf(bN, 16, TS), AF.Exp, r=[("ps", bN)], w=[pk], scale=0.125)
                    self.tt("pool", pt[:, 0:16 * nm], pt[:, 0:16 * nm], sm, ALU.mult, r=[pk, "C1B"], w=[pk])
                    self.tt("pool", pt[0:TS, 512:528], pt[0:TS, 512:528], smn[0:TS, bl, :], ALU.mult, r=[pk, "C1B"], w=[pk])
                    pts.append((pt, pk))
                bo = 4 + (ob_i % 2)
                ob_i += 1
                self.pbegin(bo)
                for h in range(4):
                    hp, par = divmod(h, 2)
                    pt, pk = pts[par]
                    for (kind, off) in (("o", 8 * hp), ("l", 16 + 8 * hp)):
                        o = self.psf(bo, 8, off=off)[64 * par:64 * par + 64, :]
                        for mt in range(nm):
                            lhs = Vs[:, mt, 64 * h:64 * h + 64] if kind == "o" else self.onesb
                            self.mm(o, bo, lhs, pt[:, 16 * mt + 8 * hp:16 * mt + 8 * hp + 8], r=[pk, "Vs", "CCB"], p0=64 * par, p1=64 * par + 64)
                        lhs = Vns[:, 64 * h:64 * h + 64] if kind == "o" else self.ones32b
                        self.mm(o, bo, lhs, pt[:, 512 + 8 * hp:512 + 8 * hp + 8], r=[pk, "Vns", "CCB"], p0=64 * par, p1=64 * par + 64, stop=True)
                self.cp("dve", mixT[:, 2 * g:2 * g + 2, cs:cs + 8], self.psf(bo, 16).rearrange("p (c q) -> p c q", c=2), r=[("ps", bo)], w=[("mqs", g)])
                lsrc = self.psf(bo, 16, off=16).rearrange("p (c q) -> p c q", c=2)
                if first:
                    self.cp("dve", Ltot[:, :, cs:cs + 8], lsrc, r=[("ps", bo)], w=[("Ls", bl)])
                else:
                    self.tt("dve", Ltot[:, :, cs:cs + 8], Ltot[:, :, cs:cs + 8], lsrc, ALU.add, r=[("ps", bo), ("Ls", bl)], w=[("Ls", bl)])
            if self.stage == -4.5:
                return
            first = False
            S.barrier()
            A.release(mg)
        A.release(m)
        S.add("dve", lambda e: e.reciprocal(Ltot, Ltot), r=[], w=["Ltot"])
        for ch in range(6):
            self.tt("dve", mixT[:, ch, :], mixT[:, ch, :], Ltot[:, ch % 2, :], ALU.mult, r=["Ltot"], w=[("mixc", ch)])
        if self.debug:
            pass
        S.barrier()
        if self.stage == -4.6:
            return
        m = A.mark()
        Wo = A.alloc("Wo1", [128, 8, 1024], BF16)
        wo_v = D["w_out"][1].rearrange("(c p) f -> p c f", p=128)
        self.dmas("pool", [(Wo[:, c, :], wo_v[:, c, :]) for c in range(8)], r=[], w=["Wo"], lane="Wo")
        self.load_ln(D["ln_mix_g"], D["ln_mix_b"], 1)
        xin = [A.alloc("xin1", [128, 1024], F32) for _ in range(2)]
        self.xb1 = A.alloc("xb1", [128, 1024], BF16)
        stat = A.alloc("stat1", [128, 16], F32)
        for i in range(NT + 1):
            P = 128 if i < NT else TS
            c0 = 128 * i
            xi = xin[i % 2]
            xk = ("xin", i % 2)
            self.dma("sp", xi[0:P, :], D["x2s"][c0:c0 + P, :], r=[], w=[xk], lane="xin%d" % (i % 2))
            self.b1_tile(P, i, (lambda c, c0=c0, P=P: mixT[:, c, c0:c0 + P]), "mixall", Wo, xi, xk, "x1o%d" % (i % 2), stat)
        S.barrier()
        A.release(m)
        if self.stage == -4.7:
            return
        A.release(ml)
        self.ffn(1, True, True)


_CACHE = {}


def _get_prog(stage, debug):
    key = (stage, debug)
    if key not in _CACHE:
        _CACHE[key] = Prog(stage, debug)
    return _CACHE[key]


def _in_maps(inp):
    f = lambda a: np.ascontiguousarray(a, dtype=np.float32)
    maps = []
    for c in range(NCORES):
        b, j = divmod(c, 4)
        cc, c0, c1, rope0, rope1 = build_consts(c)
        xb_ = np.asarray(inp["x_prompt"][b], dtype=np.float32)
        xpad = np.concatenate([np.zeros((3 * T, 1024), np.float32), xb_], 0)
        m = {
            "xp": f(inp["x_prompt"][b, T * j:T * (j + 1)]),
            "xpv": f(xpad[3 * T + T * (j - 1):3 * T + T * j]),
            "xpp": f(xpad[3 * T + T * (j - 3):3 * T + T * (j - 1)]),
            "rope0": rope0, "rope1": rope1,
            "xs": f(inp["x_sample"][4 * c:4 * c + 4].reshape(TS, 1024)),
            "mem": f(inp["mem_prompt"][b]),
            "cmkv": f(inp["cache_mem_kv"][:, 4 * c:4 * c + 4].reshape(2, 4, 256, 512)),
            "sret": f(inp["state_ret"][0, 4 * c:4 * c + 4]),
            "cw0": f(inp["cache_win_kv_g1"][4 * c:4 * c + 4].reshape(4, 128, 512)),
            "cw1": f(inp["cache_win_kv_g2"][4 * c:4 * c + 4].reshape(4, 512, 512)),
            "cw2": f(inp["cache_win_kv_g3"][4 * c:4 * c + 4].reshape(4, 2048, 512)),
            "w_in_a": f(inp["w_in_a"][0]), "w_in_b": f(inp["w_in_b"][0]), "w_out": f(inp["w_out"]),
            "w_kv": f(inp["w_kv_shared"]), "w_mem": f(inp["w_mem_kv"]),
            "ln_mix_g": f(inp["ln_mix_g"]), "ln_mix_b": f(inp["ln_mix_b"]),
            "ln_ffn_g": f(inp["ln_ffn_g"]), "ln_ffn_b": f(inp["ln_ffn_b"]),
            "w_ffn_in": f(inp["w_ffn_in"]), "w_ffn_out": f(inp["w_ffn_out"]),
            "cc": cc, "c0": c0, "c1": c1,
        }
        maps.append(m)
    return maps


def _run(inp, stage=99, debug=False):
    prog = _get_prog(stage, debug)
    maps = [{k: v for k, v in m.items() if k in prog.D} for m in _in_maps(inp)]
    res = run_bass_kernel_spmd(prog.nc, maps, core_ids=list(range(NCORES)))
    return res.results


def kernel(**inp):
    R = _run(inp)
    yp = np.stack([np.concatenate([R[4 * b + j]["yp"] for j in range(4)], 0) for b in range(2)], 0)
    ys = np.concatenate([R[c]["ys"].reshape(4, 8, 1024) for c in range(8)], 0)
    srp = np.stack([R[4 * b + 3]["srp"] for b in range(2)], 0)[None]
    srs = np.concatenate([R[c]["srs"] for c in range(8)], 0)[None]
    mkv = np.stack([np.stack([R[4 * b]["mkv"][l].reshape(256, 2, 4, 64) for b in range(2)], 0) for l in range(2)], 0)
    outs = [yp, ys, srp, srs, mkv]
    for g, W in enumerate(WINS):
        outs.append(np.stack([R[4 * b + 3]["wp%d" % g].reshape(W, 2, 4, 64) for b in range(2)], 0))
    for g, W in enumerate(WINS):
        outs.append(np.concatenate([R[c]["ws%d" % g].reshape(4, W, 2, 4, 64) for c in range(8)], 0))
    return tuple(np.ascontiguousarray(o, dtype=np.float32) for o in outs)
```
